# Optimizing a Trainium2 kernel written in Bass

```python
import jax, jax.numpy as jnp
from jax import lax
import numpy as np

D_MODEL = 1024
BATCH = 4
SEQ = 4096
DEPTH = 4

N_MIXERS = 3
N_A = (DEPTH + 2) // 3
N_B = (DEPTH + 1) // 3
N_C = DEPTH // 3

N_HEADS = 16
HEAD_DIM = D_MODEL // N_HEADS
D_FF = ((8 * D_MODEL // 3 + 127) // 128) * 128
ROPE_THETA = 500000.0
ROT_DIM = HEAD_DIM // 4
Q_BLOCK = 128
EPS = 1e-6
NEG = -1e30

MOBA_BLOCK = 256
MOBA_TOPK = 3
MOBA_Q_CHUNK = 32

MLA_Q_RANK = 384
MLA_KV_RANK = 256
MLA_NOPE = 64
MLA_ROPE = 32
MLA_V = 64
MLA_IN = MLA_Q_RANK + MLA_KV_RANK + MLA_ROPE

kernel_name = "hybrid_sb_moba_mla_macaron"


def rms_norm(x, g):
    xf = x.astype(jnp.float32)
    y = xf * lax.rsqrt(jnp.mean(xf * xf, axis=-1, keepdims=True) + EPS)
    return (y * g.astype(jnp.float32)).astype(x.dtype)


def rotary(x, pos):
    r = x.shape[-1]
    inv = ROPE_THETA ** (-jnp.arange(0, r, 2, dtype=jnp.float32) / r)
    ang = pos.astype(jnp.float32)[:, None] * inv[None, :]
    cos, sin = jnp.cos(ang), jnp.sin(ang)
    xf = x.astype(jnp.float32)
    x1, x2 = xf[..., : r // 2], xf[..., r // 2:]
    return jnp.concatenate([x1 * cos - x2 * sin, x1 * sin + x2 * cos], axis=-1).astype(x.dtype)


def partial_rotary(x, pos):
    return jnp.concatenate([rotary(x[..., :ROT_DIM], pos), x[..., ROT_DIM:]], axis=-1)


def to_heads(t):
    b, s, _ = t.shape
    return t.reshape(b, s, N_HEADS, -1).transpose(0, 2, 1, 3)


def from_heads(t):
    b, h, s, d = t.shape
    return t.transpose(0, 2, 1, 3).reshape(b, s, h * d)


def merge_blocks(o):
    nb, b, h, q, d = o.shape
    return o.transpose(1, 2, 0, 3, 4).reshape(b, h, nb * q, d)


def swiglu(x, w_gate_up, w_down):
    g, u = jnp.split(x @ w_gate_up, 2, axis=-1)
    return (jax.nn.silu(g) * u) @ w_down


def stick_breaking_attention(x, w_qkv, w_o):
    b, s, _ = x.shape
    q, k, v = (to_heads(t) for t in jnp.split(x @ w_qkv, 3, axis=-1))
    scale = HEAD_DIM ** -0.5
    kpos = jnp.arange(s)

    def block(i):
        q0 = i * Q_BLOCK
        qb = lax.dynamic_slice_in_dim(q, q0, Q_BLOCK, axis=2)
        z = jnp.einsum('bhqd,bhkd->bhqk', qb, k).astype(jnp.float32) * scale
        qpos = q0 + jnp.arange(Q_BLOCK)
        past = kpos[None, :] < qpos[:, None]
        log_keep = jnp.where(past, -jax.nn.softplus(z), 0.0)
        tail = lax.cumsum(log_keep, axis=3, reverse=True) - log_keep
        w = jnp.where(past, jnp.exp(jax.nn.log_sigmoid(z) + tail), 0.0)
        return jnp.einsum('bhqk,bhkd->bhqd', w.astype(v.dtype), v)

    o = merge_blocks(lax.map(block, jnp.arange(s // Q_BLOCK)))
    return from_heads(o) @ w_o


def moba_attention(x, w_qkv, w_o):
    b, s, _ = x.shape
    pos = jnp.arange(s)
    q, k, v = (to_heads(t) for t in jnp.split(x @ w_qkv, 3, axis=-1))
    q, k = partial_rotary(q, pos), partial_rotary(k, pos)
    nb = -(-s // MOBA_BLOCK)
    s_pad = nb * MOBA_BLOCK
    pad = [(0, 0), (0, 0), (0, s_pad - s), (0, 0)]
    q, k, v = jnp.pad(q, pad), jnp.pad(k, pad), jnp.pad(v, pad)
    kb = k.reshape(b, N_HEADS, nb, MOBA_BLOCK, HEAD_DIM)
    vb = v.reshape(b, N_HEADS, nb, MOBA_BLOCK, HEAD_DIM)
    scale = HEAD_DIM ** -0.5

    k_mean = jnp.mean(kb.astype(jnp.float32), axis=3)
    gate = jnp.einsum('bhsd,bhnd->bhsn', q.astype(jnp.float32), k_mean)
    q_blk = jnp.arange(s_pad) // MOBA_BLOCK
    past_blk = jnp.arange(nb)[None, :] < q_blk[:, None]
    gate = jnp.where(past_blk, gate, NEG)
    topk = min(MOBA_TOPK, nb)
    _, sel = lax.top_k(gate, topk)
    sel_valid = sel < q_blk[:, None]

    bi = jnp.arange(b)[:, None, None, None]
    hi = jnp.arange(N_HEADS)[None, :, None, None]

    def chunk(c):
        q0 = c * MOBA_Q_CHUNK
        qc = lax.dynamic_slice_in_dim(q, q0, MOBA_Q_CHUNK, axis=2)
        idx = lax.dynamic_slice_in_dim(sel, q0, MOBA_Q_CHUNK, axis=2)
        valid = lax.dynamic_slice_in_dim(sel_valid, q0, MOBA_Q_CHUNK, axis=2)
        kg = kb[bi, hi, idx]
        vg = vb[bi, hi, idx]
        l_sel = jnp.einsum('bhqd,bhqjkd->bhqjk', qc, kg).astype(jnp.float32) * scale
        l_sel = jnp.where(valid[..., None], l_sel, NEG)
        l_sel = l_sel.reshape(b, N_HEADS, MOBA_Q_CHUNK, topk * MOBA_BLOCK)
        own = q0 // MOBA_BLOCK
        k_own = lax.dynamic_index_in_dim(kb, own, axis=2, keepdims=False)
        v_own = lax.dynamic_index_in_dim(vb, own, axis=2, keepdims=False)
        l_own = jnp.einsum('bhqd,bhkd->bhqk', qc, k_own).astype(jnp.float32) * scale
        cpos = q0 + jnp.arange(MOBA_Q_CHUNK)
        opos = own * MOBA_BLOCK + jnp.arange(MOBA_BLOCK)
        l_own = jnp.where(opos[None, :] <= cpos[:, None], l_own, NEG)
        p = jax.nn.softmax(jnp.concatenate([l_sel, l_own], axis=-1), axis=-1).astype(v.dtype)
        p_sel = p[..., : topk * MOBA_BLOCK].reshape(b, N_HEADS, MOBA_Q_CHUNK, topk, MOBA_BLOCK)
        p_own = p[..., topk * MOBA_BLOCK:]
        return (jnp.einsum('bhqjk,bhqjkd->bhqd', p_sel, vg)
                + jnp.einsum('bhqk,bhkd->bhqd', p_own, v_own))

    o = merge_blocks(lax.map(chunk, jnp.arange(s_pad // MOBA_Q_CHUNK)))[:, :, :s]
    return from_heads(o) @ w_o


def mla_attention(x, w_in, q_norm, w_uq, kv_norm, w_ukv, w_o):
    b, s, _ = x.shape
    pos = jnp.arange(s)
    cq, ckv, kr = jnp.split(x @ w_in, [MLA_Q_RANK, MLA_Q_RANK + MLA_KV_RANK], axis=-1)
    q = to_heads(rms_norm(cq, q_norm) @ w_uq)
    q_nope, q_rope = q[..., :MLA_NOPE], rotary(q[..., MLA_NOPE:], pos)
    kv = to_heads(rms_norm(ckv, kv_norm) @ w_ukv)
    k_nope, v = kv[..., :MLA_NOPE], kv[..., MLA_NOPE:]
    k_rope = rotary(kr, pos)
    scale = (MLA_NOPE + MLA_ROPE) ** -0.5
    kpos = jnp.arange(s)

    def block(i):
        q0 = i * Q_BLOCK
        qn = lax.dynamic_slice_in_dim(q_nope, q0, Q_BLOCK, axis=2)
        qr = lax.dynamic_slice_in_dim(q_rope, q0, Q_BLOCK, axis=2)
        logits = (jnp.einsum('bhqd,bhkd->bhqk', qn, k_nope)
                  + jnp.einsum('bhqr,bkr->bhqk', qr, k_rope)).astype(jnp.float32) * scale
        qpos = q0 + jnp.arange(Q_BLOCK)
        logits = jnp.where(kpos[None, :] <= qpos[:, None], logits, NEG)
        p = jax.nn.softmax(logits, axis=-1).astype(v.dtype)
        return jnp.einsum('bhqk,bhkd->bhqd', p, v)

    o = merge_blocks(lax.map(block, jnp.arange(s // Q_BLOCK)))
    return from_heads(o) @ w_o


def setup_inputs(seed: int = 0) -> dict:
    key = jax.random.key(seed)
    ks = iter(jax.random.split(key, 32))

    def w(shape, fan_in):
        return jax.random.normal(next(ks), shape, jnp.float32) * (fan_in ** -0.5)

    def gain(shape):
        return 1.0 + 0.02 * jax.random.normal(next(ks), shape, jnp.float32)

    D = D_MODEL
    return {
        "x": jax.random.normal(next(ks), (BATCH, SEQ, D), jnp.float32),
        "norm_ffn1": gain((DEPTH, D)),
        "ffn1_w_gate_up": w((DEPTH, D, 2 * D_FF), D),
        "ffn1_w_down": w((DEPTH, D_FF, D), D_FF),
        "norm_mix": gain((DEPTH, D)),
        "norm_ffn2": gain((DEPTH, D)),
        "ffn2_w_gate_up": w((DEPTH, D, 2 * D_FF), D),
        "ffn2_w_down": w((DEPTH, D_FF, D), D_FF),
        "sb_w_qkv": w((N_A, D, 3 * D), D),
        "sb_w_o": w((N_A, D, D), D),
        "moba_w_qkv": w((N_B, D, 3 * D), D),
        "moba_w_o": w((N_B, D, D), D),
        "mla_w_in": w((N_C, D, MLA_IN), D),
        "mla_q_norm": gain((N_C, MLA_Q_RANK)),
        "mla_w_uq": w((N_C, MLA_Q_RANK, N_HEADS * (MLA_NOPE + MLA_ROPE)), MLA_Q_RANK),
        "mla_kv_norm": gain((N_C, MLA_KV_RANK)),
        "mla_w_ukv": w((N_C, MLA_KV_RANK, N_HEADS * (MLA_NOPE + MLA_V)), MLA_KV_RANK),
        "mla_w_o": w((N_C, N_HEADS * MLA_V, D), N_HEADS * MLA_V),
        "final_norm": gain((D,)),
    }


def reference(x, norm_ffn1, ffn1_w_gate_up, ffn1_w_down, norm_mix, norm_ffn2,
              ffn2_w_gate_up, ffn2_w_down, sb_w_qkv, sb_w_o, moba_w_qkv, moba_w_o,
              mla_w_in, mla_q_norm, mla_w_uq, mla_kv_norm, mla_w_ukv, mla_w_o,
              final_norm):
    h = x
    for i in range(DEPTH):
        h = h + 0.5 * swiglu(rms_norm(h, norm_ffn1[i]), ffn1_w_gate_up[i], ffn1_w_down[i])
        hn = rms_norm(h, norm_mix[i])
        kind, j = i % N_MIXERS, i // N_MIXERS
        if kind == 0:
            mix = stick_breaking_attention(hn, sb_w_qkv[j], sb_w_o[j])
        elif kind == 1:
            mix = moba_attention(hn, moba_w_qkv[j], moba_w_o[j])
        else:
            mix = mla_attention(hn, mla_w_in[j], mla_q_norm[j], mla_w_uq[j],
                                mla_kv_norm[j], mla_w_ukv[j], mla_w_o[j])
        h = h + mix
        h = h + 0.5 * swiglu(rms_norm(h, norm_ffn2[i]), ffn2_w_gate_up[i], ffn2_w_down[i])
    return rms_norm(h, final_norm)
```

```python
import numpy as np
import concourse.bass as bass
import concourse.mybir as mybir

F32 = mybir.dt.float32
BF16 = mybir.dt.bfloat16
AF = mybir.ActivationFunctionType
ALU = mybir.AluOpType
AX = mybir.AxisListType

ENGS = ("pe", "act", "dve", "pool", "sp")


class Buf:
    __slots__ = ("name", "writers", "readers", "sem", "semval", "excl", "sw")

    def __init__(self, name, excl=False):
        self.name = name
        self.excl = excl
        self.writers = {}
        self.readers = {}
        self.sem = None
        self.semval = 0
        self.sw = None


class Prog:
    def __init__(self, nc):
        self.nc = nc
        self.streams = {e: [] for e in ENGS}
        self.cnt = {e: 0 for e in ENGS}
        self.sem = {}
        self.seen = {e: {} for e in ENGS}
        self.snaps = {}
        self._ctx = []
        for e in ENGS:
            cm = nc.semaphore("s_" + e)
            self.sem[e] = cm.__enter__()
            self._ctx.append(cm)
        self.nwaits = 0
        self.nops = 0
        self.dmabufs = []
        self.free_hw = []
        self.free_sw = []
        self._issw = {}

    def new_sem(self, name):
        cm = self.nc.semaphore(name)
        s = cm.__enter__()
        self._ctx.append(cm)
        return s

    def _need(self, eng, tok, waits):
        sem, val, _ = tok
        if self.seen[eng].get(sem.num, 0) >= val:
            return
        waits[sem.num] = (sem, max(val, waits.get(sem.num, (sem, 0))[1]))

    def _deps(self, eng, reads, writes, pe_accum=False, nowaw=False):
        waits = {}
        for b in reads:
            for k, tok in b.writers.items():
                self._need(eng, tok, waits)
            if b.excl:
                for k, tok in b.readers.items():
                    if k != eng:
                        self._need(eng, tok, waits)
        for b in writes:
            for k, tok in b.readers.items():
                self._need(eng, tok, waits)
            for k, tok in b.writers.items():
                if pe_accum and k == "pe":
                    continue
                if nowaw and isinstance(k, tuple):
                    continue
                self._need(eng, tok, waits)
        sn = self.seen[eng]
        for num, (sem, val) in waits.items():
            sn[num] = max(sn.get(num, 0), val)
            snap = self.snaps.get((num, val))
            if snap:
                for k2, v2 in snap.items():
                    if sn.get(k2, 0) < v2:
                        sn[k2] = v2
        self.nwaits += len(waits)
        return list(waits.values())

    def _post(self, key, tok, reads, writes):
        for b in reads:
            old = b.readers.get(key)
            if old is None or old[1] < tok[1] or old[0].num != tok[0].num:
                b.readers[key] = tok
        for b in writes:
            if b.readers:
                b.readers = {}
                b.writers = {key: tok}
            else:
                b.writers[key] = tok

    def op(self, eng, fn, reads=(), writes=(), pe_accum=False):
        waits = self._deps(eng, reads, writes, pe_accum=pe_accum)
        self.cnt[eng] += 1
        n = self.cnt[eng]
        sem = self.sem[eng]
        self.snaps[(sem.num, n)] = dict(self.seen[eng])
        tok = (sem, n, None)
        self._post(eng, tok, reads, writes)
        self.streams[eng].append((waits, fn, (sem, 1)))
        self.nops += 1

    def dma(self, eng, fn, reads=(), writes=(), sb=None, inc=16):
        dst = sb if sb is not None else writes[0]
        if eng == "pool":
            if dst.sw is None:
                dst.sw = Buf(dst.name + "_sw")
            dst = dst.sw
        if dst.sem is None:
            pool = self.free_sw if eng == "pool" else self.free_hw
            if pool:
                dst.sem, dst.semval = pool.pop()
            else:
                dst.sem = self.new_sem("d_" + dst.name)
                dst.semval = 0
            self.dmabufs.append(dst)
            self._issw[id(dst)] = (eng == "pool")
        waits = self._deps(eng, reads, writes, nowaw=True)
        dst.semval += inc
        tok = (dst.sem, dst.semval, None)
        self.snaps[(dst.sem.num, dst.semval)] = dict(self.seen[eng])
        self._post(("dma", dst.sem.num), tok, reads, writes)
        self.streams[eng].append((waits, fn, (dst.sem, inc)))
        self.nops += 1

    def barrier(self):
        for eng in ENGS:
            waits = {}
            for e2 in ENGS:
                if e2 != eng and self.cnt[e2] > 0:
                    self._need(eng, (self.sem[e2], self.cnt[e2], None), waits)
            if self.cnt[eng] > 0:
                self._need(eng, (self.sem[eng], self.cnt[eng], None), waits)
            for b in self.dmabufs:
                if b.semval > 0:
                    self._need(eng, (b.sem, b.semval, None), waits)
            sn = self.seen[eng]
            for num, (sem, val) in waits.items():
                sn[num] = max(sn.get(num, 0), val)
            self.nwaits += len(waits)
            self.streams[eng].append((list(waits.values()), None, None))
        self.recycle()

    def recycle(self):
        for b in self.dmabufs:
            (self.free_sw if self._issw[id(b)] else self.free_hw).append((b.sem, b.semval))
            b.sem = None
            b.semval = 0
        self.dmabufs = []
        self._issw = {}

    def wait_all(self, eng, bufs):
        waits = self._deps(eng, bufs, ())
        self.streams[eng].append((waits, None, None))

    def emit(self):
        nc = self.nc
        engmap = {"pe": "tensor", "act": "scalar", "dve": "vector", "pool": "gpsimd", "sp": "sync"}
        with nc.Block() as block:
            for e in ENGS:
                stream = self.streams[e]

                def body(eng, stream=stream):
                    for waits, fn, inc in stream:
                        for sem, val in waits:
                            eng.wait_ge(sem, val)
                        if fn is not None:
                            ins = fn(eng)
                            ins.then_inc(inc[0], inc[1])
                getattr(block, engmap[e])(body)

    def close(self):
        for cm in reversed(self._ctx):
            cm.__exit__(None, None, None)
from contextlib import ExitStack
from concourse.bass_utils import run_bass_kernel_spmd

D = 1024
F = 2816
NCH = D // 128
NFC = F // 128
FG = 2
NFG = NFC // FG
H = 16
HD = 64
EPS = 1e-6
DEPTH = 4
ST = 1024
TT = 512


class MK:
    def __init__(self, NT, n_layers=DEPTH, cfg=None):
        self.NT = NT
        self.cfg = cfg or {}
        nc = self.nc = bass.Bass("TRN2", target_bir_lowering=False)
        self.P = Prog(nc)
        self.dram = {}
        self.dbuf = {}

    def din(self, name, shape, dt=F32):
        t = self.nc.dram_tensor(name, list(shape), dt, kind="ExternalInput")
        self.dram[name] = t
        self.dbuf[name] = Buf(name)
        return t

    def dout(self, name, shape, dt=F32):
        t = self.nc.dram_tensor(name, list(shape), dt, kind="ExternalOutput")
        self.dram[name] = t
        self.dbuf[name] = Buf(name)
        return t

    def dscr(self, name, shape, dt):
        t = self.nc.dram_tensor(name, list(shape), dt)
        self.dram[name] = t
        self.dbuf[name] = Buf(name)
        return t

    _uid = 0

    def uname(self, name):
        MK._uid += 1
        return "%s_%d" % (name, MK._uid)

    def sb(self, stack, name, shape, dt):
        name = self.uname(name)
        t = stack.enter_context(self.nc.sbuf_tensor(name, list(shape), dt))
        return t, Buf(name)

    def psum_banks(self, stack):
        banks = []
        for i in range(8):
            nm = self.uname("psb%d" % i)
            t = stack.enter_context(self.nc.psum_tensor(nm, [128, 512], F32))
            banks.append((t, Buf(nm, excl=True)))
        return banks

    def setup_consts(self, stack, gains_dram, ngain):
        P = self.P
        self.ones_mean, self.b_ones_mean = self.sb(stack, "ones_mean", [128, 128], BF16)
        P.op("pool", lambda e: e.memset(self.ones_mean[:], 1.0 / D), writes=[self.b_ones_mean])
        self.epsc, self.b_epsc = self.sb(stack, "epsc", [128, 1], F32)
        P.op("pool", lambda e: e.memset(self.epsc[:], EPS), writes=[self.b_epsc])
        self.gains, self.b_gains = self.sb(stack, "gains_sb", [128, ngain, NCH], F32)
        P.dma("sp", lambda e: e.dma_start(out=self.gains[:], in_=gains_dram.ap()),
              reads=[self.dbuf[gains_dram.name]], writes=[self.b_gains])

    def rmsnorm_tile(self, ht, b_ht, ntok, gi, out, b_out, sq, b_sq, rstd, b_rstd, bank):
        P = self.P
        ps, b_ps = bank
        for t0 in range(0, ntok, TT):
            P.op("act", lambda e, t0=t0: e.activation(out=sq[:, :, :], in_=ht[:, :, t0:t0 + TT], func=AF.Square),
                 reads=[b_ht], writes=[b_sq])
            for c in range(NCH):
                P.op("pe", lambda e, c=c: e.matmul(ps[:], lhsT=self.ones_mean[:], rhs=sq[:, c, :],
                                                   start=(c == 0), stop=(c == NCH - 1)),
                     reads=[self.b_ones_mean, b_sq], writes=[b_ps], pe_accum=(c > 0))
            P.op("act", lambda e: e.activation(out=rstd[:], in_=ps[:], func=AF.Sqrt, bias=self.epsc[:, 0:1]),
                 reads=[b_ps, self.b_epsc], writes=[b_rstd])
            P.op("dve", lambda e: e.reciprocal(out=rstd[:], in_=rstd[:]),
                 reads=[b_rstd], writes=[b_rstd])
            for c in range(NCH):
                P.op("dve", lambda e, c=c, t0=t0: e.scalar_tensor_tensor(
                    out=out[:, c, t0:t0 + TT], in0=ht[:, c, t0:t0 + TT], scalar=self.gains[:, gi, c:c + 1],
                    in1=rstd[:], op0=ALU.mult, op1=ALU.mult),
                     reads=[b_ht, b_rstd, self.b_gains], writes=[b_out])

    def pass_ffn(self, h_src, h_dst, wgu, wd, gi, epi=None):
        P, nc, NT = self.P, self.nc, self.NT
        ST = min(1024, NT)
        with ExitStack() as stack:
            banks = self.psum_banks(stack)
            ht = [self.sb(stack, "ht%d" % i, [128, NCH, ST], F32) for i in range(1)]
            hn, b_hn = self.sb(stack, "hn", [128, NCH, ST], BF16)
            act, b_act = self.sb(stack, "act", [128, NFC, ST], BF16)
            sq, b_sq = self.sb(stack, "sq", [128, NCH, TT], BF16)
            rstd, b_rstd = self.sb(stack, "rstd", [128, TT], F32)
            sil = [self.sb(stack, "sil%d" % i, [128, TT], F32) for i in range(2)]
            wg_s = [self.sb(stack, "wgu%d" % i, [128, 2, NCH, FG * 128], BF16) for i in range(2)]
            wd_s = [self.sb(stack, "wd%d" % i, [128, NFC, 128], BF16) for i in range(2)]
            if epi is not None:
                eo_dt = BF16 if epi[0] == "norm" else F32
                eo, b_eo = self.sb(stack, "eo", [128, NCH, ST], eo_dt)
            hsrc, hdst = self.dram[h_src], self.dram[h_dst]
            wgu_t, wd_t = self.dram[wgu], self.dram[wd]
            nst = NT // ST
            wi = 0
            di = 0
            for st in range(nst):
                t_lo = st * ST
                h_t, b_h = ht[0]
                P.dma("sp", lambda e, t_lo=t_lo: e.dma_start(
                    out=h_t[:], in_=hsrc.ap()[:, :, t_lo:t_lo + ST].rearrange("c p t -> p c t")),
                    reads=[self.dbuf[h_src]], writes=[b_h])
                self.rmsnorm_tile(h_t, b_h, ST, gi, hn, b_hn, sq, b_sq, rstd, b_rstd, banks[6])
                for fg in range(NFG):
                    w_t, b_w = wg_s[wi % 2]
                    wi += 1
                    P.dma("pool", lambda e, fg=fg, w_t=w_t: e.dma_start(out=w_t[:], in_=wgu_t.ap()[fg]),
                          reads=[self.dbuf[wgu]], writes=[b_w])
                    for f2 in range(FG):
                        f = fg * FG + f2
                        for ti in range(ST // TT):
                            t0 = ti * TT
                            k = (f * 2 + ti) % 2
                            pg, b_pg = banks[k]
                            pu, b_pu = banks[2 + k]
                            for c in range(NCH):
                                P.op("pe", lambda e, c=c, w_t=w_t, f2=f2, t0=t0, pg=pg: e.matmul(
                                    pg[:], lhsT=w_t[:, 0, c, f2 * 128:(f2 + 1) * 128], rhs=hn[:, c, t0:t0 + TT],
                                    start=(c == 0), stop=(c == NCH - 1)),
                                    reads=[b_w, b_hn], writes=[b_pg], pe_accum=(c > 0))
                            for c in range(NCH):
                                P.op("pe", lambda e, c=c, w_t=w_t, f2=f2, t0=t0, pu=pu: e.matmul(
                                    pu[:], lhsT=w_t[:, 1, c, f2 * 128:(f2 + 1) * 128], rhs=hn[:, c, t0:t0 + TT],
                                    start=(c == 0), stop=(c == NCH - 1)),
                                    reads=[b_w, b_hn], writes=[b_pu], pe_accum=(c > 0))
                            s_t, b_s = sil[k]
                            P.op("act", lambda e, s_t=s_t, pg=pg: e.activation(out=s_t[:], in_=pg[:], func=AF.Silu),
                                 reads=[b_pg], writes=[b_s])
                            P.op("dve", lambda e, s_t=s_t, pu=pu, f=f, t0=t0: e.tensor_tensor(
                                out=act[:, f, t0:t0 + TT], in0=pu[:], in1=s_t[:], op=ALU.mult),
                                reads=[b_pu, b_s], writes=[b_act])
                for dc in range(NCH):
                    w_t, b_w = wd_s[di % 2]
                    di += 1
                    P.dma("pool", lambda e, dc=dc, w_t=w_t: e.dma_start(out=w_t[:], in_=wd_t.ap()[dc]),
                          reads=[self.dbuf[wd]], writes=[b_w])
                    for ti in range(ST // TT):
                        t0 = ti * TT
                        po, b_po = banks[4 + (dc * 2 + ti) % 2]
                        for f in range(NFC):
                            P.op("pe", lambda e, f=f, w_t=w_t, t0=t0, po=po: e.matmul(
                                po[:], lhsT=w_t[:, f, :], rhs=act[:, f, t0:t0 + TT],
                                start=(f == 0), stop=(f == NFC - 1)),
                                reads=[b_w, b_act], writes=[b_po], pe_accum=(f > 0))
                        P.op("dve", lambda e, dc=dc, t0=t0, po=po: e.scalar_tensor_tensor(
                            out=h_t[:, dc, t0:t0 + TT], in0=po[:], scalar=0.5, in1=h_t[:, dc, t0:t0 + TT],
                            op0=ALU.mult, op1=ALU.add),
                            reads=[b_po, b_h], writes=[b_h])
                if epi is None or epi[0] == "norm":
                    P.dma("sp", lambda e, t_lo=t_lo: e.dma_start(
                        out=hdst.ap()[:, :, t_lo:t_lo + ST].rearrange("c p t -> p c t"), in_=h_t[:]),
                        reads=[b_h], writes=[self.dbuf[h_dst]], sb=b_h)
                if epi is not None:
                    self.rmsnorm_tile(h_t, b_h, ST, epi[1], eo, b_eo, sq, b_sq, rstd, b_rstd, banks[6])
                    dst = self.dram[epi[2]]
                    P.dma("sp", lambda e, t_lo=t_lo, dst=dst: e.dma_start(
                        out=dst.ap()[:, :, t_lo:t_lo + ST].rearrange("c p t -> p c t"), in_=eo[:]),
                        reads=[b_eo], writes=[self.dbuf[epi[2]]], sb=b_eo)
            P.barrier()

    def finish(self, out_names):
        P = self.P
        P.barrier()
        P.emit()
        P.close()
        return self.nc


def lay_wgu(w):
    g = w.reshape(NCH, 128, 2, NFG, FG * 128)
    return np.ascontiguousarray(g.transpose(3, 1, 2, 0, 4))


def lay_wd(w):
    g = w.reshape(NFC, 128, NCH, 128)
    return np.ascontiguousarray(g.transpose(2, 1, 0, 3))


def lay_gain(gs):
    a = np.stack(gs, 0).reshape(len(gs), NCH, 128)
    return np.ascontiguousarray(a.transpose(2, 0, 1))


def lay_xT(x):
    return np.ascontiguousarray(x.T.reshape(NCH, 128, x.shape[0]))


def unlay_xT(y):
    return np.ascontiguousarray(y.reshape(D, y.shape[-1]).T)


BIG = 30000.0
NEG = -1e30
THETA = 500000.0
C_NTRI = 0
C_ID = 128
C_PSW = 256
C_SEL = 640
C_ONE = 768
C_EN = 896
CW = 896 + 16 * 128


def host_consts():
    c = np.zeros((128, CW), np.float32)
    s = np.arange(128)[:, None]
    j = np.arange(128)[None, :]
    c[:, C_NTRI:C_NTRI + 128] = -(s >= j).astype(np.float32)
    c[:, C_ID:C_ID + 128] = (s == j)
    pm = np.zeros((128, 128), np.float32)
    for p in range(128):
        d = p % 64
        if d < 8:
            pm[p, p + 8] = 1
        elif d < 16:
            pm[p, p - 8] = 1
    c[:, C_PSW:C_PSW + 128] = pm
    pq = np.zeros((128, 128), np.float32)
    for p in range(64, 96):
        dd = p - 64
        pq[p, p + 16 if dd < 16 else p - 16] = 1
    c[:, C_PSW + 128:C_PSW + 256] = pq
    pk = np.zeros((128, 128), np.float32)
    for p in range(32):
        pk[p, p + 16 if p < 16 else p - 16] = 1
    c[:, C_PSW + 256:C_PSW + 384] = pk
    c[0, C_SEL:C_SEL + 128] = -1
    c[32, C_SEL:C_SEL + 128] = -1
    c[:, C_ONE:C_ONE + 128] = 1
    for n in range(16):
        c[n, C_EN + n * 128:C_EN + (n + 1) * 128] = 1
    return c


def host_ropec():
    r = np.zeros((128, 8), np.float32)
    for p in range(128):
        d = p % 64
        if d < 16:
            i = d % 8
            r[p, 0] = THETA ** (-(2.0 * i) / 16.0) / (2 * np.pi)
            r[p, 1] = -1.0 if d < 8 else 1.0
        if 64 <= p < 96:
            dd = p - 64
            i = dd % 16
            r[p, 2] = THETA ** (-(2.0 * i) / 32.0) / (2 * np.pi)
            r[p, 3] = -1.0 if dd < 16 else 1.0
        if p < 32:
            i = p % 16
            r[p, 4] = THETA ** (-(2.0 * i) / 32.0) / (2 * np.pi)
            r[p, 5] = -1.0 if p < 16 else 1.0
    r[:, 6] = -np.pi
    return r


def host_masks(role, NQS=4):
    m = np.zeros((2, 128, 8, 512), np.float32)
    k = np.arange(128)[:, None]
    q = np.arange(128)[None, :]
    for d in range(8):
        for s in range(NQS):
            qb = 2 * s + role
            if d < qb:
                m[:, :, d, s * 128:(s + 1) * 128] = 1
            elif d == qb:
                m[0, :, d, s * 128:(s + 1) * 128] = (k < q)
                m[1, :, d, s * 128:(s + 1) * 128] = (k <= q)
    return m


def host_pos(role, NT):
    j = np.arange(NT // 128)[:, None]
    ql = np.arange(128)[None, :]
    return ((2 * j + role) * 128 + ql).reshape(1, NT).astype(np.float32)


class MKA(MK):
    HPC = {64: 8, 96: 4}

    def alloc_scratch(self, kds=(64, 96)):
        NT = self.NT
        self.VR = min(1024, NT)
        for KD in kds:
            self.dscr("qT%d" % KD, [16, KD, NT], BF16)
            hpc = self.HPC[KD]
            for c in range(16 // hpc):
                self.dscr("kT%d_own_c%d" % (KD, c), [hpc * KD, NT], BF16)
                self.dscr("kT%d_all_c%d" % (KD, c), [2 * hpc * KD, NT], BF16)
        for c in range(NT // self.VR):
            self.dscr("v_own_c%d" % c, [self.VR, D], BF16)
            self.dscr("v_all_c%d" % c, [2 * self.VR, D], BF16)
        self.dscr("aT", [16, 64, NT], BF16)

    def kown(self, KD, h):
        hpc = self.HPC[KD]
        return "kT%d_own_c%d" % (KD, h // hpc), (h % hpc) * KD

    def kall(self, KD, h, r):
        hpc = self.HPC[KD]
        return "kT%d_all_c%d" % (KD, h // hpc), r * hpc * KD + (h % hpc) * KD

    def vown(self, row):
        return "v_own_c%d" % (row // self.VR), row % self.VR

    def vall(self, r, row):
        return "v_all_c%d" % (row // self.VR), r * self.VR + row % self.VR

    def exchange_all(self, KD):
        for c in range(16 // self.HPC[KD]):
            self.exchange("kT%d_own_c%d" % (KD, c), "kT%d_all_c%d" % (KD, c))
        for c in range(self.NT // self.VR):
            self.exchange("v_own_c%d" % c, "v_all_c%d" % c)

    def setup_attn_consts(self, stack):
        P = self.P
        self.cst, self.b_cst = self.sb(stack, "cst", [128, CW], BF16)
        P.dma("pool", lambda e: e.dma_start(out=self.cst[:], in_=self.dram["consts"].ap()),
              reads=[self.dbuf["consts"]], writes=[self.b_cst])
        self.ropec, self.b_ropec = self.sb(stack, "ropec_sb", [128, 8], F32)
        P.dma("sp", lambda e: e.dma_start(out=self.ropec[:], in_=self.dram["ropec"].ap()),
              reads=[self.dbuf["ropec"]], writes=[self.b_ropec])

    def rope_tables(self, stack, col, nrows):
        P, NT = self.P, self.NT
        posb, b_posb = self.sb(stack, "posb", [128, NT], F32)
        P.dma("sp", lambda e: e.dma_start(out=posb[:], in_=self.dram["pos"].ap().partition_broadcast(128)),
              reads=[self.dbuf["pos"]], writes=[b_posb])
        C, b_C = self.sb(stack, "ropeC", [128, NT], F32)
        S, b_S = self.sb(stack, "ropeS", [128, NT], F32)
        tmp, b_tmp = self.sb(stack, "ropeT", [128, NT], F32)
        rc = self.ropec
        R = slice(0, nrows)
        two_pi = float(2 * np.pi)
        MAGIC = 12582912.0
        rnd, b_rnd = self.sb(stack, "ropeR", [128, NT], F32)
        for (dst, b_dst, shift) in ((S, b_S, 0.0), (C, b_C, 0.25)):
            P.op("dve", lambda e, shift=shift: e.tensor_scalar(out=tmp[R, :], in0=posb[R, :], scalar1=rc[R, col:col + 1],
                                                               scalar2=shift, op0=ALU.mult, op1=ALU.add),
                 reads=[b_posb, self.b_ropec], writes=[b_tmp])
            P.op("dve", lambda e: e.tensor_scalar(out=rnd[R, :], in0=tmp[R, :], scalar1=MAGIC, scalar2=None,
                                                  op0=ALU.add), reads=[b_tmp], writes=[b_rnd])
            P.op("dve", lambda e: e.tensor_scalar(out=rnd[R, :], in0=rnd[R, :], scalar1=-MAGIC, scalar2=None,
                                                  op0=ALU.add), reads=[b_rnd], writes=[b_rnd])
            P.op("dve", lambda e: e.tensor_tensor(out=tmp[R, :], in0=tmp[R, :], in1=rnd[R, :], op=ALU.subtract),
                 reads=[b_tmp, b_rnd], writes=[b_tmp])
            P.op("act", lambda e, dst=dst: e.activation(out=dst[R, :], in_=tmp[R, :], func=AF.Sin, scale=two_pi),
                 reads=[b_tmp], writes=[b_dst])
        P.op("dve", lambda e: e.tensor_scalar(out=S[R, :], in0=S[R, :], scalar1=rc[R, col + 1:col + 2], scalar2=None,
                                              op0=ALU.mult),
             reads=[b_S, self.b_ropec], writes=[b_S])
        return (C, b_C), (S, b_S)

    def apply_rope(self, ps, b_ps, ps2, b_ps2, nrows, pswcol, C, b_C, S, b_S, t0, scale, out, b_out,
                   qb, b_qb, t1, b_t1, t2, b_t2):
        P = self.P
        R = slice(0, nrows)
        P.op("act", lambda e: e.copy(out=qb[R, :], in_=ps[R, :]), reads=[b_ps], writes=[b_qb])
        P.op("pe", lambda e: e.matmul(ps2[R, :], lhsT=self.cst[R, pswcol:pswcol + nrows], rhs=qb[R, :],
                                      start=True, stop=True),
             reads=[self.b_cst, b_qb], writes=[b_ps2])
        P.op("dve", lambda e: e.scalar_tensor_tensor(out=t1[R, :], in0=ps[R, :], scalar=scale,
                                                     in1=C[R, t0:t0 + TT], op0=ALU.mult, op1=ALU.mult),
             reads=[b_ps, b_C], writes=[b_t1])
        P.op("dve", lambda e: e.scalar_tensor_tensor(out=t2[R, :], in0=ps2[R, :], scalar=scale,
                                                     in1=S[R, t0:t0 + TT], op0=ALU.mult, op1=ALU.mult),
             reads=[b_ps2, b_S], writes=[b_t2])
        P.op("pool", lambda e: e.tensor_tensor(out=out[R, :], in0=t1[R, :], in1=t2[R, :], op=ALU.add),
             reads=[b_t1, b_t2], writes=[b_out])

    def pass_proj_qkv(self, kind, wname, hn_name):
        P, NT = self.P, self.NT
        qT = self.dram["qT64"]
        with ExitStack() as stack:
            banks = self.psum_banks(stack)
            w, b_w = self.sb(stack, "wqkv", [128, NCH, 3 * D], BF16)
            wd = self.dram[wname]
            for c in range(NCH):
                P.dma("pool", lambda e, c=c: e.dma_start(out=w[:, c, :], in_=wd.ap()[:, c, :]),
                      reads=[self.dbuf[wname]], writes=[b_w])
            hn = [self.sb(stack, "phn%d" % i, [128, NCH, TT], BF16) for i in range(2)]
            osb = [self.sb(stack, "posb%d" % i, [128, TT], BF16) for i in range(4)]
            vsb = [self.sb(stack, "pvsb%d" % i, [128, D], BF16) for i in range(2)]
            rope = kind == "moba"
            if rope:
                (C, b_C), (S, b_S) = self.rope_tables(stack, 0, 128)
                qb = [self.sb(stack, "rqb%d" % i, [128, TT], BF16) for i in range(2)]
                t1 = [self.sb(stack, "rt1%d" % i, [128, TT], F32) for i in range(2)]
                t2 = [self.sb(stack, "rt2%d" % i, [128, TT], F32) for i in range(2)]
            hnd = self.dram[hn_name]
            oi = 0
            vi = 0
            pi = 0
            for tt in range(NT // TT):
                t0 = tt * TT
                h_t, b_h = hn[tt % 2]
                P.dma("sp", lambda e, t0=t0, h_t=h_t: e.dma_start(
                    out=h_t[:], in_=hnd.ap()[:, :, t0:t0 + TT].rearrange("c p t -> p c t")),
                    reads=[self.dbuf[hn_name]], writes=[b_h])
                for which in range(2):
                    for pr in range(8):
                        ps, b_ps = banks[pi % 2]
                        ps2, b_ps2 = banks[2 + pi % 2]
                        pi += 1
                        col = which * D + pr * 128
                        for c in range(NCH):
                            P.op("pe", lambda e, c=c, col=col, ps=ps, h_t=h_t: e.matmul(
                                ps[:], lhsT=w[:, c, col:col + 128], rhs=h_t[:, c, :],
                                start=(c == 0), stop=(c == NCH - 1)),
                                reads=[b_w, b_h], writes=[b_ps], pe_accum=(c > 0))
                        o_t, b_o = osb[oi % 4]
                        oi += 1
                        scale = 0.125 if which == 0 else 1.0
                        if rope:
                            k2 = pi % 2
                            self.apply_rope(ps, b_ps, ps2, b_ps2, 128, C_PSW, C, b_C, S, b_S, t0, scale, o_t, b_o,
                                            qb[k2][0], qb[k2][1], t1[k2][0], t1[k2][1], t2[k2][0], t2[k2][1])
                        else:
                            P.op("act", lambda e, o_t=o_t, ps=ps, scale=scale: e.activation(
                                out=o_t[:], in_=ps[:], func=AF.Copy, scale=scale),
                                reads=[b_ps], writes=[b_o])
                        if which == 0:
                            dst_ap = qT.ap()[2 * pr:2 * pr + 2, :, t0:t0 + TT].rearrange("h d t -> (h d) t")
                            P.dma("sp", lambda e, o_t=o_t, dst_ap=dst_ap: e.dma_start(out=dst_ap, in_=o_t[:]),
                                  reads=[b_o], writes=[self.dbuf["qT64"]], sb=b_o)
                        else:
                            kn, kr0 = self.kown(64, 2 * pr)
                            dst_ap = self.dram[kn].ap()[kr0:kr0 + 128, t0:t0 + TT]
                            P.dma("sp", lambda e, o_t=o_t, dst_ap=dst_ap: e.dma_start(out=dst_ap, in_=o_t[:]),
                                  reads=[b_o], writes=[self.dbuf[kn]], sb=b_o)
                for tb in range(TT // 128):
                    v_t, b_v = vsb[vi % 2]
                    vi += 1
                    for half in range(2):
                        ps, b_ps = banks[4 + half]
                        for c in range(NCH):
                            P.op("pe", lambda e, c=c, half=half, tb=tb, ps=ps, h_t=h_t: e.matmul(
                                ps[:], lhsT=h_t[:, c, tb * 128:(tb + 1) * 128],
                                rhs=w[:, c, 2 * D + half * 512:2 * D + (half + 1) * 512],
                                start=(c == 0), stop=(c == NCH - 1)),
                                reads=[b_w, b_h], writes=[b_ps], pe_accum=(c > 0))
                        P.op("dve", lambda e, half=half, ps=ps, v_t=v_t: e.tensor_copy(
                            out=v_t[:, half * 512:(half + 1) * 512], in_=ps[:]),
                            reads=[b_ps], writes=[b_v])
                    vn, r0 = self.vown(t0 + tb * 128)
                    P.dma("sp", lambda e, v_t=v_t, r0=r0, vn=vn: e.dma_start(
                        out=self.dram[vn].ap()[r0:r0 + 128, :], in_=v_t[:]),
                        reads=[b_v], writes=[self.dbuf[vn]], sb=b_v)
            P.barrier()
        self.exchange_all(64)
        P.barrier()

    def pass_proj_mla(self, w_in, w_uq, w_uk, w_uv, gname, hn_name):
        P, NT = self.P, self.NT
        qT = self.dram["qT96"]
        b_qT = self.dbuf["qT96"]
        cst, b_cst = self.cst, self.b_cst
        with ExitStack() as stack:
            banks = self.psum_banks(stack)
            win, b_win = self.sb(stack, "win", [128, NCH, 672], BF16)
            wuq, b_wuq = self.sb(stack, "wuq", [128, 3, 1536], BF16)
            wuk, b_wuk = self.sb(stack, "wuk", [128, 2, 1024], BF16)
            wuv, b_wuv = self.sb(stack, "wuv", [128, 2, 1024], BF16)
            for (t, b, nm) in ((win, b_win, w_in), (wuq, b_wuq, w_uq), (wuk, b_wuk, w_uk), (wuv, b_wuv, w_uv)):
                P.dma("pool", lambda e, t=t, nm=nm: e.dma_start(out=t[:], in_=self.dram[nm].ap()),
                      reads=[self.dbuf[nm]], writes=[b])
            mg, b_mg = self.sb(stack, "mlag", [128, 5], F32)
            P.dma("sp", lambda e: e.dma_start(out=mg[:], in_=self.dram[gname].ap()),
                  reads=[self.dbuf[gname]], writes=[b_mg])
            (Cq, b_Cq), (Sq, b_Sq) = self.rope_tables(stack, 2, 96)
            (Ck, b_Ck), (Sk, b_Sk) = self.rope_tables(stack, 4, 32)
            hn = [self.sb(stack, "mhn%d" % i, [128, NCH, TT], BF16) for i in range(2)]
            cqf, b_cqf = self.sb(stack, "cqf", [128, 3, TT], F32)
            ckf, b_ckf = self.sb(stack, "ckf", [128, 2, TT], F32)
            sq, b_sq = self.sb(stack, "msq", [128, 3, TT], BF16)
            rstd, b_rstd = self.sb(stack, "mrstd", [128, TT], F32)
            cqn, b_cqn = self.sb(stack, "cqn", [128, 3, TT], BF16)
            ckn, b_ckn = self.sb(stack, "ckn", [128, 2, TT], BF16)
            osb = [self.sb(stack, "mosb%d" % i, [128, TT], BF16) for i in range(4)]
            vsb = [self.sb(stack, "mvsb%d" % i, [128, D], BF16) for i in range(2)]
            qb = [self.sb(stack, "mqb%d" % i, [128, TT], BF16) for i in range(2)]
            t1 = [self.sb(stack, "mt1%d" % i, [128, TT], F32) for i in range(2)]
            t2 = [self.sb(stack, "mt2%d" % i, [128, TT], F32) for i in range(2)]
            hnd = self.dram[hn_name]
            oi = 0
            vi = 0
            pi = 0
            for tt in range(NT // TT):
                t0 = tt * TT
                h_t, b_h = hn[tt % 2]
                P.dma("sp", lambda e, t0=t0, h_t=h_t: e.dma_start(
                    out=h_t[:], in_=hnd.ap()[:, :, t0:t0 + TT].rearrange("c p t -> p c t")),
                    reads=[self.dbuf[hn_name]], writes=[b_h])
                for m in range(5):
                    ps, b_ps = banks[pi % 2]
                    pi += 1
                    for c in range(NCH):
                        P.op("pe", lambda e, c=c, m=m, ps=ps, h_t=h_t: e.matmul(
                            ps[:], lhsT=win[:, c, m * 128:(m + 1) * 128], rhs=h_t[:, c, :],
                            start=(c == 0), stop=(c == NCH - 1)),
                            reads=[b_win, b_h], writes=[b_ps], pe_accum=(c > 0))
                    if m < 3:
                        P.op("act", lambda e, m=m, ps=ps: e.copy(out=cqf[:, m, :], in_=ps[:]),
                             reads=[b_ps], writes=[b_cqf])
                    else:
                        P.op("act", lambda e, m=m, ps=ps: e.copy(out=ckf[:, m - 3, :], in_=ps[:]),
                             reads=[b_ps], writes=[b_ckf])
                ps, b_ps = banks[pi % 2]
                ps2, b_ps2 = banks[2 + pi % 2]
                pi += 1
                for c in range(NCH):
                    P.op("pe", lambda e, c=c, ps=ps, h_t=h_t: e.matmul(
                        ps[0:32, :], lhsT=win[:, c, 640:672], rhs=h_t[:, c, :],
                        start=(c == 0), stop=(c == NCH - 1)),
                        reads=[b_win, b_h], writes=[b_ps], pe_accum=(c > 0))
                kr_t, b_kr = osb[oi % 4]
                oi += 1
                self.apply_rope(ps, b_ps, ps2, b_ps2, 32, C_PSW + 256, Ck, b_Ck, Sk, b_Sk, t0, 1.0, kr_t, b_kr,
                                qb[0][0], qb[0][1], t1[0][0], t1[0][1], t2[0][0], t2[0][1])
                for h in range(H):
                    kn, kr0 = self.kown(96, h)
                    P.dma("sp", lambda e, kn=kn, kr0=kr0, kr_t=kr_t, t0=t0: e.dma_start(
                        out=self.dram[kn].ap()[kr0 + 64:kr0 + 96, t0:t0 + TT], in_=kr_t[0:32, :]),
                        reads=[b_kr], writes=[self.dbuf[kn]], sb=b_kr)
                for (src, b_src, nch, gcol, dst, b_dst, inv) in ((cqf, b_cqf, 3, 0, cqn, b_cqn, 1.0 / 384.0),
                                                                 (ckf, b_ckf, 2, 3, ckn, b_ckn, 1.0 / 256.0)):
                    psn, b_psn = banks[6]
                    P.op("act", lambda e, src=src, nch=nch: e.activation(out=sq[:, 0:nch, :], in_=src[:, 0:nch, :],
                                                                         func=AF.Square),
                         reads=[b_src], writes=[b_sq])
                    for m in range(nch):
                        P.op("pe", lambda e, m=m, nch=nch, psn=psn: e.matmul(
                            psn[:], lhsT=cst[:, C_ONE:C_ONE + 128], rhs=sq[:, m, :], start=(m == 0), stop=(m == nch - 1)),
                            reads=[b_cst, b_sq], writes=[b_psn], pe_accum=(m > 0))
                    P.op("act", lambda e, psn=psn, inv=inv: e.activation(out=rstd[:], in_=psn[:], func=AF.Sqrt,
                                                                         bias=self.epsc[:, 0:1], scale=inv),
                         reads=[b_psn, self.b_epsc], writes=[b_rstd])
                    P.op("dve", lambda e: e.reciprocal(out=rstd[:], in_=rstd[:]), reads=[b_rstd], writes=[b_rstd])
                    for m in range(nch):
                        P.op("dve", lambda e, m=m, src=src, dst=dst, gcol=gcol: e.scalar_tensor_tensor(
                            out=dst[:, m, :], in0=src[:, m, :], scalar=mg[:, gcol + m:gcol + m + 1], in1=rstd[:],
                            op0=ALU.mult, op1=ALU.mult),
                            reads=[b_src, b_rstd, b_mg], writes=[b_dst])
                for h in range(H):
                    ps, b_ps = banks[pi % 2]
                    ps2, b_ps2 = banks[2 + pi % 2]
                    k2 = pi % 2
                    pi += 1
                    for m in range(3):
                        P.op("pe", lambda e, m=m, h=h, ps=ps: e.matmul(
                            ps[0:96, :], lhsT=wuq[:, m, h * 96:(h + 1) * 96], rhs=cqn[:, m, :],
                            start=(m == 0), stop=(m == 2)),
                            reads=[b_wuq, b_cqn], writes=[b_ps], pe_accum=(m > 0))
                    o_t, b_o = osb[oi % 4]
                    oi += 1
                    self.apply_rope(ps, b_ps, ps2, b_ps2, 96, C_PSW + 128, Cq, b_Cq, Sq, b_Sq, t0, 1.0, o_t, b_o,
                                    qb[k2][0], qb[k2][1], t1[k2][0], t1[k2][1], t2[k2][0], t2[k2][1])
                    P.dma("sp", lambda e, h=h, o_t=o_t, t0=t0: e.dma_start(
                        out=qT.ap()[h, :, t0:t0 + TT], in_=o_t[0:96, :]),
                        reads=[b_o], writes=[b_qT], sb=b_o)
                for pr in range(8):
                    ps, b_ps = banks[pi % 2]
                    pi += 1
                    for m in range(2):
                        P.op("pe", lambda e, m=m, pr=pr, ps=ps: e.matmul(
                            ps[:], lhsT=wuk[:, m, pr * 128:(pr + 1) * 128], rhs=ckn[:, m, :],
                            start=(m == 0), stop=(m == 1)),
                            reads=[b_wuk, b_ckn], writes=[b_ps], pe_accum=(m > 0))
                    o_t, b_o = osb[oi % 4]
                    oi += 1
                    P.op("act", lambda e, o_t=o_t, ps=ps: e.copy(out=o_t[:], in_=ps[:]), reads=[b_ps], writes=[b_o])
                    for two in range(2):
                        kn, kr0 = self.kown(96, 2 * pr + two)
                        P.dma("sp", lambda e, kn=kn, kr0=kr0, two=two, o_t=o_t, t0=t0: e.dma_start(
                            out=self.dram[kn].ap()[kr0:kr0 + 64, t0:t0 + TT], in_=o_t[two * 64:(two + 1) * 64, :]),
                            reads=[b_o], writes=[self.dbuf[kn]], sb=b_o)
                for tb in range(TT // 128):
                    v_t, b_v = vsb[vi % 2]
                    vi += 1
                    for half in range(2):
                        ps, b_ps = banks[4 + half]
                        for m in range(2):
                            P.op("pe", lambda e, m=m, half=half, tb=tb, ps=ps: e.matmul(
                                ps[:], lhsT=ckn[:, m, tb * 128:(tb + 1) * 128],
                                rhs=wuv[:, m, half * 512:(half + 1) * 512], start=(m == 0), stop=(m == 1)),
                                reads=[b_wuv, b_ckn], writes=[b_ps], pe_accum=(m > 0))
                        P.op("dve", lambda e, half=half, ps=ps, v_t=v_t: e.tensor_copy(
                            out=v_t[:, half * 512:(half + 1) * 512], in_=ps[:]),
                            reads=[b_ps], writes=[b_v])
                    vn, r0 = self.vown(t0 + tb * 128)
                    P.dma("sp", lambda e, v_t=v_t, r0=r0, vn=vn: e.dma_start(
                        out=self.dram[vn].ap()[r0:r0 + 128, :], in_=v_t[:]),
                        reads=[b_v], writes=[self.dbuf[vn]], sb=b_v)
            P.barrier()
        self.exchange_all(96)
        P.barrier()

    def exchange(self, src, dst):
        P = self.P
        groups = [[0, 1], [2, 3], [4, 5], [6, 7]]
        s_t, d_t = self.dram[src], self.dram[dst]
        P.dma("pool", lambda e: e.collective_compute("AllGather", ALU.bypass, replica_groups=groups,
                                                     ins=[s_t.ap()], outs=[d_t.ap()]),
              reads=[self.dbuf[src]], writes=[self.dbuf[dst]], sb=self.dbuf[dst], inc=1)

    def pass_o(self, wname, h_src, h_dst):
        P, NT = self.P, self.NT
        aT = self.dram["aT"]
        hs, hd = self.dram[h_src], self.dram[h_dst]
        with ExitStack() as stack:
            banks = self.psum_banks(stack)
            w, b_w = self.sb(stack, "wo", [128, NCH, D], BF16)
            P.dma("pool", lambda e: e.dma_start(out=w[:], in_=self.dram[wname].ap()),
                  reads=[self.dbuf[wname]], writes=[b_w])
            at = [self.sb(stack, "oat%d" % i, [128, NCH, TT], BF16) for i in range(2)]
            ht = [self.sb(stack, "oht%d" % i, [128, NCH, TT], F32) for i in range(2)]
            pi = 0
            for tt in range(NT // TT):
                t0 = tt * TT
                a_t, b_a = at[tt % 2]
                h_t, b_h = ht[tt % 2]
                for two in range(2):
                    P.dma("sp", lambda e, t0=t0, a_t=a_t, two=two: e.dma_start(
                        out=a_t[two * 64:(two + 1) * 64, :, :],
                        in_=aT.ap()[:, :, t0:t0 + TT].rearrange("(c two) d t -> two d c t", two=2)[two]),
                        reads=[self.dbuf["aT"]], writes=[b_a])
                P.dma("act", lambda e, t0=t0, h_t=h_t: e.dma_start(
                    out=h_t[:], in_=hs.ap()[:, :, t0:t0 + TT].rearrange("c p t -> p c t")),
                    reads=[self.dbuf[h_src]], writes=[b_h])
                for dc in range(NCH):
                    ps, b_ps = banks[pi % 4]
                    pi += 1
                    for c in range(NCH):
                        P.op("pe", lambda e, c=c, dc=dc, ps=ps, a_t=a_t: e.matmul(
                            ps[:], lhsT=w[:, c, dc * 128:(dc + 1) * 128], rhs=a_t[:, c, :],
                            start=(c == 0), stop=(c == NCH - 1)),
                            reads=[b_w, b_a], writes=[b_ps], pe_accum=(c > 0))
                    P.op("dve", lambda e, dc=dc, ps=ps, h_t=h_t: e.tensor_tensor(
                        out=h_t[:, dc, :], in0=ps[:], in1=h_t[:, dc, :], op=ALU.add),
                        reads=[b_ps, b_h], writes=[b_h])
                P.dma("sp", lambda e, t0=t0, h_t=h_t: e.dma_start(
                    out=hd.ap()[:, :, t0:t0 + TT].rearrange("c p t -> p c t"), in_=h_t[:]),
                    reads=[b_h], writes=[self.dbuf[h_dst]], sb=b_h)
            P.barrier()

    def pass_attn(self, kind):
        P, NT = self.P, self.NT
        KD = 96 if kind == "mla" else 64
        NJ = NT // 128
        NQT = NT // TT
        qT, aT = self.dram["qT%d" % KD], self.dram["aT"]
        b_qT, b_aT = self.dbuf["qT%d" % KD], self.dbuf["aT"]
        cst, b_cst = self.cst, self.b_cst
        sm_scale = float(96 ** -0.5) if kind == "mla" else 1.0
        with ExitStack() as stack:
            banks = self.psum_banks(stack)
            vsb, b_vsb = self.sb(stack, "avsb", [128, 2 * NJ, D], BF16)
            for r in range(2):
                for j0 in range(0, NJ, 4):
                    vn, row0 = self.vall(r, j0 * 128)
                    P.dma("sp" if r == 0 else "pool", lambda e, row0=row0, r=r, j0=j0, vn=vn: e.dma_start(
                        out=vsb[:, r * NJ + j0:r * NJ + j0 + 4, :],
                        in_=self.dram[vn].ap()[row0:row0 + 512, :].rearrange("(j p) f -> p j f", p=128)),
                        reads=[self.dbuf[vn]], writes=[b_vsb])
            mask, b_mask = self.sb(stack, "amask", [128, 8, TT], BF16)
            mk_idx = 0 if kind == "sb" else 1
            P.dma("pool", lambda e: e.dma_start(out=mask[:], in_=self.dram["masks"].ap()[mk_idx]),
                  reads=[self.dbuf["masks"]], writes=[b_mask])
            ksb = [self.sb(stack, "aksb%d" % i, [KD, 2, NT], BF16) for i in range(2)]
            qsb = [self.sb(stack, "aqsb%d" % i, [KD, NT], BF16) for i in range(2)]
            e_t = [self.sb(stack, "ae%d" % i, [128, TT], F32) for i in range(2)]
            L_t = [self.sb(stack, "aL%d" % i, [128, TT], BF16) for i in range(2)]
            w_t = [self.sb(stack, "aw%d" % i, [128, TT], BF16) for i in range(3)]
            carry = [self.sb(stack, "acar%d" % i, [64, TT], BF16) for i in range(2)]
            cf, b_cf = self.sb(stack, "acf", [64, TT], F32)
            o_f, b_of = self.sb(stack, "aof", [64, TT], F32)
            rden, b_rden = self.sb(stack, "arden", [64, TT], F32)
            o_sb = [self.sb(stack, "aosb%d" % i, [64, TT], BF16) for i in range(2)]
            if kind == "moba":
                negsel = [self.sb(stack, "ansel%d" % i, [16, NT], BF16) for i in range(2)]
                km, b_km = self.sb(stack, "akm", [64, 2, NJ], F32)
                kms, b_kms = self.sb(stack, "akms", [64, NJ], F32)
                khi, b_khi = self.sb(stack, "akhi", [64, NJ], BF16)
                klo, b_klo = self.sb(stack, "aklo", [64, NJ], BF16)
                gm = [self.sb(stack, "agm%d" % i, [128, 16], F32) for i in range(2)]
                m8 = [self.sb(stack, "am8%d" % i, [128, 8], F32) for i in range(2)]
                sel = [self.sb(stack, "asel%d" % i, [128, NJ], F32) for i in range(2)]
                nsb = [self.sb(stack, "ansb%d" % i, [128, NJ], BF16) for i in range(2)]
                pa, b_pa = self.sb(stack, "apa", [128, NJ, NJ], F32)
                p01, b_p01 = self.sb(stack, "ap01", [128, NJ, NJ], F32)
                o01, b_o01 = self.sb(stack, "ao01", [128, NJ, NJ], F32)
                pat = [[1, NJ], [-1, NJ]]
                P.op("pool", lambda e: e.memset(pa[:], 0.0), writes=[b_pa])
                P.op("pool", lambda e: e.affine_select(out=pa[:], in_=pa[:], pattern=pat, compare_op=ALU.is_ge,
                                                       fill=NEG, base=-1, channel_multiplier=0),
                     reads=[b_pa], writes=[b_pa])
                P.op("pool", lambda e: e.memset(p01[:], 1.0), writes=[b_p01])
                P.op("pool", lambda e: e.affine_select(out=p01[:], in_=p01[:], pattern=pat, compare_op=ALU.is_ge,
                                                       fill=0.0, base=-1, channel_multiplier=0),
                     reads=[b_p01], writes=[b_p01])
                P.op("pool", lambda e: e.memset(o01[:], 1.0), writes=[b_o01])
                P.op("pool", lambda e: e.affine_select(out=o01[:], in_=o01[:], pattern=pat, compare_op=ALU.is_equal,
                                                       fill=0.0, base=0, channel_multiplier=0),
                     reads=[b_o01], writes=[b_o01])
                for g_t, b_g in gm:
                    P.op("pool", lambda e, g_t=g_t: e.memset(g_t[:], NEG), writes=[b_g])
            zi = 0
            wi = 0
            ci = 0
            chain = 0
            for h in range(H):
                k_t, b_k = ksb[h % 2]
                q_t, b_q = qsb[h % 2]
                for r in range(2):
                    kn, row0 = self.kall(KD, h, r)
                    P.dma("sp" if r == 0 else "pool", lambda e, row0=row0, r=r, k_t=k_t, kn=kn: e.dma_start(
                        out=k_t[:, r, :], in_=self.dram[kn].ap()[row0:row0 + KD, :]),
                        reads=[self.dbuf[kn]], writes=[b_k])
                P.dma("sp", lambda e, h=h, q_t=q_t: e.dma_start(out=q_t[:], in_=qT.ap()[h]),
                      reads=[b_qT], writes=[b_q])
                if kind == "moba":
                    ns_t, b_ns = negsel[h % 2]
                    P.op("dve", lambda e, k_t=k_t: e.tensor_reduce(
                        out=km[:], in_=k_t[0:64, :, :].rearrange("d r (j k) -> d r j k", k=128),
                        axis=AX.X, op=ALU.add), reads=[b_k], writes=[b_km])
                    P.op("dve", lambda e: e.tensor_tensor(out=kms[:], in0=km[:, 0, :], in1=km[:, 1, :], op=ALU.add),
                         reads=[b_km], writes=[b_kms])
                    P.op("dve", lambda e: e.tensor_scalar(out=kms[:], in0=kms[:], scalar1=1.0 / 256.0, scalar2=None,
                                                          op0=ALU.mult), reads=[b_kms], writes=[b_kms])
                    P.op("dve", lambda e: e.tensor_copy(out=khi[:], in_=kms[:]), reads=[b_kms], writes=[b_khi])
                    P.op("dve", lambda e: e.tensor_tensor(out=klo[:], in0=kms[:], in1=khi[:], op=ALU.subtract),
                         reads=[b_kms, b_khi], writes=[b_klo])
                    for js in range(NJ):
                        psG, b_psG = banks[6]
                        psT, b_psT = banks[7]
                        g_t, b_g = gm[js % 2]
                        m_t, b_m = m8[js % 2]
                        s_t, b_s = sel[js % 2]
                        n_t, b_n = nsb[js % 2]
                        P.op("pe", lambda e, js=js, q_t=q_t, psG=psG: e.matmul(
                            psG[:, 0:NJ], lhsT=q_t[0:64, js * 128:(js + 1) * 128], rhs=khi[:], start=True, stop=False),
                            reads=[b_q, b_khi], writes=[b_psG])
                        P.op("pe", lambda e, js=js, q_t=q_t, psG=psG: e.matmul(
                            psG[:, 0:NJ], lhsT=q_t[0:64, js * 128:(js + 1) * 128], rhs=klo[:], start=False, stop=True),
                            reads=[b_q, b_klo], writes=[b_psG], pe_accum=True)
                        P.op("dve", lambda e, js=js, g_t=g_t, psG=psG: e.tensor_tensor(
                            out=g_t[:, 0:NJ], in0=psG[:, 0:NJ], in1=pa[:, js, :], op=ALU.add),
                            reads=[b_psG, b_pa], writes=[b_g])
                        P.op("dve", lambda e, g_t=g_t, m_t=m_t: e.max(out=m_t[:], in_=g_t[:]),
                             reads=[b_g], writes=[b_m])
                        P.op("dve", lambda e, g_t=g_t, m_t=m_t, s_t=s_t: e.tensor_scalar(
                            out=s_t[:], in0=g_t[:, 0:NJ], scalar1=m_t[:, 2:3], scalar2=None, op0=ALU.is_ge),
                            reads=[b_g, b_m], writes=[b_s])
                        P.op("dve", lambda e, js=js, s_t=s_t: e.tensor_tensor(
                            out=s_t[:], in0=s_t[:], in1=p01[:, js, :], op=ALU.mult),
                            reads=[b_s, b_p01], writes=[b_s])
                        P.op("dve", lambda e, js=js, s_t=s_t: e.tensor_tensor(
                            out=s_t[:], in0=s_t[:], in1=o01[:, js, :], op=ALU.add),
                            reads=[b_s, b_o01], writes=[b_s])
                        P.op("dve", lambda e, s_t=s_t, n_t=n_t: e.tensor_scalar(
                            out=n_t[:], in0=s_t[:], scalar1=-1.0, scalar2=BIG, op0=ALU.add, op1=ALU.mult),
                            reads=[b_s], writes=[b_n])
                        P.op("pe", lambda e, n_t=n_t, psT=psT: e.matmul(
                            psT[0:NJ, 0:128], lhsT=n_t[:], rhs=cst[:, C_ID:C_ID + 128], start=True, stop=True),
                            reads=[b_n, b_cst], writes=[b_psT])
                        P.op("act", lambda e, js=js, ns_t=ns_t, psT=psT: e.copy(
                            out=ns_t[0:NJ, js * 128:(js + 1) * 128], in_=psT[0:NJ, 0:128]),
                            reads=[b_psT], writes=[b_ns])
                for qt in range(NQT):
                    nkb = 8 * qt + 8
                    psO, b_psO = banks[4 + chain % 2]
                    psC, b_psC = banks[2 + chain % 2]
                    chain += 1
                    qtile = q_t[0:KD, qt * TT:(qt + 1) * TT]
                    for i in range(nkb):
                        g = (nkb - 1 - i) if kind == "sb" else i
                        r, j = g % 2, g // 2
                        d = g - 8 * qt
                        kblk = k_t[0:KD, r, j * 128:(j + 1) * 128]
                        vblk = vsb[:, r * NJ + j, h * 64:(h + 1) * 64]
                        psZ, b_psZ = banks[zi % 2]
                        zi += 1
                        ww, b_ww = w_t[wi % 3]
                        wi += 1
                        if kind == "sb":
                            ee, b_ee = e_t[i % 2]
                            LL, b_LL = L_t[i % 2]
                            P.op("pe", lambda e, psZ=psZ, kblk=kblk, qtile=qtile: e.matmul(
                                psZ[:], lhsT=kblk, rhs=qtile, start=True, stop=True),
                                reads=[b_k, b_q], writes=[b_psZ])
                            P.op("act", lambda e, psZ=psZ, ee=ee: e.activation(out=ee[:], in_=psZ[:], func=AF.Exp),
                                 reads=[b_psZ], writes=[b_ee])
                            P.op("act", lambda e, ee=ee, LL=LL: e.activation(out=LL[:], in_=ee[:], func=AF.Ln, bias=1.0),
                                 reads=[b_ee], writes=[b_LL])
                            if d >= 0:
                                P.op("pool", lambda e, LL=LL, d=d: e.tensor_tensor(
                                    out=LL[:], in0=LL[:], in1=mask[:, d, :], op=ALU.mult),
                                    reads=[b_LL, b_mask], writes=[b_LL])
                            psA, b_psA = banks[6 + i % 2]
                            P.op("pe", lambda e, psA=psA, kblk=kblk, qtile=qtile: e.matmul(
                                psA[:], lhsT=kblk, rhs=qtile, start=True, stop=False),
                                reads=[b_k, b_q], writes=[b_psA])
                            P.op("pe", lambda e, psA=psA, LL=LL, i=i: e.matmul(
                                psA[:], lhsT=cst[:, C_NTRI:C_NTRI + 128], rhs=LL[:], start=False, stop=(i == 0)),
                                reads=[b_cst, b_LL], writes=[b_psA], pe_accum=True)
                            if i > 0:
                                cc, b_cc = carry[(ci - 1) % 2]
                                P.op("pe", lambda e, psA=psA, cc=cc: e.matmul(
                                    psA[:], lhsT=cst[0:64, C_SEL:C_SEL + 128], rhs=cc[:], start=False, stop=True),
                                    reads=[b_cst, b_cc], writes=[b_psA], pe_accum=True)
                            P.op("act", lambda e, psA=psA, ww=ww: e.activation(out=ww[:], in_=psA[:], func=AF.Exp),
                                 reads=[b_psA], writes=[b_ww])
                            if d >= 0:
                                P.op("pool", lambda e, ww=ww, d=d: e.tensor_tensor(
                                    out=ww[:], in0=ww[:], in1=mask[:, d, :], op=ALU.mult),
                                    reads=[b_ww, b_mask], writes=[b_ww])
                            P.op("pe", lambda e, psO=psO, vblk=vblk, ww=ww, i=i, nkb=nkb: e.matmul(
                                psO[0:64, :], lhsT=vblk, rhs=ww[:], start=(i == 0), stop=(i == nkb - 1)),
                                reads=[b_vsb, b_ww], writes=[b_psO], pe_accum=(i > 0))
                            if i < nkb - 1:
                                P.op("pe", lambda e, psC=psC, LL=LL: e.matmul(
                                    psC[0:64, :], lhsT=cst[:, C_ONE:C_ONE + 64], rhs=LL[:], start=True, stop=True),
                                    reads=[b_cst, b_LL], writes=[b_psC])
                                cc, b_cc = carry[ci % 2]
                                ci += 1
                                if i == 0:
                                    P.op("dve", lambda e, psC=psC: e.tensor_copy(out=cf[:], in_=psC[0:64, :]),
                                         reads=[b_psC], writes=[b_cf])
                                else:
                                    P.op("dve", lambda e, psC=psC: e.tensor_tensor(
                                        out=cf[:], in0=psC[0:64, :], in1=cf[:], op=ALU.add),
                                        reads=[b_psC, b_cf], writes=[b_cf])
                                P.op("pool", lambda e, cc=cc: e.tensor_copy(out=cc[:], in_=cf[:]),
                                     reads=[b_cf], writes=[b_cc])
                                P.op("pool", lambda e, cc=cc: e.tensor_tensor(
                                    out=cc[32:64, :], in0=cf[32:64, :], in1=cc[32:64, :], op=ALU.subtract),
                                    reads=[b_cf, b_cc], writes=[b_cc])
                        else:
                            P.op("pe", lambda e, psZ=psZ, kblk=kblk, qtile=qtile: e.matmul(
                                psZ[:], lhsT=kblk, rhs=qtile, start=True, stop=(kind != "moba")),
                                reads=[b_k, b_q], writes=[b_psZ])
                            if kind == "moba":
                                n = g // 2
                                P.op("pe", lambda e, psZ=psZ, n=n, ns_t=ns_t, qt=qt: e.matmul(
                                    psZ[:], lhsT=cst[0:NJ, C_EN + n * 128:C_EN + (n + 1) * 128],
                                    rhs=ns_t[0:NJ, qt * TT:(qt + 1) * TT], start=False, stop=True),
                                    reads=[b_cst, b_ns], writes=[b_psZ], pe_accum=True)
                            P.op("act", lambda e, psZ=psZ, ww=ww: e.activation(
                                out=ww[:], in_=psZ[:], func=AF.Exp, scale=sm_scale),
                                reads=[b_psZ], writes=[b_ww])
                            if d >= 0:
                                P.op("pool", lambda e, ww=ww, d=d: e.tensor_tensor(
                                    out=ww[:], in0=ww[:], in1=mask[:, d, :], op=ALU.mult),
                                    reads=[b_ww, b_mask], writes=[b_ww])
                            P.op("pe", lambda e, psO=psO, vblk=vblk, ww=ww, i=i, nkb=nkb: e.matmul(
                                psO[0:64, :], lhsT=vblk, rhs=ww[:], start=(i == 0), stop=(i == nkb - 1)),
                                reads=[b_vsb, b_ww], writes=[b_psO], pe_accum=(i > 0))
                            P.op("pe", lambda e, psC=psC, ww=ww, i=i, nkb=nkb: e.matmul(
                                psC[0:64, :], lhsT=cst[:, C_ONE:C_ONE + 64], rhs=ww[:],
                                start=(i == 0), stop=(i == nkb - 1)),
                                reads=[b_cst, b_ww], writes=[b_psC], pe_accum=(i > 0))
                    oo, b_oo = o_sb[chain % 2]
                    if kind == "sb":
                        P.op("act", lambda e, psO=psO, oo=oo: e.copy(out=oo[:], in_=psO[0:64, :]),
                             reads=[b_psO], writes=[b_oo])
                    else:
                        P.op("dve", lambda e, psC=psC: e.reciprocal(out=rden[:], in_=psC[0:64, :]),
                             reads=[b_psC], writes=[b_rden])
                        P.op("act", lambda e, psO=psO: e.copy(out=o_f[:], in_=psO[0:64, :]),
                             reads=[b_psO], writes=[b_of])
                        P.op("pool", lambda e, oo=oo: e.tensor_tensor(out=oo[:], in0=o_f[:], in1=rden[:], op=ALU.mult),
                             reads=[b_of, b_rden], writes=[b_oo])
                    P.dma("sp", lambda e, h=h, qt=qt, oo=oo: e.dma_start(
                        out=aT.ap()[h, :, qt * TT:(qt + 1) * TT], in_=oo[:]),
                        reads=[b_oo], writes=[b_aT], sb=b_oo)
            P.barrier()


def lay_rows(w):
    k = w.shape[0] // 128
    return np.ascontiguousarray(w.reshape(k, 128, w.shape[1]).transpose(1, 0, 2))


def own_tokens(xb, role):
    S = xb.shape[0]
    return np.ascontiguousarray(xb.reshape(S // 256, 2, 128, -1)[:, role].reshape(S // 2, -1))


def merge_tokens(y0, y1):
    n = y0.shape[0] // 128
    out = np.stack([y0.reshape(n, 128, -1), y1.reshape(n, 128, -1)], axis=1)
    return out.reshape(2 * y0.shape[0], -1)


NT_CORE = 2048
KINDS = ["sb", "moba", "mla", "sb"]


def build_full(NT=NT_CORE, depth=DEPTH):
    m = MKA(NT)
    m.din("xT", [NCH, 128, NT])
    m.din("gains", [128, 3 * depth + 1, NCH])
    m.din("consts", [128, CW]); m.din("ropec", [128, 8]); m.din("masks", [2, 128, 8, 512]); m.din("pos", [1, NT])
    for i in range(depth):
        for f in (1, 2):
            m.din("wgu%d_%d" % (f, i), [NFG, 128, 2, NCH, FG * 128])
            m.din("wd%d_%d" % (f, i), [NCH, 128, NFC, 128])
        kind = KINDS[i]
        if kind == "mla":
            m.din("w_in_%d" % i, [128, NCH, 672]); m.din("w_uq_%d" % i, [128, 3, 1536])
            m.din("w_uk_%d" % i, [128, 2, 1024]); m.din("w_uv_%d" % i, [128, 2, 1024]); m.din("mlag_%d" % i, [128, 5])
        else:
            m.din("wqkv_%d" % i, [128, NCH, 3 * D])
        m.din("wo_%d" % i, [128, NCH, D])
    m.dout("yT", [NCH, 128, NT])
    m.dscr("hA", [NCH, 128, NT], F32); m.dscr("hB", [NCH, 128, NT], F32); m.dscr("hnmix", [NCH, 128, NT], BF16)
    m.alloc_scratch()
    with ExitStack() as cst:
        m.setup_consts(cst, m.dram["gains"], 3 * depth + 1)
        m.setup_attn_consts(cst)
        cur = "xT"
        for i in range(depth):
            kind = KINDS[i]
            m.pass_ffn(cur, "hA", "wgu1_%d" % i, "wd1_%d" % i, 3 * i, epi=("norm", 3 * i + 1, "hnmix"))
            if kind == "mla":
                m.pass_proj_mla("w_in_%d" % i, "w_uq_%d" % i, "w_uk_%d" % i, "w_uv_%d" % i, "mlag_%d" % i, "hnmix")
            else:
                m.pass_proj_qkv(kind, "wqkv_%d" % i, "hnmix")
            m.pass_attn(kind)
            m.pass_o("wo_%d" % i, "hA", "hB")
            last = i == depth - 1
            m.pass_ffn("hB", "hA", "wgu2_%d" % i, "wd2_%d" % i, 3 * i + 2,
                       epi=(("final", 3 * depth, "yT") if last else None))
            cur = "hA"
        nc = m.finish(["yT"])
    return m, nc


def host_inputs(inp, NT=NT_CORE, depth=DEPTH):
    f32 = lambda a: np.asarray(a, dtype=np.float32)
    shared = {}
    gs = []
    for i in range(depth):
        gs += [f32(inp["norm_ffn1"])[i], f32(inp["norm_mix"])[i], f32(inp["norm_ffn2"])[i]]
    gs.append(f32(inp["final_norm"]))
    shared["gains"] = lay_gain(gs)
    shared["consts"] = host_consts()
    shared["ropec"] = host_ropec()
    cnt = {"sb": 0, "moba": 0, "mla": 0}
    for i in range(depth):
        shared["wgu1_%d" % i] = lay_wgu(f32(inp["ffn1_w_gate_up"])[i])
        shared["wd1_%d" % i] = lay_wd(f32(inp["ffn1_w_down"])[i])
        shared["wgu2_%d" % i] = lay_wgu(f32(inp["ffn2_w_gate_up"])[i])
        shared["wd2_%d" % i] = lay_wd(f32(inp["ffn2_w_down"])[i])
        kind = KINDS[i]
        j = cnt[kind]
        cnt[kind] += 1
        if kind == "mla":
            shared["w_in_%d" % i] = lay_rows(f32(inp["mla_w_in"])[j])
            shared["w_uq_%d" % i] = lay_rows(f32(inp["mla_w_uq"])[j])
            wukv = f32(inp["mla_w_ukv"])[j].reshape(256, 16, 2, 64)
            shared["w_uk_%d" % i] = lay_rows(np.ascontiguousarray(wukv[:, :, 0].reshape(256, 1024)))
            shared["w_uv_%d" % i] = lay_rows(np.ascontiguousarray(wukv[:, :, 1].reshape(256, 1024)))
            shared["mlag_%d" % i] = np.ascontiguousarray(np.concatenate(
                [f32(inp["mla_q_norm"])[j].reshape(3, 128), f32(inp["mla_kv_norm"])[j].reshape(2, 128)], 0).T)
            shared["wo_%d" % i] = lay_rows(f32(inp["mla_w_o"])[j])
        elif kind == "moba":
            shared["wqkv_%d" % i] = lay_rows(f32(inp["moba_w_qkv"])[j])
            shared["wo_%d" % i] = lay_rows(f32(inp["moba_w_o"])[j])
        else:
            shared["wqkv_%d" % i] = lay_rows(f32(inp["sb_w_qkv"])[j])
            shared["wo_%d" % i] = lay_rows(f32(inp["sb_w_o"])[j])
    x = f32(inp["x"])
    in_maps = []
    for c in range(8):
        b, r = c // 2, c % 2
        d = dict(shared)
        d["xT"] = lay_xT(own_tokens(x[b], r))
        d["masks"] = host_masks(r)
        d["pos"] = host_pos(r, NT)
        in_maps.append(d)
    return in_maps


_CACHE = {}


def kernel(**inputs):
    if "nc" not in _CACHE:
        _CACHE["nc"] = build_full()[1]
    nc = _CACHE["nc"]
    in_maps = host_inputs(inputs)
    res = run_bass_kernel_spmd(nc, in_maps, core_ids=list(range(8)))
    B = 4
    out = np.empty((B, 2 * NT_CORE, D), np.float32)
    for b in range(B):
        out[b] = merge_tokens(unlay_xT(res.results[2 * b]["yT"]), unlay_xT(res.results[2 * b + 1]["yT"]))
    return out
```

```python
import numpy as np
import concourse.bass as bass
import concourse.mybir as mybir

F32 = mybir.dt.float32
BF16 = mybir.dt.bfloat16
AF = mybir.ActivationFunctionType
ALU = mybir.AluOpType
AX = mybir.AxisListType

ENGS = ("pe", "act", "dve", "pool", "sp")


class Buf:
    __slots__ = ("name", "writers", "readers", "sem", "semval", "excl", "sw", "preaders")

    def __init__(self, name, excl=False):
        self.name = name
        self.excl = excl
        self.writers = {}
        self.readers = {}
        self.preaders = {}
        self.sem = None
        self.semval = 0
        self.sw = None


class Prog:
    def __init__(self, nc):
        self.nc = nc
        self.streams = {e: [] for e in ENGS}
        self.cnt = {e: 0 for e in ENGS}
        self.sem = {}
        self.seen = {e: {} for e in ENGS}
        self.snaps = {}
        self._ctx = []
        for e in ENGS:
            cm = nc.semaphore("s_" + e)
            self.sem[e] = cm.__enter__()
            self._ctx.append(cm)
        self.nwaits = 0
        self.nops = 0
        self.dmabufs = []
        self.free = {"hw": [], "sw": [], "cc": []}
        self._issw = {}

    def new_sem(self, name):
        self._semuid = getattr(self, "_semuid", 0) + 1
        cm = self.nc.semaphore("%s_%d" % (name, self._semuid))
        s = cm.__enter__()
        self._ctx.append(cm)
        return s

    def _need(self, eng, tok, waits):
        sem, val, _ = tok
        if self.seen[eng].get(sem.num, 0) >= val:
            return
        waits[sem.num] = (sem, max(val, waits.get(sem.num, (sem, 0))[1]))

    def _deps(self, eng, reads, writes, pe_accum=False, nowaw=False):
        waits = {}
        for b in reads:
            for k, tok in b.writers.items():
                self._need(eng, tok, waits)
            if b.excl:
                for k, tok in b.readers.items():
                    if k != eng:
                        self._need(eng, tok, waits)
        for b in writes:
            for k, tok in b.readers.items():
                self._need(eng, tok, waits)
            for k, tok in b.preaders.items():
                self._need(eng, tok, waits)
            for k, tok in b.writers.items():
                if pe_accum and k == "pe":
                    continue
                if nowaw and isinstance(k, tuple):
                    continue
                self._need(eng, tok, waits)
        sn = self.seen[eng]
        for num, (sem, val) in waits.items():
            sn[num] = max(sn.get(num, 0), val)
            snap = self.snaps.get((num, val))
            if snap:
                for k2, v2 in snap.items():
                    if sn.get(k2, 0) < v2:
                        sn[k2] = v2
        self.nwaits += len(waits)
        return list(waits.values())

    def _post(self, key, tok, reads, writes):
        for b in reads:
            old = b.readers.get(key)
            if old is None or old[1] < tok[1] or old[0].num != tok[0].num:
                b.readers[key] = tok
        for b in writes:
            if b.readers:
                b.preaders = b.readers
                b.readers = {}
                b.writers = {key: tok}
            else:
                b.writers[key] = tok

    def op(self, eng, fn, reads=(), writes=(), pe_accum=False):
        waits = self._deps(eng, reads, writes, pe_accum=pe_accum)
        self.cnt[eng] += 1
        n = self.cnt[eng]
        sem = self.sem[eng]
        self.snaps[(sem.num, n)] = dict(self.seen[eng])
        tok = (sem, n, None)
        self._post(eng, tok, reads, writes)
        self.streams[eng].append((waits, fn, (sem, 1)))
        self.nops += 1

    def dma(self, eng, fn, reads=(), writes=(), sb=None, inc=16, cls=None):
        dst = sb if sb is not None else writes[0]
        if cls is None:
            cls = "sw" if eng == "pool" else "hw"
        if cls == "sw":
            if dst.sw is None:
                dst.sw = Buf(dst.name + "_sw")
            dst = dst.sw
        if dst.sem is None:
            pool = self.free[cls]
            if pool:
                dst.sem, dst.semval = pool.pop()
            else:
                dst.sem = self.new_sem("d_" + dst.name)
                dst.semval = 0
            self.dmabufs.append(dst)
            self._issw[id(dst)] = cls
        waits = self._deps(eng, reads, writes, nowaw=True)
        dst.semval += inc
        tok = (dst.sem, dst.semval, None)
        self.snaps[(dst.sem.num, dst.semval)] = dict(self.seen[eng])
        self._post(("dma", dst.sem.num), tok, reads, writes)
        self.streams[eng].append((waits, fn, (dst.sem, inc)))
        self.nops += 1

    def barrier(self):
        for eng in ENGS:
            waits = {}
            for e2 in ENGS:
                if e2 != eng and self.cnt[e2] > 0:
                    self._need(eng, (self.sem[e2], self.cnt[e2], None), waits)
            if self.cnt[eng] > 0:
                self._need(eng, (self.sem[eng], self.cnt[eng], None), waits)
            for b in self.dmabufs:
                if b.semval > 0:
                    self._need(eng, (b.sem, b.semval, None), waits)
            sn = self.seen[eng]
            for num, (sem, val) in waits.items():
                sn[num] = max(sn.get(num, 0), val)
            self.nwaits += len(waits)
            self.streams[eng].append((list(waits.values()), None, None))
        self.recycle()

    def recycle(self):
        for b in self.dmabufs:
            self.free[self._issw[id(b)]].append((b.sem, b.semval))
            b.sem = None
            b.semval = 0
        self.dmabufs = []
        self._issw = {}

    def wait_all(self, eng, bufs):
        waits = self._deps(eng, bufs, ())
        self.streams[eng].append((waits, None, None))

    def emit(self):
        nc = self.nc
        engmap = {"pe": "tensor", "act": "scalar", "dve": "vector", "pool": "gpsimd", "sp": "sync"}
        with nc.Block() as block:
            for e in ENGS:
                stream = self.streams[e]

                def body(eng, stream=stream):
                    for waits, fn, inc in stream:
                        for sem, val in waits:
                            eng.wait_ge(sem, val)
                        if fn is not None:
                            ins = fn(eng)
                            ins.then_inc(inc[0], inc[1])
                getattr(block, engmap[e])(body)

    def close(self):
        for cm in reversed(self._ctx):
            cm.__exit__(None, None, None)
from contextlib import ExitStack
from concourse.bass_utils import run_bass_kernel_spmd

D = 1024
F = 2816
NCH = D // 128
NFC = F // 128
FG = 2
NFG = NFC // FG
H = 16
HD = 64
EPS = 1e-6
DEPTH = 4
ST = 1024
TT = 512


class MK:
    def __init__(self, NT, n_layers=DEPTH, cfg=None):
        self.NT = NT
        self.cfg = cfg or {}
        nc = self.nc = bass.Bass("TRN2", target_bir_lowering=False)
        self.P = Prog(nc)
        self.dram = {}
        self.dbuf = {}

    def din(self, name, shape, dt=F32):
        t = self.nc.dram_tensor(name, list(shape), dt, kind="ExternalInput")
        self.dram[name] = t
        self.dbuf[name] = Buf(name)
        return t

    def dout(self, name, shape, dt=F32):
        t = self.nc.dram_tensor(name, list(shape), dt, kind="ExternalOutput")
        self.dram[name] = t
        self.dbuf[name] = Buf(name)
        return t

    def dscr(self, name, shape, dt):
        t = self.nc.dram_tensor(name, list(shape), dt)
        self.dram[name] = t
        self.dbuf[name] = Buf(name)
        return t

    _uid = 0

    def uname(self, name):
        MK._uid += 1
        return "%s_%d" % (name, MK._uid)

    def sb(self, stack, name, shape, dt):
        name = self.uname(name)
        t = stack.enter_context(self.nc.sbuf_tensor(name, list(shape), dt))
        return t, Buf(name)

    def psum_banks(self, stack):
        banks = []
        for i in range(8):
            nm = self.uname("psb%d" % i)
            t = stack.enter_context(self.nc.psum_tensor(nm, [128, 512], F32))
            banks.append((t, Buf(nm, excl=True)))
        return banks

    def setup_consts(self, stack, gains_dram, ngain):
        P = self.P
        self.ones_mean, self.b_ones_mean = self.sb(stack, "ones_mean", [128, 128], BF16)
        P.op("pool", lambda e: e.memset(self.ones_mean[:], 1.0 / D), writes=[self.b_ones_mean])
        self.epsc, self.b_epsc = self.sb(stack, "epsc", [128, 1], F32)
        P.op("pool", lambda e: e.memset(self.epsc[:], EPS), writes=[self.b_epsc])
        self.gains, self.b_gains = self.sb(stack, "gains_sb", [128, ngain, NCH], F32)
        P.dma("sp", lambda e: e.dma_start(out=self.gains[:], in_=gains_dram.ap()),
              reads=[self.dbuf[gains_dram.name]], writes=[self.b_gains])

    def rmsnorm_tile(self, ht, b_ht, ntok, gi, out, b_out, sq, b_sq, rstd, b_rstd, bank):
        P = self.P
        ps, b_ps = bank
        for t0 in range(0, ntok, TT):
            P.op("act", lambda e, t0=t0: e.activation(out=sq[:, :, :], in_=ht[:, :, t0:t0 + TT], func=AF.Square),
                 reads=[b_ht], writes=[b_sq])
            for c in range(NCH):
                P.op("pe", lambda e, c=c: e.matmul(ps[:], lhsT=self.ones_mean[:], rhs=sq[:, c, :],
                                                   start=(c == 0), stop=(c == NCH - 1)),
                     reads=[self.b_ones_mean, b_sq], writes=[b_ps], pe_accum=(c > 0))
            P.op("act", lambda e: e.activation(out=rstd[:], in_=ps[:], func=AF.Sqrt, bias=self.epsc[:, 0:1]),
                 reads=[b_ps, self.b_epsc], writes=[b_rstd])
            P.op("dve", lambda e: e.reciprocal(out=rstd[:], in_=rstd[:]),
                 reads=[b_rstd], writes=[b_rstd])
            for c in range(NCH):
                P.op("dve", lambda e, c=c, t0=t0: e.scalar_tensor_tensor(
                    out=out[:, c, t0:t0 + TT], in0=ht[:, c, t0:t0 + TT], scalar=self.gains[:, gi, c:c + 1],
                    in1=rstd[:], op0=ALU.mult, op1=ALU.mult),
                     reads=[b_ht, b_rstd, self.b_gains], writes=[b_out])

    def pass_ffn(self, h_src, h_dst, wgu, wd, gi, epi=None):
        P, nc, NT = self.P, self.nc, self.NT
        ST = min(1024, NT)
        with ExitStack() as stack:
            banks = self.psum_banks(stack)
            ht = [self.sb(stack, "ht%d" % i, [128, NCH, ST], F32) for i in range(1)]
            hn, b_hn = self.sb(stack, "hn", [128, NCH, ST], BF16)
            act, b_act = self.sb(stack, "act", [128, NFC, ST], BF16)
            sq, b_sq = self.sb(stack, "sq", [128, NCH, TT], BF16)
            rstd, b_rstd = self.sb(stack, "rstd", [128, TT], F32)
            sil = [self.sb(stack, "sil%d" % i, [128, TT], F32) for i in range(2)]
            wg_s = [self.sb(stack, "wgu%d" % i, [128, 2, NCH, FG * 128], BF16) for i in range(2)]
            wd_s = [self.sb(stack, "wd%d" % i, [128, NFC, 128], BF16) for i in range(2)]
            if epi is not None:
                eo_dt = BF16 if epi[0] == "norm" else F32
                eo, b_eo = self.sb(stack, "eo", [128, NCH, ST], eo_dt)
            hsrc, hdst = self.dram[h_src], self.dram[h_dst]
            wgu_t, wd_t = self.dram[wgu], self.dram[wd]
            nst = NT // ST
            wi = 0
            di = 0
            for st in range(nst):
                t_lo = st * ST
                h_t, b_h = ht[0]
                P.dma("sp", lambda e, t_lo=t_lo: e.dma_start(
                    out=h_t[:], in_=hsrc.ap()[:, :, t_lo:t_lo + ST].rearrange("c p t -> p c t")),
                    reads=[self.dbuf[h_src]], writes=[b_h])
                self.rmsnorm_tile(h_t, b_h, ST, gi, hn, b_hn, sq, b_sq, rstd, b_rstd, banks[6])
                for fg in range(NFG):
                    w_t, b_w = wg_s[wi % 2]
                    wi += 1
                    P.dma("pool", lambda e, fg=fg, w_t=w_t: e.dma_start(out=w_t[:], in_=wgu_t.ap()[fg]),
                          reads=[self.dbuf[wgu]], writes=[b_w])
                    for f2 in range(FG):
                        f = fg * FG + f2
                        for ti in range(ST // TT):
                            t0 = ti * TT
                            k = (f * 2 + ti) % 2
                            pg, b_pg = banks[k]
                            pu, b_pu = banks[2 + k]
                            for c in range(NCH):
                                P.op("pe", lambda e, c=c, w_t=w_t, f2=f2, t0=t0, pg=pg: e.matmul(
                                    pg[:], lhsT=w_t[:, 0, c, f2 * 128:(f2 + 1) * 128], rhs=hn[:, c, t0:t0 + TT],
                                    start=(c == 0), stop=(c == NCH - 1)),
                                    reads=[b_w, b_hn], writes=[b_pg], pe_accum=(c > 0))
                            for c in range(NCH):
                                P.op("pe", lambda e, c=c, w_t=w_t, f2=f2, t0=t0, pu=pu: e.matmul(
                                    pu[:], lhsT=w_t[:, 1, c, f2 * 128:(f2 + 1) * 128], rhs=hn[:, c, t0:t0 + TT],
                                    start=(c == 0), stop=(c == NCH - 1)),
                                    reads=[b_w, b_hn], writes=[b_pu], pe_accum=(c > 0))
                            s_t, b_s = sil[k]
                            P.op("act", lambda e, s_t=s_t, pg=pg: e.activation(out=s_t[:], in_=pg[:], func=AF.Silu),
                                 reads=[b_pg], writes=[b_s])
                            P.op("dve", lambda e, s_t=s_t, pu=pu, f=f, t0=t0: e.tensor_tensor(
                                out=act[:, f, t0:t0 + TT], in0=pu[:], in1=s_t[:], op=ALU.mult),
                                reads=[b_pu, b_s], writes=[b_act])
                for dc in range(NCH):
                    w_t, b_w = wd_s[di % 2]
                    di += 1
                    P.dma("pool", lambda e, dc=dc, w_t=w_t: e.dma_start(out=w_t[:], in_=wd_t.ap()[dc]),
                          reads=[self.dbuf[wd]], writes=[b_w])
                    for ti in range(ST // TT):
                        t0 = ti * TT
                        po, b_po = banks[4 + (dc * 2 + ti) % 2]
                        for f in range(NFC):
                            P.op("pe", lambda e, f=f, w_t=w_t, t0=t0, po=po: e.matmul(
                                po[:], lhsT=w_t[:, f, :], rhs=act[:, f, t0:t0 + TT],
                                start=(f == 0), stop=(f == NFC - 1)),
                                reads=[b_w, b_act], writes=[b_po], pe_accum=(f > 0))
                        P.op("dve", lambda e, dc=dc, t0=t0, po=po: e.scalar_tensor_tensor(
                            out=h_t[:, dc, t0:t0 + TT], in0=po[:], scalar=0.5, in1=h_t[:, dc, t0:t0 + TT],
                            op0=ALU.mult, op1=ALU.add),
                            reads=[b_po, b_h], writes=[b_h])
                if epi is None or epi[0] == "norm":
                    P.dma("sp", lambda e, t_lo=t_lo: e.dma_start(
                        out=hdst.ap()[:, :, t_lo:t_lo + ST].rearrange("c p t -> p c t"), in_=h_t[:]),
                        reads=[b_h], writes=[self.dbuf[h_dst]], sb=b_h)
                if epi is not None:
                    self.rmsnorm_tile(h_t, b_h, ST, epi[1], eo, b_eo, sq, b_sq, rstd, b_rstd, banks[6])
                    dst = self.dram[epi[2]]
                    P.dma("sp", lambda e, t_lo=t_lo, dst=dst: e.dma_start(
                        out=dst.ap()[:, :, t_lo:t_lo + ST].rearrange("c p t -> p c t"), in_=eo[:]),
                        reads=[b_eo], writes=[self.dbuf[epi[2]]], sb=b_eo)
            P.barrier()

    def finish(self, out_names):
        P = self.P
        P.barrier()
        P.emit()
        P.close()
        return self.nc


def lay_wgu(w):
    g = w.reshape(NCH, 128, 2, NFG, FG * 128)
    return np.ascontiguousarray(g.transpose(3, 1, 2, 0, 4))


def lay_wd(w):
    g = w.reshape(NFC, 128, NCH, 128)
    return np.ascontiguousarray(g.transpose(2, 1, 0, 3))


def lay_gain(gs):
    a = np.stack(gs, 0).reshape(len(gs), NCH, 128)
    return np.ascontiguousarray(a.transpose(2, 0, 1))


def lay_xT(x):
    return np.ascontiguousarray(x.T.reshape(NCH, 128, x.shape[0]))


def unlay_xT(y):
    return np.ascontiguousarray(y.reshape(D, y.shape[-1]).T)


BIG = 30000.0
NEG = -1e30
THETA = 500000.0
C_NTRI = 0
C_ID = 128
C_PSW = 256
C_SEL = 640
C_ONE = 768
C_EN = 896
CW = 896 + 16 * 128


def host_consts():
    c = np.zeros((128, CW), np.float32)
    s = np.arange(128)[:, None]
    j = np.arange(128)[None, :]
    c[:, C_NTRI:C_NTRI + 128] = -(s >= j).astype(np.float32)
    c[:, C_ID:C_ID + 128] = (s == j)
    pm = np.zeros((128, 128), np.float32)
    for p in range(128):
        d = p % 64
        if d < 8:
            pm[p, p + 8] = 1
        elif d < 16:
            pm[p, p - 8] = 1
    c[:, C_PSW:C_PSW + 128] = pm
    pq = np.zeros((128, 128), np.float32)
    for p in range(64, 96):
        dd = p - 64
        pq[p, p + 16 if dd < 16 else p - 16] = 1
    c[:, C_PSW + 128:C_PSW + 256] = pq
    pk = np.zeros((128, 128), np.float32)
    for p in range(32):
        pk[p, p + 16 if p < 16 else p - 16] = 1
    c[:, C_PSW + 256:C_PSW + 384] = pk
    c[0, C_SEL:C_SEL + 128] = -1
    c[32, C_SEL:C_SEL + 128] = -1
    c[:, C_ONE:C_ONE + 128] = 1
    for n in range(16):
        c[n, C_EN + n * 128:C_EN + (n + 1) * 128] = 1
    return c


def host_ropec():
    r = np.zeros((128, 8), np.float32)
    for p in range(128):
        d = p % 64
        if d < 16:
            i = d % 8
            r[p, 0] = THETA ** (-(2.0 * i) / 16.0) / (2 * np.pi)
            r[p, 1] = -1.0 if d < 8 else 1.0
        if 64 <= p < 96:
            dd = p - 64
            i = dd % 16
            r[p, 2] = THETA ** (-(2.0 * i) / 32.0) / (2 * np.pi)
            r[p, 3] = -1.0 if dd < 16 else 1.0
        if p < 32:
            i = p % 16
            r[p, 4] = THETA ** (-(2.0 * i) / 32.0) / (2 * np.pi)
            r[p, 5] = -1.0 if p < 16 else 1.0
    r[:, 6] = -np.pi
    return r


def host_masks(role, NQS=4):
    m = np.zeros((2, 128, 8, 512), np.float32)
    k = np.arange(128)[:, None]
    q = np.arange(128)[None, :]
    for d in range(8):
        for s in range(NQS):
            qb = 2 * s + role
            if d < qb:
                m[:, :, d, s * 128:(s + 1) * 128] = 1
            elif d == qb:
                m[0, :, d, s * 128:(s + 1) * 128] = (k < q)
                m[1, :, d, s * 128:(s + 1) * 128] = (k <= q)
    return m


def host_pos(role, NT):
    j = np.arange(NT // 128)[:, None]
    ql = np.arange(128)[None, :]
    return ((2 * j + role) * 128 + ql).reshape(1, NT).astype(np.float32)


class MKA(MK):
    HPC = {64: 8, 96: 4}

    def alloc_scratch(self, kds=(64, 96)):
        NT = self.NT
        self.VR = min(1024, NT)
        for KD in kds:
            self.dscr("qT%d" % KD, [16, KD, NT], BF16)
            hpc = self.HPC[KD]
            for c in range(16 // hpc):
                self.dscr("kT%d_own_c%d" % (KD, c), [hpc * KD, NT], BF16)
                self.dscr("kT%d_all_c%d" % (KD, c), [2 * hpc * KD, NT], BF16)
        for c in range(NT // self.VR):
            self.dscr("v_own_c%d" % c, [self.VR, D], BF16)
            self.dscr("v_all_c%d" % c, [2 * self.VR, D], BF16)
        self.dscr("aT", [16, 64, NT], BF16)

    def kown(self, KD, h):
        hpc = self.HPC[KD]
        return "kT%d_own_c%d" % (KD, h // hpc), (h % hpc) * KD

    def kall(self, KD, h, r):
        hpc = self.HPC[KD]
        return "kT%d_all_c%d" % (KD, h // hpc), r * hpc * KD + (h % hpc) * KD

    def vown(self, row):
        return "v_own_c%d" % (row // self.VR), row % self.VR

    def vall(self, r, row):
        return "v_all_c%d" % (row // self.VR), r * self.VR + row % self.VR

    def exchange_all(self, KD):
        for c in range(16 // self.HPC[KD]):
            self.exchange("kT%d_own_c%d" % (KD, c), "kT%d_all_c%d" % (KD, c))
        for c in range(self.NT // self.VR):
            self.exchange("v_own_c%d" % c, "v_all_c%d" % c)

    def setup_attn_consts(self, stack):
        P = self.P
        self.cst, self.b_cst = self.sb(stack, "cst", [128, CW], BF16)
        P.dma("pool", lambda e: e.dma_start(out=self.cst[:], in_=self.dram["consts"].ap()),
              reads=[self.dbuf["consts"]], writes=[self.b_cst])
        self.ropec, self.b_ropec = self.sb(stack, "ropec_sb", [128, 8], F32)
        P.dma("sp", lambda e: e.dma_start(out=self.ropec[:], in_=self.dram["ropec"].ap()),
              reads=[self.dbuf["ropec"]], writes=[self.b_ropec])

    def rope_tables(self, stack, col, nrows):
        P, NT = self.P, self.NT
        posb, b_posb = self.sb(stack, "posb", [128, NT], F32)
        P.dma("sp", lambda e: e.dma_start(out=posb[:], in_=self.dram["pos"].ap().partition_broadcast(128)),
              reads=[self.dbuf["pos"]], writes=[b_posb])
        C, b_C = self.sb(stack, "ropeC", [128, NT], F32)
        S, b_S = self.sb(stack, "ropeS", [128, NT], F32)
        tmp, b_tmp = self.sb(stack, "ropeT", [128, NT], F32)
        rc = self.ropec
        R = slice(0, nrows)
        two_pi = float(2 * np.pi)
        MAGIC = 12582912.0
        rnd, b_rnd = self.sb(stack, "ropeR", [128, NT], F32)
        for (dst, b_dst, shift) in ((S, b_S, 0.0), (C, b_C, 0.25)):
            P.op("dve", lambda e, shift=shift: e.tensor_scalar(out=tmp[R, :], in0=posb[R, :], scalar1=rc[R, col:col + 1],
                                                               scalar2=shift, op0=ALU.mult, op1=ALU.add),
                 reads=[b_posb, self.b_ropec], writes=[b_tmp])
            P.op("dve", lambda e: e.tensor_scalar(out=rnd[R, :], in0=tmp[R, :], scalar1=MAGIC, scalar2=None,
                                                  op0=ALU.add), reads=[b_tmp], writes=[b_rnd])
            P.op("dve", lambda e: e.tensor_scalar(out=rnd[R, :], in0=rnd[R, :], scalar1=-MAGIC, scalar2=None,
                                                  op0=ALU.add), reads=[b_rnd], writes=[b_rnd])
            P.op("dve", lambda e: e.tensor_tensor(out=tmp[R, :], in0=tmp[R, :], in1=rnd[R, :], op=ALU.subtract),
                 reads=[b_tmp, b_rnd], writes=[b_tmp])
            P.op("act", lambda e, dst=dst: e.activation(out=dst[R, :], in_=tmp[R, :], func=AF.Sin, scale=two_pi),
                 reads=[b_tmp], writes=[b_dst])
        P.op("dve", lambda e: e.tensor_scalar(out=S[R, :], in0=S[R, :], scalar1=rc[R, col + 1:col + 2], scalar2=None,
                                              op0=ALU.mult),
             reads=[b_S, self.b_ropec], writes=[b_S])
        return (C, b_C), (S, b_S)

    def apply_rope(self, ps, b_ps, ps2, b_ps2, nrows, pswcol, C, b_C, S, b_S, t0, scale, out, b_out,
                   qb, b_qb, t1, b_t1, t2, b_t2):
        P = self.P
        R = slice(0, nrows)
        P.op("act", lambda e: e.copy(out=qb[R, :], in_=ps[R, :]), reads=[b_ps], writes=[b_qb])
        P.op("pe", lambda e: e.matmul(ps2[R, :], lhsT=self.cst[R, pswcol:pswcol + nrows], rhs=qb[R, :],
                                      start=True, stop=True),
             reads=[self.b_cst, b_qb], writes=[b_ps2])
        P.op("dve", lambda e: e.scalar_tensor_tensor(out=t1[R, :], in0=ps[R, :], scalar=scale,
                                                     in1=C[R, t0:t0 + TT], op0=ALU.mult, op1=ALU.mult),
             reads=[b_ps, b_C], writes=[b_t1])
        P.op("dve", lambda e: e.scalar_tensor_tensor(out=t2[R, :], in0=ps2[R, :], scalar=scale,
                                                     in1=S[R, t0:t0 + TT], op0=ALU.mult, op1=ALU.mult),
             reads=[b_ps2, b_S], writes=[b_t2])
        P.op("pool", lambda e: e.tensor_tensor(out=out[R, :], in0=t1[R, :], in1=t2[R, :], op=ALU.add),
             reads=[b_t1, b_t2], writes=[b_out])

    def pass_proj_qkv(self, kind, wname, hn_name):
        P, NT = self.P, self.NT
        qT = self.dram["qT64"]
        with ExitStack() as stack:
            banks = self.psum_banks(stack)
            w, b_w = self.sb(stack, "wqkv", [128, NCH, 3 * D], BF16)
            wd = self.dram[wname]
            for c in range(NCH):
                P.dma("pool", lambda e, c=c: e.dma_start(out=w[:, c, :], in_=wd.ap()[:, c, :]),
                      reads=[self.dbuf[wname]], writes=[b_w])
            hn = [self.sb(stack, "phn%d" % i, [128, NCH, TT], BF16) for i in range(2)]
            osb = [self.sb(stack, "posb%d" % i, [128, TT], BF16) for i in range(4)]
            vsb = [self.sb(stack, "pvsb%d" % i, [128, D], BF16) for i in range(2)]
            rope = kind == "moba"
            if rope:
                (C, b_C), (S, b_S) = self.rope_tables(stack, 0, 128)
                qb = [self.sb(stack, "rqb%d" % i, [128, TT], BF16) for i in range(2)]
                t1 = [self.sb(stack, "rt1%d" % i, [128, TT], F32) for i in range(2)]
                t2 = [self.sb(stack, "rt2%d" % i, [128, TT], F32) for i in range(2)]
            hnd = self.dram[hn_name]
            oi = 0
            vi = 0
            pi = 0
            for tt in range(NT // TT):
                t0 = tt * TT
                h_t, b_h = hn[tt % 2]
                P.dma("sp", lambda e, t0=t0, h_t=h_t: e.dma_start(
                    out=h_t[:], in_=hnd.ap()[:, :, t0:t0 + TT].rearrange("c p t -> p c t")),
                    reads=[self.dbuf[hn_name]], writes=[b_h])
                for which in range(2):
                    for pr in range(8):
                        ps, b_ps = banks[pi % 2]
                        ps2, b_ps2 = banks[2 + pi % 2]
                        pi += 1
                        col = which * D + pr * 128
                        for c in range(NCH):
                            P.op("pe", lambda e, c=c, col=col, ps=ps, h_t=h_t: e.matmul(
                                ps[:], lhsT=w[:, c, col:col + 128], rhs=h_t[:, c, :],
                                start=(c == 0), stop=(c == NCH - 1)),
                                reads=[b_w, b_h], writes=[b_ps], pe_accum=(c > 0))
                        o_t, b_o = osb[oi % 4]
                        oi += 1
                        scale = 0.125 if which == 0 else 1.0
                        if rope:
                            k2 = pi % 2
                            self.apply_rope(ps, b_ps, ps2, b_ps2, 128, C_PSW, C, b_C, S, b_S, t0, scale, o_t, b_o,
                                            qb[k2][0], qb[k2][1], t1[k2][0], t1[k2][1], t2[k2][0], t2[k2][1])
                        else:
                            P.op("act", lambda e, o_t=o_t, ps=ps, scale=scale: e.activation(
                                out=o_t[:], in_=ps[:], func=AF.Copy, scale=scale),
                                reads=[b_ps], writes=[b_o])
                        if which == 0:
                            dst_ap = qT.ap()[2 * pr:2 * pr + 2, :, t0:t0 + TT].rearrange("h d t -> (h d) t")
                            P.dma("sp", lambda e, o_t=o_t, dst_ap=dst_ap: e.dma_start(out=dst_ap, in_=o_t[:]),
                                  reads=[b_o], writes=[self.dbuf["qT64"]], sb=b_o)
                        else:
                            kn, kr0 = self.kown(64, 2 * pr)
                            dst_ap = self.dram[kn].ap()[kr0:kr0 + 128, t0:t0 + TT]
                            P.dma("sp", lambda e, o_t=o_t, dst_ap=dst_ap: e.dma_start(out=dst_ap, in_=o_t[:]),
                                  reads=[b_o], writes=[self.dbuf[kn]], sb=b_o)
                for tb in range(TT // 128):
                    v_t, b_v = vsb[vi % 2]
                    vi += 1
                    for half in range(2):
                        ps, b_ps = banks[4 + half]
                        for c in range(NCH):
                            P.op("pe", lambda e, c=c, half=half, tb=tb, ps=ps, h_t=h_t: e.matmul(
                                ps[:], lhsT=h_t[:, c, tb * 128:(tb + 1) * 128],
                                rhs=w[:, c, 2 * D + half * 512:2 * D + (half + 1) * 512],
                                start=(c == 0), stop=(c == NCH - 1)),
                                reads=[b_w, b_h], writes=[b_ps], pe_accum=(c > 0))
                        P.op("dve", lambda e, half=half, ps=ps, v_t=v_t: e.tensor_copy(
                            out=v_t[:, half * 512:(half + 1) * 512], in_=ps[:]),
                            reads=[b_ps], writes=[b_v])
                    vn, r0 = self.vown(t0 + tb * 128)
                    P.dma("sp", lambda e, v_t=v_t, r0=r0, vn=vn: e.dma_start(
                        out=self.dram[vn].ap()[r0:r0 + 128, :], in_=v_t[:]),
                        reads=[b_v], writes=[self.dbuf[vn]], sb=b_v)
            P.barrier()
        self.exchange_all(64)
        P.barrier()

    def pass_proj_mla(self, w_in, w_uq, w_uk, w_uv, gname, hn_name):
        P, NT = self.P, self.NT
        qT = self.dram["qT96"]
        b_qT = self.dbuf["qT96"]
        cst, b_cst = self.cst, self.b_cst
        with ExitStack() as stack:
            banks = self.psum_banks(stack)
            win, b_win = self.sb(stack, "win", [128, NCH, 672], BF16)
            wuq, b_wuq = self.sb(stack, "wuq", [128, 3, 1536], BF16)
            wuk, b_wuk = self.sb(stack, "wuk", [128, 2, 1024], BF16)
            wuv, b_wuv = self.sb(stack, "wuv", [128, 2, 1024], BF16)
            for (t, b, nm) in ((win, b_win, w_in), (wuq, b_wuq, w_uq), (wuk, b_wuk, w_uk), (wuv, b_wuv, w_uv)):
                P.dma("pool", lambda e, t=t, nm=nm: e.dma_start(out=t[:], in_=self.dram[nm].ap()),
                      reads=[self.dbuf[nm]], writes=[b])
            mg, b_mg = self.sb(stack, "mlag", [128, 5], F32)
            P.dma("sp", lambda e: e.dma_start(out=mg[:], in_=self.dram[gname].ap()),
                  reads=[self.dbuf[gname]], writes=[b_mg])
            (Cq, b_Cq), (Sq, b_Sq) = self.rope_tables(stack, 2, 96)
            (Ck, b_Ck), (Sk, b_Sk) = self.rope_tables(stack, 4, 32)
            hn = [self.sb(stack, "mhn%d" % i, [128, NCH, TT], BF16) for i in range(2)]
            cqf, b_cqf = self.sb(stack, "cqf", [128, 3, TT], F32)
            ckf, b_ckf = self.sb(stack, "ckf", [128, 2, TT], F32)
            sq, b_sq = self.sb(stack, "msq", [128, 3, TT], BF16)
            rstd, b_rstd = self.sb(stack, "mrstd", [128, TT], F32)
            cqn, b_cqn = self.sb(stack, "cqn", [128, 3, TT], BF16)
            ckn, b_ckn = self.sb(stack, "ckn", [128, 2, TT], BF16)
            osb = [self.sb(stack, "mosb%d" % i, [128, TT], BF16) for i in range(4)]
            vsb = [self.sb(stack, "mvsb%d" % i, [128, D], BF16) for i in range(2)]
            qb = [self.sb(stack, "mqb%d" % i, [128, TT], BF16) for i in range(2)]
            t1 = [self.sb(stack, "mt1%d" % i, [128, TT], F32) for i in range(2)]
            t2 = [self.sb(stack, "mt2%d" % i, [128, TT], F32) for i in range(2)]
            hnd = self.dram[hn_name]
            oi = 0
            vi = 0
            pi = 0
            for tt in range(NT // TT):
                t0 = tt * TT
                h_t, b_h = hn[tt % 2]
                P.dma("sp", lambda e, t0=t0, h_t=h_t: e.dma_start(
                    out=h_t[:], in_=hnd.ap()[:, :, t0:t0 + TT].rearrange("c p t -> p c t")),
                    reads=[self.dbuf[hn_name]], writes=[b_h])
                for m in range(5):
                    ps, b_ps = banks[pi % 2]
                    pi += 1
                    for c in range(NCH):
                        P.op("pe", lambda e, c=c, m=m, ps=ps, h_t=h_t: e.matmul(
                            ps[:], lhsT=win[:, c, m * 128:(m + 1) * 128], rhs=h_t[:, c, :],
                            start=(c == 0), stop=(c == NCH - 1)),
                            reads=[b_win, b_h], writes=[b_ps], pe_accum=(c > 0))
                    if m < 3:
                        P.op("act", lambda e, m=m, ps=ps: e.copy(out=cqf[:, m, :], in_=ps[:]),
                             reads=[b_ps], writes=[b_cqf])
                    else:
                        P.op("act", lambda e, m=m, ps=ps: e.copy(out=ckf[:, m - 3, :], in_=ps[:]),
                             reads=[b_ps], writes=[b_ckf])
                ps, b_ps = banks[pi % 2]
                ps2, b_ps2 = banks[2 + pi % 2]
                pi += 1
                for c in range(NCH):
                    P.op("pe", lambda e, c=c, ps=ps, h_t=h_t: e.matmul(
                        ps[0:32, :], lhsT=win[:, c, 640:672], rhs=h_t[:, c, :],
                        start=(c == 0), stop=(c == NCH - 1)),
                        reads=[b_win, b_h], writes=[b_ps], pe_accum=(c > 0))
                kr_t, b_kr = osb[oi % 4]
                oi += 1
                self.apply_rope(ps, b_ps, ps2, b_ps2, 32, C_PSW + 256, Ck, b_Ck, Sk, b_Sk, t0, 1.0, kr_t, b_kr,
                                qb[0][0], qb[0][1], t1[0][0], t1[0][1], t2[0][0], t2[0][1])
                for h in range(H):
                    kn, kr0 = self.kown(96, h)
                    P.dma("sp", lambda e, kn=kn, kr0=kr0, kr_t=kr_t, t0=t0: e.dma_start(
                        out=self.dram[kn].ap()[kr0 + 64:kr0 + 96, t0:t0 + TT], in_=kr_t[0:32, :]),
                        reads=[b_kr], writes=[self.dbuf[kn]], sb=b_kr)
                for (src, b_src, nch, gcol, dst, b_dst, inv) in ((cqf, b_cqf, 3, 0, cqn, b_cqn, 1.0 / 384.0),
                                                                 (ckf, b_ckf, 2, 3, ckn, b_ckn, 1.0 / 256.0)):
                    psn, b_psn = banks[6]
                    P.op("act", lambda e, src=src, nch=nch: e.activation(out=sq[:, 0:nch, :], in_=src[:, 0:nch, :],
                                                                         func=AF.Square),
                         reads=[b_src], writes=[b_sq])
                    for m in range(nch):
                        P.op("pe", lambda e, m=m, nch=nch, psn=psn: e.matmul(
                            psn[:], lhsT=cst[:, C_ONE:C_ONE + 128], rhs=sq[:, m, :], start=(m == 0), stop=(m == nch - 1)),
                            reads=[b_cst, b_sq], writes=[b_psn], pe_accum=(m > 0))
                    P.op("act", lambda e, psn=psn, inv=inv: e.activation(out=rstd[:], in_=psn[:], func=AF.Sqrt,
                                                                         bias=self.epsc[:, 0:1], scale=inv),
                         reads=[b_psn, self.b_epsc], writes=[b_rstd])
                    P.op("dve", lambda e: e.reciprocal(out=rstd[:], in_=rstd[:]), reads=[b_rstd], writes=[b_rstd])
                    for m in range(nch):
                        P.op("dve", lambda e, m=m, src=src, dst=dst, gcol=gcol: e.scalar_tensor_tensor(
                            out=dst[:, m, :], in0=src[:, m, :], scalar=mg[:, gcol + m:gcol + m + 1], in1=rstd[:],
                            op0=ALU.mult, op1=ALU.mult),
                            reads=[b_src, b_rstd, b_mg], writes=[b_dst])
                for h in range(H):
                    ps, b_ps = banks[pi % 2]
                    ps2, b_ps2 = banks[2 + pi % 2]
                    k2 = pi % 2
                    pi += 1
                    for m in range(3):
                        P.op("pe", lambda e, m=m, h=h, ps=ps: e.matmul(
                            ps[0:96, :], lhsT=wuq[:, m, h * 96:(h + 1) * 96], rhs=cqn[:, m, :],
                            start=(m == 0), stop=(m == 2)),
                            reads=[b_wuq, b_cqn], writes=[b_ps], pe_accum=(m > 0))
                    o_t, b_o = osb[oi % 4]
                    oi += 1
                    self.apply_rope(ps, b_ps, ps2, b_ps2, 96, C_PSW + 128, Cq, b_Cq, Sq, b_Sq, t0, 1.0, o_t, b_o,
                                    qb[k2][0], qb[k2][1], t1[k2][0], t1[k2][1], t2[k2][0], t2[k2][1])
                    P.dma("sp", lambda e, h=h, o_t=o_t, t0=t0: e.dma_start(
                        out=qT.ap()[h, :, t0:t0 + TT], in_=o_t[0:96, :]),
                        reads=[b_o], writes=[b_qT], sb=b_o)
                for pr in range(8):
                    ps, b_ps = banks[pi % 2]
                    pi += 1
                    for m in range(2):
                        P.op("pe", lambda e, m=m, pr=pr, ps=ps: e.matmul(
                            ps[:], lhsT=wuk[:, m, pr * 128:(pr + 1) * 128], rhs=ckn[:, m, :],
                            start=(m == 0), stop=(m == 1)),
                            reads=[b_wuk, b_ckn], writes=[b_ps], pe_accum=(m > 0))
                    o_t, b_o = osb[oi % 4]
                    oi += 1
                    P.op("act", lambda e, o_t=o_t, ps=ps: e.copy(out=o_t[:], in_=ps[:]), reads=[b_ps], writes=[b_o])
                    for two in range(2):
                        kn, kr0 = self.kown(96, 2 * pr + two)
                        P.dma("sp", lambda e, kn=kn, kr0=kr0, two=two, o_t=o_t, t0=t0: e.dma_start(
                            out=self.dram[kn].ap()[kr0:kr0 + 64, t0:t0 + TT], in_=o_t[two * 64:(two + 1) * 64, :]),
                            reads=[b_o], writes=[self.dbuf[kn]], sb=b_o)
                for tb in range(TT // 128):
                    v_t, b_v = vsb[vi % 2]
                    vi += 1
                    for half in range(2):
                        ps, b_ps = banks[4 + half]
                        for m in range(2):
                            P.op("pe", lambda e, m=m, half=half, tb=tb, ps=ps: e.matmul(
                                ps[:], lhsT=ckn[:, m, tb * 128:(tb + 1) * 128],
                                rhs=wuv[:, m, half * 512:(half + 1) * 512], start=(m == 0), stop=(m == 1)),
                                reads=[b_wuv, b_ckn], writes=[b_ps], pe_accum=(m > 0))
                        P.op("dve", lambda e, half=half, ps=ps, v_t=v_t: e.tensor_copy(
                            out=v_t[:, half * 512:(half + 1) * 512], in_=ps[:]),
                            reads=[b_ps], writes=[b_v])
                    vn, r0 = self.vown(t0 + tb * 128)
                    P.dma("sp", lambda e, v_t=v_t, r0=r0, vn=vn: e.dma_start(
                        out=self.dram[vn].ap()[r0:r0 + 128, :], in_=v_t[:]),
                        reads=[b_v], writes=[self.dbuf[vn]], sb=b_v)
            P.barrier()
        self.exchange_all(96)
        P.barrier()

    def exchange(self, src, dst):
        P = self.P
        groups = [[0, 1], [2, 3], [4, 5], [6, 7]]
        s_t, d_t = self.dram[src], self.dram[dst]
        P.dma("pool", lambda e: e.collective_compute("AllGather", ALU.bypass, replica_groups=groups,
                                                     ins=[s_t.ap()], outs=[d_t.ap()]),
              reads=[self.dbuf[src]], writes=[self.dbuf[dst]], sb=self.dbuf[dst], inc=1, cls="cc")

    def pass_o(self, wname, h_src, h_dst):
        P, NT = self.P, self.NT
        aT = self.dram["aT"]
        hs, hd = self.dram[h_src], self.dram[h_dst]
        with ExitStack() as stack:
            banks = self.psum_banks(stack)
            w, b_w = self.sb(stack, "wo", [128, NCH, D], BF16)
            P.dma("pool", lambda e: e.dma_start(out=w[:], in_=self.dram[wname].ap()),
                  reads=[self.dbuf[wname]], writes=[b_w])
            at = [self.sb(stack, "oat%d" % i, [128, NCH, TT], BF16) for i in range(2)]
            ht = [self.sb(stack, "oht%d" % i, [128, NCH, TT], F32) for i in range(2)]
            pi = 0
            for tt in range(NT // TT):
                t0 = tt * TT
                a_t, b_a = at[tt % 2]
                h_t, b_h = ht[tt % 2]
                for two in range(2):
                    P.dma("sp", lambda e, t0=t0, a_t=a_t, two=two: e.dma_start(
                        out=a_t[two * 64:(two + 1) * 64, :, :],
                        in_=aT.ap()[:, :, t0:t0 + TT].rearrange("(c two) d t -> two d c t", two=2)[two]),
                        reads=[self.dbuf["aT"]], writes=[b_a])
                P.dma("act", lambda e, t0=t0, h_t=h_t: e.dma_start(
                    out=h_t[:], in_=hs.ap()[:, :, t0:t0 + TT].rearrange("c p t -> p c t")),
                    reads=[self.dbuf[h_src]], writes=[b_h])
                for dc in range(NCH):
                    ps, b_ps = banks[pi % 4]
                    pi += 1
                    for c in range(NCH):
                        P.op("pe", lambda e, c=c, dc=dc, ps=ps, a_t=a_t: e.matmul(
                            ps[:], lhsT=w[:, c, dc * 128:(dc + 1) * 128], rhs=a_t[:, c, :],
                            start=(c == 0), stop=(c == NCH - 1)),
                            reads=[b_w, b_a], writes=[b_ps], pe_accum=(c > 0))
                    P.op("dve", lambda e, dc=dc, ps=ps, h_t=h_t: e.tensor_tensor(
                        out=h_t[:, dc, :], in0=ps[:], in1=h_t[:, dc, :], op=ALU.add),
                        reads=[b_ps, b_h], writes=[b_h])
                P.dma("sp", lambda e, t0=t0, h_t=h_t: e.dma_start(
                    out=hd.ap()[:, :, t0:t0 + TT].rearrange("c p t -> p c t"), in_=h_t[:]),
                    reads=[b_h], writes=[self.dbuf[h_dst]], sb=b_h)
            P.barrier()

    def pass_attn(self, kind):
        P, NT = self.P, self.NT
        KD = 96 if kind == "mla" else 64
        NJ = NT // 128
        NQT = NT // TT
        qT, aT = self.dram["qT%d" % KD], self.dram["aT"]
        b_qT, b_aT = self.dbuf["qT%d" % KD], self.dbuf["aT"]
        cst, b_cst = self.cst, self.b_cst
        sm_scale = float(96 ** -0.5) if kind == "mla" else 1.0
        with ExitStack() as stack:
            banks = self.psum_banks(stack)
            vsb, b_vsb = self.sb(stack, "avsb", [128, 2 * NJ, D], BF16)
            for r in range(2):
                for j0 in range(0, NJ, 4):
                    vn, row0 = self.vall(r, j0 * 128)
                    P.dma("sp" if r == 0 else "pool", lambda e, row0=row0, r=r, j0=j0, vn=vn: e.dma_start(
                        out=vsb[:, r * NJ + j0:r * NJ + j0 + 4, :],
                        in_=self.dram[vn].ap()[row0:row0 + 512, :].rearrange("(j p) f -> p j f", p=128)),
                        reads=[self.dbuf[vn]], writes=[b_vsb])
            mask, b_mask = self.sb(stack, "amask", [128, 8, TT], BF16)
            mk_idx = 0 if kind == "sb" else 1
            P.dma("pool", lambda e: e.dma_start(out=mask[:], in_=self.dram["masks"].ap()[mk_idx]),
                  reads=[self.dbuf["masks"]], writes=[b_mask])
            P.op("dve", lambda e: e.tensor_scalar(out=mask[:], in0=mask[:], scalar1=-1.0, scalar2=BIG,
                                                  op0=ALU.add, op1=ALU.mult), reads=[b_mask], writes=[b_mask])
            ksb = [self.sb(stack, "aksb%d" % i, [KD, 2, NT], BF16) for i in range(2)]
            qsb = [self.sb(stack, "aqsb%d" % i, [KD, NT], BF16) for i in range(2)]
            e_t = [self.sb(stack, "ae%d" % i, [128, TT], F32) for i in range(2)]
            L_t = [self.sb(stack, "aL%d" % i, [128, TT], BF16) for i in range(2)]
            w_t = [self.sb(stack, "aw%d" % i, [128, TT], BF16) for i in range(3)]
            carry = [self.sb(stack, "acar%d" % i, [64, TT], BF16) for i in range(2)]
            cf, b_cf = self.sb(stack, "acf", [64, TT], F32)
            o_f, b_of = self.sb(stack, "aof", [64, TT], F32)
            rden, b_rden = self.sb(stack, "arden", [64, TT], F32)
            o_sb = [self.sb(stack, "aosb%d" % i, [64, TT], BF16) for i in range(2)]
            if kind == "moba":
                negsel = [self.sb(stack, "ansel%d" % i, [16, NT], BF16) for i in range(2)]
                km, b_km = self.sb(stack, "akm", [64, 2, NJ], F32)
                kms, b_kms = self.sb(stack, "akms", [64, NJ], F32)
                khi, b_khi = self.sb(stack, "akhi", [64, NJ], BF16)
                klo, b_klo = self.sb(stack, "aklo", [64, NJ], BF16)
                gm = [self.sb(stack, "agm%d" % i, [128, 16], F32) for i in range(2)]
                m8 = [self.sb(stack, "am8%d" % i, [128, 8], F32) for i in range(2)]
                sel = [self.sb(stack, "asel%d" % i, [128, NJ], F32) for i in range(2)]
                nsb = [self.sb(stack, "ansb%d" % i, [128, NJ], BF16) for i in range(2)]
                pa, b_pa = self.sb(stack, "apa", [128, NJ, NJ], F32)
                p01, b_p01 = self.sb(stack, "ap01", [128, NJ, NJ], F32)
                o01, b_o01 = self.sb(stack, "ao01", [128, NJ, NJ], F32)
                pat = [[1, NJ], [-1, NJ]]
                P.op("pool", lambda e: e.memset(pa[:], 0.0), writes=[b_pa])
                P.op("pool", lambda e: e.affine_select(out=pa[:], in_=pa[:], pattern=pat, compare_op=ALU.is_ge,
                                                       fill=NEG, base=-1, channel_multiplier=0),
                     reads=[b_pa], writes=[b_pa])
                P.op("pool", lambda e: e.memset(p01[:], 1.0), writes=[b_p01])
                P.op("pool", lambda e: e.affine_select(out=p01[:], in_=p01[:], pattern=pat, compare_op=ALU.is_ge,
                                                       fill=0.0, base=-1, channel_multiplier=0),
                     reads=[b_p01], writes=[b_p01])
                P.op("pool", lambda e: e.memset(o01[:], 1.0), writes=[b_o01])
                P.op("pool", lambda e: e.affine_select(out=o01[:], in_=o01[:], pattern=pat, compare_op=ALU.is_equal,
                                                       fill=0.0, base=0, channel_multiplier=0),
                     reads=[b_o01], writes=[b_o01])
                for g_t, b_g in gm:
                    P.op("pool", lambda e, g_t=g_t: e.memset(g_t[:], NEG), writes=[b_g])
            def load_head(h):
                k_t, b_k = ksb[h % 2]
                q_t, b_q = qsb[h % 2]
                for r in range(2):
                    kn, row0 = self.kall(KD, h, r)
                    P.dma("sp" if r == 0 else "pool", lambda e, row0=row0, r=r, k_t=k_t, kn=kn: e.dma_start(
                        out=k_t[:, r, :], in_=self.dram[kn].ap()[row0:row0 + KD, :]),
                        reads=[self.dbuf[kn]], writes=[b_k])
                P.dma("sp", lambda e, h=h, q_t=q_t: e.dma_start(out=q_t[:], in_=qT.ap()[h]),
                      reads=[b_qT], writes=[b_q])

            def gate_head(h):
                k_t, b_k = ksb[h % 2]
                q_t, b_q = qsb[h % 2]
                ns_t, b_ns = negsel[h % 2]
                P.op("dve", lambda e, k_t=k_t: e.tensor_reduce(
                    out=km[:], in_=k_t[0:64, :, :].rearrange("d r (j k) -> d r j k", k=128),
                    axis=AX.X, op=ALU.add), reads=[b_k], writes=[b_km])
                P.op("dve", lambda e: e.tensor_tensor(out=kms[:], in0=km[:, 0, :], in1=km[:, 1, :], op=ALU.add),
                     reads=[b_km], writes=[b_kms])
                P.op("dve", lambda e: e.tensor_scalar(out=kms[:], in0=kms[:], scalar1=1.0 / 256.0, scalar2=None,
                                                      op0=ALU.mult), reads=[b_kms], writes=[b_kms])
                P.op("dve", lambda e: e.tensor_copy(out=khi[:], in_=kms[:]), reads=[b_kms], writes=[b_khi])
                P.op("dve", lambda e: e.tensor_tensor(out=klo[:], in0=kms[:], in1=khi[:], op=ALU.subtract),
                     reads=[b_kms, b_khi], writes=[b_klo])
                for js in range(NJ):
                    psG, b_psG = banks[2]
                    psT, b_psT = banks[3]
                    g_t, b_g = gm[js % 2]
                    m_t, b_m = m8[js % 2]
                    s_t, b_s = sel[js % 2]
                    n_t, b_n = nsb[js % 2]
                    P.op("pe", lambda e, js=js, q_t=q_t, psG=psG: e.matmul(
                        psG[:, 0:NJ], lhsT=q_t[0:64, js * 128:(js + 1) * 128], rhs=khi[:], start=True, stop=False),
                        reads=[b_q, b_khi], writes=[b_psG])
                    P.op("pe", lambda e, js=js, q_t=q_t, psG=psG: e.matmul(
                        psG[:, 0:NJ], lhsT=q_t[0:64, js * 128:(js + 1) * 128], rhs=klo[:], start=False, stop=True),
                        reads=[b_q, b_klo], writes=[b_psG], pe_accum=True)
                    P.op("dve", lambda e, js=js, g_t=g_t, psG=psG: e.tensor_tensor(
                        out=g_t[:, 0:NJ], in0=psG[:, 0:NJ], in1=pa[:, js, :], op=ALU.add),
                        reads=[b_psG, b_pa], writes=[b_g])
                    P.op("dve", lambda e, g_t=g_t, m_t=m_t: e.max(out=m_t[:], in_=g_t[:]),
                         reads=[b_g], writes=[b_m])
                    P.op("dve", lambda e, g_t=g_t, m_t=m_t, s_t=s_t: e.tensor_scalar(
                        out=s_t[:], in0=g_t[:, 0:NJ], scalar1=m_t[:, 2:3], scalar2=None, op0=ALU.is_ge),
                        reads=[b_g, b_m], writes=[b_s])
                    P.op("dve", lambda e, js=js, s_t=s_t: e.tensor_tensor(
                        out=s_t[:], in0=s_t[:], in1=p01[:, js, :], op=ALU.mult),
                        reads=[b_s, b_p01], writes=[b_s])
                    P.op("dve", lambda e, js=js, s_t=s_t: e.tensor_tensor(
                        out=s_t[:], in0=s_t[:], in1=o01[:, js, :], op=ALU.add),
                        reads=[b_s, b_o01], writes=[b_s])
                    P.op("dve", lambda e, s_t=s_t, n_t=n_t: e.tensor_scalar(
                        out=n_t[:], in0=s_t[:], scalar1=-1.0, scalar2=BIG, op0=ALU.add, op1=ALU.mult),
                        reads=[b_s], writes=[b_n])
                    P.op("pe", lambda e, n_t=n_t, psT=psT: e.matmul(
                        psT[0:NJ, 0:128], lhsT=n_t[:], rhs=cst[:, C_ID:C_ID + 128], start=True, stop=True),
                        reads=[b_n, b_cst], writes=[b_psT])
                    P.op("dve", lambda e, js=js, ns_t=ns_t, psT=psT: e.tensor_copy(
                        out=ns_t[0:NJ, js * 128:(js + 1) * 128], in_=psT[0:NJ, 0:128]),
                        reads=[b_psT], writes=[b_ns])

            tiles = []
            chain = 0
            for h in range(H):
                for qt in range(NQT):
                    nkb = 8 * qt + 8
                    for i in range(nkb):
                        g = (nkb - 1 - i) if kind == "sb" else i
                        tiles.append(dict(h=h, qt=qt, i=i, nkb=nkb, g=g, r=g % 2, j=g // 2, d=g - 8 * qt,
                                          chain=chain, t=len(tiles)))
                    chain += 1
            NTL = len(tiles)
            Lr = [self.sb(stack, "aLr%d" % i, [128, TT], BF16) for i in range(5)]
            wr = [self.sb(stack, "awr%d" % i, [128, TT], BF16) for i in range(4)]
            car = [self.sb(stack, "acr%d" % i, [64, TT], BF16) for i in range(4)]

            def views(T):
                h = T["h"]
                k_t, b_k = ksb[h % 2]
                q_t, b_q = qsb[h % 2]
                kblk = k_t[0:KD, T["r"], T["j"] * 128:(T["j"] + 1) * 128]
                vblk = vsb[:, T["r"] * NJ + T["j"], h * 64:(h + 1) * 64]
                qtile = q_t[0:KD, T["qt"] * TT:(T["qt"] + 1) * TT]
                return kblk, vblk, qtile, b_k, b_q

            def epilogue(T):
                h, qt, ch = T["h"], T["qt"], T["chain"]
                oo, b_oo = o_sb[ch % 2]
                if kind == "sb":
                    psO, b_psO = banks[6 + ch % 2]
                    P.op("dve", lambda e, psO=psO, oo=oo: e.tensor_copy(out=oo[:], in_=psO[0:64, :]),
                         reads=[b_psO], writes=[b_oo])
                else:
                    psO, b_psO = banks[4 + ch % 2]
                    psD, b_psD = banks[6 + ch % 2]
                    P.op("dve", lambda e, psD=psD: e.reciprocal(out=rden[:], in_=psD[0:64, :]),
                         reads=[b_psD], writes=[b_rden])
                    P.op("dve", lambda e, psO=psO, oo=oo: e.tensor_tensor(
                        out=oo[:], in0=psO[0:64, :], in1=rden[:], op=ALU.mult),
                        reads=[b_psO, b_rden], writes=[b_oo])
                P.dma("sp", lambda e, h=h, qt=qt, oo=oo: e.dma_start(
                    out=aT.ap()[h, :, qt * TT:(qt + 1) * TT], in_=oo[:]),
                    reads=[b_oo], writes=[b_aT], sb=b_oo)

            def head_prologue(T):
                h = T["h"]
                if T["qt"] == 0 and T["i"] == 0 and h == 0:
                    load_head(0)
                    if kind == "moba":
                        gate_head(0)
                if T["qt"] == 0 and T["i"] == 4 and h + 1 < H:
                    load_head(h + 1)
                if kind == "moba" and T["qt"] == min(2, NQT - 1) and T["i"] == 0 and h + 1 < H and NQT > 1:
                    gate_head(h + 1)
                if kind == "moba" and NQT == 1 and T["i"] == T["nkb"] - 1 and h + 1 < H:
                    gate_head(h + 1)

            if kind == "sb":
                def s1a(T):
                    head_prologue(T)
                    t = T["t"]
                    kblk, vblk, qtile, b_k, b_q = views(T)
                    psZ, b_psZ = banks[t % 2]
                    ee, b_ee = e_t[t % 2]
                    LL, b_LL = Lr[t % 5]
                    dg = T["d"] >= 0
                    P.op("pe", lambda e: e.matmul(psZ[:], lhsT=kblk, rhs=qtile, start=True, stop=(not dg)),
                         reads=[b_k, b_q], writes=[b_psZ])
                    if dg:
                        d = T["d"]
                        P.op("pe", lambda e: e.matmul(psZ[:], lhsT=cst[:, C_ID:C_ID + 128], rhs=mask[:, d, :],
                                                      start=False, stop=True),
                             reads=[b_cst, b_mask], writes=[b_psZ], pe_accum=True)
                    P.op("act", lambda e: e.activation(out=ee[:], in_=psZ[:], func=AF.Exp),
                         reads=[b_psZ], writes=[b_ee])
                    P.op("act", lambda e: e.activation(out=LL[:], in_=ee[:], func=AF.Ln, bias=1.0),
                         reads=[b_ee], writes=[b_LL])

                def s1b(T):
                    t, i = T["t"], T["i"]
                    if i == T["nkb"] - 1:
                        return
                    LL, b_LL = Lr[t % 5]
                    psC, b_psC = banks[4 + t % 2]
                    cc, b_cc = car[t % 4]
                    P.op("pe", lambda e: e.matmul(psC[0:64, :], lhsT=cst[:, C_ONE:C_ONE + 64], rhs=LL[:],
                                                  start=True, stop=True),
                         reads=[b_cst, b_LL], writes=[b_psC])
                    if i == 0:
                        P.op("dve", lambda e: e.tensor_copy(out=cf[:], in_=psC[0:64, :]),
                             reads=[b_psC], writes=[b_cf])
                    else:
                        P.op("dve", lambda e: e.tensor_tensor(out=cf[:], in0=psC[0:64, :], in1=cf[:], op=ALU.add),
                             reads=[b_psC, b_cf], writes=[b_cf])
                    P.op("pool", lambda e: e.tensor_copy(out=cc[:], in_=cf[:]), reads=[b_cf], writes=[b_cc])
                    P.op("dve", lambda e: e.tensor_tensor(out=cc[32:64, :], in0=cf[32:64, :], in1=cc[32:64, :],
                                                          op=ALU.subtract),
                         reads=[b_cf, b_cc], writes=[b_cc])

                def s2(T):
                    t, i = T["t"], T["i"]
                    kblk, vblk, qtile, b_k, b_q = views(T)
                    LL, b_LL = Lr[t % 5]
                    psA, b_psA = banks[2 + t % 2]
                    ww, b_ww = wr[t % 4]
                    P.op("pe", lambda e: e.matmul(psA[:], lhsT=kblk, rhs=qtile, start=True, stop=False),
                         reads=[b_k, b_q], writes=[b_psA])
                    P.op("pe", lambda e: e.matmul(psA[:], lhsT=cst[:, C_NTRI:C_NTRI + 128], rhs=LL[:],
                                                  start=False, stop=(i == 0 and T["d"] < 0)),
                         reads=[b_cst, b_LL], writes=[b_psA], pe_accum=True)
                    dg = T["d"] >= 0
                    if i > 0:
                        cc, b_cc = car[(t - 1) % 4]
                        P.op("pe", lambda e: e.matmul(psA[:], lhsT=cst[0:64, C_SEL:C_SEL + 128], rhs=cc[:],
                                                      start=False, stop=(not dg)),
                             reads=[b_cst, b_cc], writes=[b_psA], pe_accum=True)
                    if dg:
                        d = T["d"]
                        P.op("pe", lambda e: e.matmul(psA[:], lhsT=cst[:, C_ID:C_ID + 128], rhs=mask[:, d, :],
                                                      start=False, stop=True),
                             reads=[b_cst, b_mask], writes=[b_psA], pe_accum=True)
                    P.op("act", lambda e: e.activation(out=ww[:], in_=psA[:], func=AF.Exp),
                         reads=[b_psA], writes=[b_ww])

                def s3(T):
                    t, i, nkb, ch = T["t"], T["i"], T["nkb"], T["chain"]
                    kblk, vblk, qtile, b_k, b_q = views(T)
                    ww, b_ww = wr[t % 4]
                    psO, b_psO = banks[6 + ch % 2]
                    P.op("pe", lambda e: e.matmul(psO[0:64, :], lhsT=vblk, rhs=ww[:], start=(i == 0), stop=(i == nkb - 1)),
                         reads=[b_vsb, b_ww], writes=[b_psO], pe_accum=(i > 0))
                    if i == nkb - 1:
                        epilogue(T)
                stages = [(s1a, 0), (s1b, 1), (s2, 2), (s3, 3)]
            else:
                def s1(T):
                    head_prologue(T)
                    t = T["t"]
                    kblk, vblk, qtile, b_k, b_q = views(T)
                    psZ, b_psZ = banks[t % 2]
                    ww, b_ww = wr[t % 4]
                    dg = T["d"] >= 0
                    P.op("pe", lambda e: e.matmul(psZ[:], lhsT=kblk, rhs=qtile, start=True,
                                                  stop=(kind != "moba" and not dg)),
                         reads=[b_k, b_q], writes=[b_psZ])
                    if kind == "moba":
                        n = T["g"] // 2
                        qt = T["qt"]
                        ns_t, b_ns = negsel[T["h"] % 2]
                        P.op("pe", lambda e: e.matmul(
                            psZ[:], lhsT=cst[0:NJ, C_EN + n * 128:C_EN + (n + 1) * 128],
                            rhs=ns_t[0:NJ, qt * TT:(qt + 1) * TT], start=False, stop=(not dg)),
                            reads=[b_cst, b_ns], writes=[b_psZ], pe_accum=True)
                    if dg:
                        d = T["d"]
                        P.op("pe", lambda e: e.matmul(psZ[:], lhsT=cst[:, C_ID:C_ID + 128], rhs=mask[:, d, :],
                                                      start=False, stop=True),
                             reads=[b_cst, b_mask], writes=[b_psZ], pe_accum=True)
                    P.op("act", lambda e: e.activation(out=ww[:], in_=psZ[:], func=AF.Exp, scale=sm_scale),
                         reads=[b_psZ], writes=[b_ww])

                def s2(T):
                    t, i, nkb, ch = T["t"], T["i"], T["nkb"], T["chain"]
                    kblk, vblk, qtile, b_k, b_q = views(T)
                    ww, b_ww = wr[t % 4]
                    psO, b_psO = banks[4 + ch % 2]
                    psD, b_psD = banks[6 + ch % 2]
                    P.op("pe", lambda e: e.matmul(psO[0:64, :], lhsT=vblk, rhs=ww[:], start=(i == 0), stop=(i == nkb - 1)),
                         reads=[b_vsb, b_ww], writes=[b_psO], pe_accum=(i > 0))
                    P.op("pe", lambda e: e.matmul(psD[0:64, :], lhsT=cst[:, C_ONE:C_ONE + 64], rhs=ww[:],
                                                  start=(i == 0), stop=(i == nkb - 1)),
                         reads=[b_cst, b_ww], writes=[b_psD], pe_accum=(i > 0))
                    if i == nkb - 1:
                        epilogue(T)
                stages = [(s1, 0), (s2, 2)]
            maxlag = max(l for _, l in stages)
            for step in range(NTL + maxlag):
                for fn, lag in stages:
                    t = step - lag
                    if 0 <= t < NTL:
                        fn(tiles[t])
            P.barrier()


def lay_rows(w):
    k = w.shape[0] // 128
    return np.ascontiguousarray(w.reshape(k, 128, w.shape[1]).transpose(1, 0, 2))


def own_tokens(xb, role):
    S = xb.shape[0]
    return np.ascontiguousarray(xb.reshape(S // 256, 2, 128, -1)[:, role].reshape(S // 2, -1))


def merge_tokens(y0, y1):
    n = y0.shape[0] // 128
    out = np.stack([y0.reshape(n, 128, -1), y1.reshape(n, 128, -1)], axis=1)
    return out.reshape(2 * y0.shape[0], -1)


NT_CORE = 2048
KINDS = ["sb", "moba", "mla", "sb"]


def build_full(NT=NT_CORE, depth=DEPTH):
    m = MKA(NT)
    m.din("xT", [NCH, 128, NT])
    m.din("gains", [128, 3 * depth + 1, NCH])
    m.din("consts", [128, CW]); m.din("ropec", [128, 8]); m.din("masks", [2, 128, 8, 512]); m.din("pos", [1, NT])
    for i in range(depth):
        for f in (1, 2):
            m.din("wgu%d_%d" % (f, i), [NFG, 128, 2, NCH, FG * 128])
            m.din("wd%d_%d" % (f, i), [NCH, 128, NFC, 128])
        kind = KINDS[i]
        if kind == "mla":
            m.din("w_in_%d" % i, [128, NCH, 672]); m.din("w_uq_%d" % i, [128, 3, 1536])
            m.din("w_uk_%d" % i, [128, 2, 1024]); m.din("w_uv_%d" % i, [128, 2, 1024]); m.din("mlag_%d" % i, [128, 5])
        else:
            m.din("wqkv_%d" % i, [128, NCH, 3 * D])
        m.din("wo_%d" % i, [128, NCH, D])
    m.dout("yT", [NCH, 128, NT])
    m.dscr("hA", [NCH, 128, NT], F32); m.dscr("hB", [NCH, 128, NT], F32); m.dscr("hnmix", [NCH, 128, NT], BF16)
    m.alloc_scratch()
    with ExitStack() as cst:
        m.setup_consts(cst, m.dram["gains"], 3 * depth + 1)
        m.setup_attn_consts(cst)
        cur = "xT"
        for i in range(depth):
            kind = KINDS[i]
            m.pass_ffn(cur, "hA", "wgu1_%d" % i, "wd1_%d" % i, 3 * i, epi=("norm", 3 * i + 1, "hnmix"))
            if kind == "mla":
                m.pass_proj_mla("w_in_%d" % i, "w_uq_%d" % i, "w_uk_%d" % i, "w_uv_%d" % i, "mlag_%d" % i, "hnmix")
            else:
                m.pass_proj_qkv(kind, "wqkv_%d" % i, "hnmix")
            m.pass_attn(kind)
            m.pass_o("wo_%d" % i, "hA", "hB")
            last = i == depth - 1
            m.pass_ffn("hB", "hA", "wgu2_%d" % i, "wd2_%d" % i, 3 * i + 2,
                       epi=(("final", 3 * depth, "yT") if last else None))
            cur = "hA"
        nc = m.finish(["yT"])
    return m, nc


def host_inputs(inp, NT=NT_CORE, depth=DEPTH):
    f32 = lambda a: np.asarray(a, dtype=np.float32)
    shared = {}
    gs = []
    for i in range(depth):
        gs += [f32(inp["norm_ffn1"])[i], f32(inp["norm_mix"])[i], f32(inp["norm_ffn2"])[i]]
    gs.append(f32(inp["final_norm"]))
    shared["gains"] = lay_gain(gs)
    shared["consts"] = host_consts()
    shared["ropec"] = host_ropec()
    cnt = {"sb": 0, "moba": 0, "mla": 0}
    for i in range(depth):
        shared["wgu1_%d" % i] = lay_wgu(f32(inp["ffn1_w_gate_up"])[i])
        shared["wd1_%d" % i] = lay_wd(f32(inp["ffn1_w_down"])[i])
        shared["wgu2_%d" % i] = lay_wgu(f32(inp["ffn2_w_gate_up"])[i])
        shared["wd2_%d" % i] = lay_wd(f32(inp["ffn2_w_down"])[i])
        kind = KINDS[i]
        j = cnt[kind]
        cnt[kind] += 1
        if kind == "mla":
            shared["w_in_%d" % i] = lay_rows(f32(inp["mla_w_in"])[j])
            shared["w_uq_%d" % i] = lay_rows(f32(inp["mla_w_uq"])[j])
            wukv = f32(inp["mla_w_ukv"])[j].reshape(256, 16, 2, 64)
            shared["w_uk_%d" % i] = lay_rows(np.ascontiguousarray(wukv[:, :, 0].reshape(256, 1024)))
            shared["w_uv_%d" % i] = lay_rows(np.ascontiguousarray(wukv[:, :, 1].reshape(256, 1024)))
            shared["mlag_%d" % i] = np.ascontiguousarray(np.concatenate(
                [f32(inp["mla_q_norm"])[j].reshape(3, 128), f32(inp["mla_kv_norm"])[j].reshape(2, 128)], 0).T)
            shared["wo_%d" % i] = lay_rows(f32(inp["mla_w_o"])[j])
        elif kind == "moba":
            shared["wqkv_%d" % i] = lay_rows(f32(inp["moba_w_qkv"])[j])
            shared["wo_%d" % i] = lay_rows(f32(inp["moba_w_o"])[j])
        else:
            shared["wqkv_%d" % i] = lay_rows(f32(inp["sb_w_qkv"])[j])
            shared["wo_%d" % i] = lay_rows(f32(inp["sb_w_o"])[j])
    x = f32(inp["x"])
    in_maps = []
    for c in range(8):
        b, r = c // 2, c % 2
        d = dict(shared)
        d["xT"] = lay_xT(own_tokens(x[b], r))
        d["masks"] = host_masks(r)
        d["pos"] = host_pos(r, NT)
        in_maps.append(d)
    return in_maps


_CACHE = {}


def kernel(**inputs):
    if "nc" not in _CACHE:
        _CACHE["nc"] = build_full()[1]
    nc = _CACHE["nc"]
    in_maps = host_inputs(inputs)
    res = run_bass_kernel_spmd(nc, in_maps, core_ids=list(range(8)))
    B = 4
    out = np.empty((B, 2 * NT_CORE, D), np.float32)
    for b in range(B):
        out[b] = merge_tokens(unlay_xT(res.results[2 * b]["yT"]), unlay_xT(res.results[2 * b + 1]["yT"]))
    return out
```

```python
import numpy as np
import concourse.bass as bass
import concourse.mybir as mybir

F32 = mybir.dt.float32
BF16 = mybir.dt.bfloat16
AF = mybir.ActivationFunctionType
ALU = mybir.AluOpType
AX = mybir.AxisListType

ENGS = ("pe", "act", "dve", "pool", "sp")


class Buf:
    __slots__ = ("name", "writers", "readers", "sem", "semval", "excl", "sw", "preaders")

    def __init__(self, name, excl=False):
        self.name = name
        self.excl = excl
        self.writers = {}
        self.readers = {}
        self.preaders = {}
        self.sem = None
        self.semval = 0
        self.sw = None


class Prog:
    def __init__(self, nc):
        self.nc = nc
        self.streams = {e: [] for e in ENGS}
        self.cnt = {e: 0 for e in ENGS}
        self.sem = {}
        self.seen = {e: {} for e in ENGS}
        self.snaps = {}
        self._ctx = []
        for e in ENGS:
            cm = nc.semaphore("s_" + e)
            self.sem[e] = cm.__enter__()
            self._ctx.append(cm)
        self.nwaits = 0
        self.nops = 0
        self.dmabufs = []
        self.free = {"hw": [], "sw": [], "cc": []}
        self._issw = {}

    def new_sem(self, name):
        self._semuid = getattr(self, "_semuid", 0) + 1
        cm = self.nc.semaphore("%s_%d" % (name, self._semuid))
        s = cm.__enter__()
        self._ctx.append(cm)
        return s

    def _need(self, eng, tok, waits):
        sem, val, _ = tok
        if self.seen[eng].get(sem.num, 0) >= val:
            return
        waits[sem.num] = (sem, max(val, waits.get(sem.num, (sem, 0))[1]))

    def _deps(self, eng, reads, writes, pe_accum=False, nowaw=False):
        waits = {}
        for b in reads:
            for k, tok in b.writers.items():
                self._need(eng, tok, waits)
            if b.excl:
                for k, tok in b.readers.items():
                    if k != eng:
                        self._need(eng, tok, waits)
        for b in writes:
            for k, tok in b.readers.items():
                self._need(eng, tok, waits)
            for k, tok in b.preaders.items():
                self._need(eng, tok, waits)
            for k, tok in b.writers.items():
                if pe_accum and k == "pe":
                    continue
                if nowaw and isinstance(k, tuple):
                    continue
                self._need(eng, tok, waits)
        sn = self.seen[eng]
        for num, (sem, val) in waits.items():
            sn[num] = max(sn.get(num, 0), val)
            snap = self.snaps.get((num, val))
            if snap:
                for k2, v2 in snap.items():
                    if sn.get(k2, 0) < v2:
                        sn[k2] = v2
        self.nwaits += len(waits)
        return list(waits.values())

    def _post(self, key, tok, reads, writes):
        for b in reads:
            old = b.readers.get(key)
            if old is None or old[1] < tok[1] or old[0].num != tok[0].num:
                b.readers[key] = tok
        for b in writes:
            if b.readers:
                b.preaders = b.readers
                b.readers = {}
                b.writers = {key: tok}
            else:
                b.writers[key] = tok

    def op(self, eng, fn, reads=(), writes=(), pe_accum=False):
        waits = self._deps(eng, reads, writes, pe_accum=pe_accum)
        self.cnt[eng] += 1
        n = self.cnt[eng]
        sem = self.sem[eng]
        self.snaps[(sem.num, n)] = dict(self.seen[eng])
        tok = (sem, n, None)
        self._post(eng, tok, reads, writes)
        self.streams[eng].append((waits, fn, (sem, 1)))
        self.nops += 1

    def dma(self, eng, fn, reads=(), writes=(), sb=None, inc=16, cls=None):
        dst = sb if sb is not None else writes[0]
        if cls is None:
            cls = "sw" if eng == "pool" else "hw"
        if cls == "sw":
            if dst.sw is None:
                dst.sw = Buf(dst.name + "_sw")
            dst = dst.sw
        if dst.sem is None:
            pool = self.free[cls]
            if pool:
                dst.sem, dst.semval = pool.pop()
            else:
                dst.sem = self.new_sem("d_" + dst.name)
                dst.semval = 0
            self.dmabufs.append(dst)
            self._issw[id(dst)] = cls
        waits = self._deps(eng, reads, writes, nowaw=True)
        dst.semval += inc
        tok = (dst.sem, dst.semval, None)
        self.snaps[(dst.sem.num, dst.semval)] = dict(self.seen[eng])
        self._post(("dma", dst.sem.num), tok, reads, writes)
        self.streams[eng].append((waits, fn, (dst.sem, inc)))
        self.nops += 1

    def barrier(self):
        for eng in ENGS:
            waits = {}
            for e2 in ENGS:
                if e2 != eng and self.cnt[e2] > 0:
                    self._need(eng, (self.sem[e2], self.cnt[e2], None), waits)
            if self.cnt[eng] > 0:
                self._need(eng, (self.sem[eng], self.cnt[eng], None), waits)
            for b in self.dmabufs:
                if b.semval > 0:
                    self._need(eng, (b.sem, b.semval, None), waits)
            sn = self.seen[eng]
            for num, (sem, val) in waits.items():
                sn[num] = max(sn.get(num, 0), val)
            self.nwaits += len(waits)
            self.streams[eng].append((list(waits.values()), None, None))
        self.recycle()

    def recycle(self):
        for b in self.dmabufs:
            self.free[self._issw[id(b)]].append((b.sem, b.semval))
            b.sem = None
            b.semval = 0
        self.dmabufs = []
        self._issw = {}

    def wait_all(self, eng, bufs):
        waits = self._deps(eng, bufs, ())
        self.streams[eng].append((waits, None, None))

    def emit(self):
        nc = self.nc
        engmap = {"pe": "tensor", "act": "scalar", "dve": "vector", "pool": "gpsimd", "sp": "sync"}
        with nc.Block() as block:
            for e in ENGS:
                stream = self.streams[e]

                def body(eng, stream=stream):
                    for waits, fn, inc in stream:
                        for sem, val in waits:
                            eng.wait_ge(sem, val)
                        if fn is not None:
                            ins = fn(eng)
                            ins.then_inc(inc[0], inc[1])
                getattr(block, engmap[e])(body)

    def close(self):
        for cm in reversed(self._ctx):
            cm.__exit__(None, None, None)
from contextlib import ExitStack
from concourse.bass_utils import run_bass_kernel_spmd

D = 1024
F = 2816
NCH = D // 128
NFC = F // 128
FG = 2
NFG = NFC // FG
H = 16
HD = 64
EPS = 1e-6
DEPTH = 4
ST = 1024
TT = 512


class MK:
    def __init__(self, NT, n_layers=DEPTH, cfg=None):
        self.NT = NT
        self.cfg = cfg or {}
        nc = self.nc = bass.Bass("TRN2", target_bir_lowering=False)
        self.P = Prog(nc)
        self.dram = {}
        self.dbuf = {}

    def din(self, name, shape, dt=F32):
        t = self.nc.dram_tensor(name, list(shape), dt, kind="ExternalInput")
        self.dram[name] = t
        self.dbuf[name] = Buf(name)
        return t

    def dout(self, name, shape, dt=F32):
        t = self.nc.dram_tensor(name, list(shape), dt, kind="ExternalOutput")
        self.dram[name] = t
        self.dbuf[name] = Buf(name)
        return t

    def dscr(self, name, shape, dt):
        t = self.nc.dram_tensor(name, list(shape), dt)
        self.dram[name] = t
        self.dbuf[name] = Buf(name)
        return t

    _uid = 0

    def uname(self, name):
        MK._uid += 1
        return "%s_%d" % (name, MK._uid)

    def sb(self, stack, name, shape, dt):
        name = self.uname(name)
        t = stack.enter_context(self.nc.sbuf_tensor(name, list(shape), dt))
        return t, Buf(name)

    def psum_banks(self, stack):
        banks = []
        for i in range(8):
            nm = self.uname("psb%d" % i)
            t = stack.enter_context(self.nc.psum_tensor(nm, [128, 512], F32))
            banks.append((t, Buf(nm, excl=True)))
        return banks

    def setup_consts(self, stack, gains_dram, ngain):
        P = self.P
        self.ones_mean, self.b_ones_mean = self.sb(stack, "ones_mean", [128, 128], BF16)
        P.op("pool", lambda e: e.memset(self.ones_mean[:], 1.0 / D), writes=[self.b_ones_mean])
        self.epsc, self.b_epsc = self.sb(stack, "epsc", [128, 1], F32)
        P.op("pool", lambda e: e.memset(self.epsc[:], EPS), writes=[self.b_epsc])
        self.gains, self.b_gains = self.sb(stack, "gains_sb", [128, ngain, NCH], F32)
        P.dma("sp", lambda e: e.dma_start(out=self.gains[:], in_=gains_dram.ap()),
              reads=[self.dbuf[gains_dram.name]], writes=[self.b_gains])

    def rmsnorm_tile(self, ht, b_ht, ntok, gi, out, b_out, sq, b_sq, rstd, b_rstd, bank):
        P = self.P
        ps, b_ps = bank
        for t0 in range(0, ntok, TT):
            P.op("act", lambda e, t0=t0: e.activation(out=sq[:, :, :], in_=ht[:, :, t0:t0 + TT], func=AF.Square),
                 reads=[b_ht], writes=[b_sq])
            for c in range(NCH):
                P.op("pe", lambda e, c=c: e.matmul(ps[:], lhsT=self.ones_mean[:], rhs=sq[:, c, :],
                                                   start=(c == 0), stop=(c == NCH - 1)),
                     reads=[self.b_ones_mean, b_sq], writes=[b_ps], pe_accum=(c > 0))
            P.op("act", lambda e: e.activation(out=rstd[:], in_=ps[:], func=AF.Sqrt, bias=self.epsc[:, 0:1]),
                 reads=[b_ps, self.b_epsc], writes=[b_rstd])
            P.op("dve", lambda e: e.reciprocal(out=rstd[:], in_=rstd[:]),
                 reads=[b_rstd], writes=[b_rstd])
            for c in range(NCH):
                P.op("dve", lambda e, c=c, t0=t0: e.scalar_tensor_tensor(
                    out=out[:, c, t0:t0 + TT], in0=ht[:, c, t0:t0 + TT], scalar=self.gains[:, gi, c:c + 1],
                    in1=rstd[:], op0=ALU.mult, op1=ALU.mult),
                     reads=[b_ht, b_rstd, self.b_gains], writes=[b_out])

    def pass_ffn(self, h_src, h_dst, wgu, wd, gi, epi=None):
        P, nc, NT = self.P, self.nc, self.NT
        ST = min(1024, NT)
        with ExitStack() as stack:
            banks = self.psum_banks(stack)
            ht = [self.sb(stack, "ht%d" % i, [128, NCH, ST], F32) for i in range(1)]
            hn, b_hn = self.sb(stack, "hn", [128, NCH, ST], BF16)
            act, b_act = self.sb(stack, "act", [128, NFC, ST], BF16)
            sq, b_sq = self.sb(stack, "sq", [128, NCH, TT], BF16)
            rstd, b_rstd = self.sb(stack, "rstd", [128, TT], F32)
            sil = [self.sb(stack, "sil%d" % i, [128, TT], F32) for i in range(2)]
            wg_s = [self.sb(stack, "wgu%d" % i, [128, 2, NCH, FG * 128], BF16) for i in range(2)]
            wd_s = [self.sb(stack, "wd%d" % i, [128, NFC, 128], BF16) for i in range(2)]
            if epi is not None:
                eo_dt = BF16 if epi[0] == "norm" else F32
                eo, b_eo = self.sb(stack, "eo", [128, NCH, ST], eo_dt)
            hsrc, hdst = self.dram[h_src], self.dram[h_dst]
            wgu_t, wd_t = self.dram[wgu], self.dram[wd]
            nst = NT // ST
            wi = 0
            di = 0
            for st in range(nst):
                t_lo = st * ST
                h_t, b_h = ht[0]
                P.dma("sp", lambda e, t_lo=t_lo: e.dma_start(
                    out=h_t[:], in_=hsrc.ap()[:, :, t_lo:t_lo + ST].rearrange("c p t -> p c t")),
                    reads=[self.dbuf[h_src]], writes=[b_h])
                self.rmsnorm_tile(h_t, b_h, ST, gi, hn, b_hn, sq, b_sq, rstd, b_rstd, banks[6])
                for fg in range(NFG):
                    w_t, b_w = wg_s[wi % 2]
                    wi += 1
                    P.dma("pool", lambda e, fg=fg, w_t=w_t: e.dma_start(out=w_t[:], in_=wgu_t.ap()[fg]),
                          reads=[self.dbuf[wgu]], writes=[b_w])
                    for f2 in range(FG):
                        f = fg * FG + f2
                        for ti in range(ST // TT):
                            t0 = ti * TT
                            k = (f * 2 + ti) % 2
                            pg, b_pg = banks[k]
                            pu, b_pu = banks[2 + k]
                            for c in range(NCH):
                                P.op("pe", lambda e, c=c, w_t=w_t, f2=f2, t0=t0, pg=pg: e.matmul(
                                    pg[:], lhsT=w_t[:, 0, c, f2 * 128:(f2 + 1) * 128], rhs=hn[:, c, t0:t0 + TT],
                                    start=(c == 0), stop=(c == NCH - 1)),
                                    reads=[b_w, b_hn], writes=[b_pg], pe_accum=(c > 0))
                            for c in range(NCH):
                                P.op("pe", lambda e, c=c, w_t=w_t, f2=f2, t0=t0, pu=pu: e.matmul(
                                    pu[:], lhsT=w_t[:, 1, c, f2 * 128:(f2 + 1) * 128], rhs=hn[:, c, t0:t0 + TT],
                                    start=(c == 0), stop=(c == NCH - 1)),
                                    reads=[b_w, b_hn], writes=[b_pu], pe_accum=(c > 0))
                            s_t, b_s = sil[k]
                            P.op("act", lambda e, s_t=s_t, pg=pg: e.activation(out=s_t[:], in_=pg[:], func=AF.Silu),
                                 reads=[b_pg], writes=[b_s])
                            P.op("dve", lambda e, s_t=s_t, pu=pu, f=f, t0=t0: e.tensor_tensor(
                                out=act[:, f, t0:t0 + TT], in0=pu[:], in1=s_t[:], op=ALU.mult),
                                reads=[b_pu, b_s], writes=[b_act])
                for dc in range(NCH):
                    w_t, b_w = wd_s[di % 2]
                    di += 1
                    P.dma("pool", lambda e, dc=dc, w_t=w_t: e.dma_start(out=w_t[:], in_=wd_t.ap()[dc]),
                          reads=[self.dbuf[wd]], writes=[b_w])
                    for ti in range(ST // TT):
                        t0 = ti * TT
                        po, b_po = banks[4 + (dc * 2 + ti) % 2]
                        for f in range(NFC):
                            P.op("pe", lambda e, f=f, w_t=w_t, t0=t0, po=po: e.matmul(
                                po[:], lhsT=w_t[:, f, :], rhs=act[:, f, t0:t0 + TT],
                                start=(f == 0), stop=(f == NFC - 1)),
                                reads=[b_w, b_act], writes=[b_po], pe_accum=(f > 0))
                        P.op("dve", lambda e, dc=dc, t0=t0, po=po: e.scalar_tensor_tensor(
                            out=h_t[:, dc, t0:t0 + TT], in0=po[:], scalar=0.5, in1=h_t[:, dc, t0:t0 + TT],
                            op0=ALU.mult, op1=ALU.add),
                            reads=[b_po, b_h], writes=[b_h])
                if epi is None or epi[0] == "norm":
                    P.dma("sp", lambda e, t_lo=t_lo: e.dma_start(
                        out=hdst.ap()[:, :, t_lo:t_lo + ST].rearrange("c p t -> p c t"), in_=h_t[:]),
                        reads=[b_h], writes=[self.dbuf[h_dst]], sb=b_h)
                if epi is not None:
                    self.rmsnorm_tile(h_t, b_h, ST, epi[1], eo, b_eo, sq, b_sq, rstd, b_rstd, banks[6])
                    dst = self.dram[epi[2]]
                    P.dma("sp", lambda e, t_lo=t_lo, dst=dst: e.dma_start(
                        out=dst.ap()[:, :, t_lo:t_lo + ST].rearrange("c p t -> p c t"), in_=eo[:]),
                        reads=[b_eo], writes=[self.dbuf[epi[2]]], sb=b_eo)
            P.barrier()

    def finish(self, out_names):
        P = self.P
        P.barrier()
        P.emit()
        P.close()
        return self.nc


def lay_wgu(w):
    g = w.reshape(NCH, 128, 2, NFG, FG * 128)
    return np.ascontiguousarray(g.transpose(3, 1, 2, 0, 4))


def lay_wd(w):
    g = w.reshape(NFC, 128, NCH, 128)
    return np.ascontiguousarray(g.transpose(2, 1, 0, 3))


def lay_gain(gs):
    a = np.stack(gs, 0).reshape(len(gs), NCH, 128)
    return np.ascontiguousarray(a.transpose(2, 0, 1))


def lay_xT(x):
    return np.ascontiguousarray(x.T.reshape(NCH, 128, x.shape[0]))


def unlay_xT(y):
    return np.ascontiguousarray(y.reshape(D, y.shape[-1]).T)


BIG = 30000.0
NEG = -1e30
THETA = 500000.0
C_NTRI = 0
C_ID = 128
C_PSW = 256
C_SEL = 640
C_ONE = 768
C_EN = 896
CW = 896 + 16 * 128


def host_consts():
    c = np.zeros((128, CW), np.float32)
    s = np.arange(128)[:, None]
    j = np.arange(128)[None, :]
    c[:, C_NTRI:C_NTRI + 128] = -(s >= j).astype(np.float32)
    c[:, C_ID:C_ID + 128] = (s == j)
    pm = np.zeros((128, 128), np.float32)
    for p in range(128):
        d = p % 64
        if d < 8:
            pm[p, p + 8] = 1
        elif d < 16:
            pm[p, p - 8] = 1
    c[:, C_PSW:C_PSW + 128] = pm
    pq = np.zeros((128, 128), np.float32)
    for p in range(64, 96):
        dd = p - 64
        pq[p, p + 16 if dd < 16 else p - 16] = 1
    c[:, C_PSW + 128:C_PSW + 256] = pq
    pk = np.zeros((128, 128), np.float32)
    for p in range(32):
        pk[p, p + 16 if p < 16 else p - 16] = 1
    c[:, C_PSW + 256:C_PSW + 384] = pk
    c[0, C_SEL:C_SEL + 128] = -1
    c[32, C_SEL:C_SEL + 128] = -1
    c[:, C_ONE:C_ONE + 128] = 1
    for n in range(16):
        c[n, C_EN + n * 128:C_EN + (n + 1) * 128] = 1
    return c


def host_ropec():
    r = np.zeros((128, 8), np.float32)
    for p in range(128):
        d = p % 64
        if d < 16:
            i = d % 8
            r[p, 0] = THETA ** (-(2.0 * i) / 16.0) / (2 * np.pi)
            r[p, 1] = -1.0 if d < 8 else 1.0
        if 64 <= p < 96:
            dd = p - 64
            i = dd % 16
            r[p, 2] = THETA ** (-(2.0 * i) / 32.0) / (2 * np.pi)
            r[p, 3] = -1.0 if dd < 16 else 1.0
        if p < 32:
            i = p % 16
            r[p, 4] = THETA ** (-(2.0 * i) / 32.0) / (2 * np.pi)
            r[p, 5] = -1.0 if p < 16 else 1.0
    r[:, 6] = -np.pi
    return r


def host_masks(role, NQS=4):
    m = np.zeros((2, 128, 8, 512), np.float32)
    k = np.arange(128)[:, None]
    q = np.arange(128)[None, :]
    for d in range(8):
        for s in range(NQS):
            qb = 2 * s + role
            if d < qb:
                m[:, :, d, s * 128:(s + 1) * 128] = 1
            elif d == qb:
                m[0, :, d, s * 128:(s + 1) * 128] = (k < q)
                m[1, :, d, s * 128:(s + 1) * 128] = (k <= q)
    return m


def host_pos(role, NT):
    j = np.arange(NT // 128)[:, None]
    ql = np.arange(128)[None, :]
    return ((2 * j + role) * 128 + ql).reshape(1, NT).astype(np.float32)


class MKA(MK):
    HPC = {64: 8, 96: 4}

    def alloc_scratch(self, kds=(64, 96)):
        NT = self.NT
        self.VR = min(1024, NT)
        for KD in kds:
            self.dscr("qT%d" % KD, [16, KD, NT], BF16)
            hpc = self.HPC[KD]
            for c in range(16 // hpc):
                self.dscr("kT%d_own_c%d" % (KD, c), [hpc * KD, NT], BF16)
                self.dscr("kT%d_all_c%d" % (KD, c), [2 * hpc * KD, NT], BF16)
        for c in range(NT // self.VR):
            self.dscr("v_own_c%d" % c, [self.VR, D], BF16)
            self.dscr("v_all_c%d" % c, [2 * self.VR, D], BF16)
        self.dscr("aT", [16, 64, NT], BF16)

    def kown(self, KD, h):
        hpc = self.HPC[KD]
        return "kT%d_own_c%d" % (KD, h // hpc), (h % hpc) * KD

    def kall(self, KD, h, r):
        hpc = self.HPC[KD]
        return "kT%d_all_c%d" % (KD, h // hpc), r * hpc * KD + (h % hpc) * KD

    def vown(self, row):
        return "v_own_c%d" % (row // self.VR), row % self.VR

    def vall(self, r, row):
        return "v_all_c%d" % (row // self.VR), r * self.VR + row % self.VR

    def exchange_all(self, KD):
        for c in range(16 // self.HPC[KD]):
            self.exchange("kT%d_own_c%d" % (KD, c), "kT%d_all_c%d" % (KD, c))
        for c in range(self.NT // self.VR):
            self.exchange("v_own_c%d" % c, "v_all_c%d" % c)

    def setup_attn_consts(self, stack):
        P = self.P
        self.cst, self.b_cst = self.sb(stack, "cst", [128, CW], BF16)
        P.dma("pool", lambda e: e.dma_start(out=self.cst[:], in_=self.dram["consts"].ap()),
              reads=[self.dbuf["consts"]], writes=[self.b_cst])
        self.ropec, self.b_ropec = self.sb(stack, "ropec_sb", [128, 8], F32)
        P.dma("sp", lambda e: e.dma_start(out=self.ropec[:], in_=self.dram["ropec"].ap()),
              reads=[self.dbuf["ropec"]], writes=[self.b_ropec])

    def rope_tables(self, stack, col, nrows):
        P, NT = self.P, self.NT
        posb, b_posb = self.sb(stack, "posb", [128, NT], F32)
        P.dma("sp", lambda e: e.dma_start(out=posb[:], in_=self.dram["pos"].ap().partition_broadcast(128)),
              reads=[self.dbuf["pos"]], writes=[b_posb])
        C, b_C = self.sb(stack, "ropeC", [128, NT], F32)
        S, b_S = self.sb(stack, "ropeS", [128, NT], F32)
        tmp, b_tmp = self.sb(stack, "ropeT", [128, NT], F32)
        rc = self.ropec
        R = slice(0, nrows)
        two_pi = float(2 * np.pi)
        MAGIC = 12582912.0
        rnd, b_rnd = self.sb(stack, "ropeR", [128, NT], F32)
        for (dst, b_dst, shift) in ((S, b_S, 0.0), (C, b_C, 0.25)):
            P.op("dve", lambda e, shift=shift: e.tensor_scalar(out=tmp[R, :], in0=posb[R, :], scalar1=rc[R, col:col + 1],
                                                               scalar2=shift, op0=ALU.mult, op1=ALU.add),
                 reads=[b_posb, self.b_ropec], writes=[b_tmp])
            P.op("dve", lambda e: e.tensor_scalar(out=rnd[R, :], in0=tmp[R, :], scalar1=MAGIC, scalar2=None,
                                                  op0=ALU.add), reads=[b_tmp], writes=[b_rnd])
            P.op("dve", lambda e: e.tensor_scalar(out=rnd[R, :], in0=rnd[R, :], scalar1=-MAGIC, scalar2=None,
                                                  op0=ALU.add), reads=[b_rnd], writes=[b_rnd])
            P.op("dve", lambda e: e.tensor_tensor(out=tmp[R, :], in0=tmp[R, :], in1=rnd[R, :], op=ALU.subtract),
                 reads=[b_tmp, b_rnd], writes=[b_tmp])
            P.op("act", lambda e, dst=dst: e.activation(out=dst[R, :], in_=tmp[R, :], func=AF.Sin, scale=two_pi),
                 reads=[b_tmp], writes=[b_dst])
        P.op("dve", lambda e: e.tensor_scalar(out=S[R, :], in0=S[R, :], scalar1=rc[R, col + 1:col + 2], scalar2=None,
                                              op0=ALU.mult),
             reads=[b_S, self.b_ropec], writes=[b_S])
        return (C, b_C), (S, b_S)

    def apply_rope(self, ps, b_ps, ps2, b_ps2, nrows, pswcol, C, b_C, S, b_S, t0, scale, out, b_out,
                   qb, b_qb, t1, b_t1, t2, b_t2):
        P = self.P
        R = slice(0, nrows)
        P.op("act", lambda e: e.copy(out=qb[R, :], in_=ps[R, :]), reads=[b_ps], writes=[b_qb])
        P.op("pe", lambda e: e.matmul(ps2[R, :], lhsT=self.cst[R, pswcol:pswcol + nrows], rhs=qb[R, :],
                                      start=True, stop=True),
             reads=[self.b_cst, b_qb], writes=[b_ps2])
        P.op("dve", lambda e: e.scalar_tensor_tensor(out=t1[R, :], in0=ps[R, :], scalar=scale,
                                                     in1=C[R, t0:t0 + TT], op0=ALU.mult, op1=ALU.mult),
             reads=[b_ps, b_C], writes=[b_t1])
        P.op("dve", lambda e: e.scalar_tensor_tensor(out=t2[R, :], in0=ps2[R, :], scalar=scale,
                                                     in1=S[R, t0:t0 + TT], op0=ALU.mult, op1=ALU.mult),
             reads=[b_ps2, b_S], writes=[b_t2])
        P.op("pool", lambda e: e.tensor_tensor(out=out[R, :], in0=t1[R, :], in1=t2[R, :], op=ALU.add),
             reads=[b_t1, b_t2], writes=[b_out])

    def pass_proj_qkv(self, kind, wname, hn_name):
        P, NT = self.P, self.NT
        qT = self.dram["qT64"]
        with ExitStack() as stack:
            banks = self.psum_banks(stack)
            w, b_w = self.sb(stack, "wqkv", [128, NCH, 3 * D], BF16)
            wd = self.dram[wname]
            for c in range(NCH):
                P.dma("pool", lambda e, c=c: e.dma_start(out=w[:, c, :], in_=wd.ap()[:, c, :]),
                      reads=[self.dbuf[wname]], writes=[b_w])
            hn = [self.sb(stack, "phn%d" % i, [128, NCH, TT], BF16) for i in range(2)]
            osb = [self.sb(stack, "posb%d" % i, [128, TT], BF16) for i in range(4)]
            vsb = [self.sb(stack, "pvsb%d" % i, [128, D], BF16) for i in range(2)]
            rope = kind == "moba"
            if rope:
                (C, b_C), (S, b_S) = self.rope_tables(stack, 0, 128)
                qb = [self.sb(stack, "rqb%d" % i, [128, TT], BF16) for i in range(2)]
                t1 = [self.sb(stack, "rt1%d" % i, [128, TT], F32) for i in range(2)]
                t2 = [self.sb(stack, "rt2%d" % i, [128, TT], F32) for i in range(2)]
            hnd = self.dram[hn_name]
            oi = 0
            vi = 0
            pi = 0
            for tt in range(NT // TT):
                t0 = tt * TT
                h_t, b_h = hn[tt % 2]
                P.dma("sp", lambda e, t0=t0, h_t=h_t: e.dma_start(
                    out=h_t[:], in_=hnd.ap()[:, :, t0:t0 + TT].rearrange("c p t -> p c t")),
                    reads=[self.dbuf[hn_name]], writes=[b_h])
                for which in range(2):
                    for pr in range(8):
                        ps, b_ps = banks[pi % 2]
                        ps2, b_ps2 = banks[2 + pi % 2]
                        pi += 1
                        col = which * D + pr * 128
                        for c in range(NCH):
                            P.op("pe", lambda e, c=c, col=col, ps=ps, h_t=h_t: e.matmul(
                                ps[:], lhsT=w[:, c, col:col + 128], rhs=h_t[:, c, :],
                                start=(c == 0), stop=(c == NCH - 1)),
                                reads=[b_w, b_h], writes=[b_ps], pe_accum=(c > 0))
                        o_t, b_o = osb[oi % 4]
                        oi += 1
                        scale = 0.125 if which == 0 else 1.0
                        if rope:
                            k2 = pi % 2
                            self.apply_rope(ps, b_ps, ps2, b_ps2, 128, C_PSW, C, b_C, S, b_S, t0, scale, o_t, b_o,
                                            qb[k2][0], qb[k2][1], t1[k2][0], t1[k2][1], t2[k2][0], t2[k2][1])
                        else:
                            P.op("act", lambda e, o_t=o_t, ps=ps, scale=scale: e.activation(
                                out=o_t[:], in_=ps[:], func=AF.Copy, scale=scale),
                                reads=[b_ps], writes=[b_o])
                        if which == 0:
                            dst_ap = qT.ap()[2 * pr:2 * pr + 2, :, t0:t0 + TT].rearrange("h d t -> (h d) t")
                            P.dma("sp", lambda e, o_t=o_t, dst_ap=dst_ap: e.dma_start(out=dst_ap, in_=o_t[:]),
                                  reads=[b_o], writes=[self.dbuf["qT64"]], sb=b_o)
                        else:
                            kn, kr0 = self.kown(64, 2 * pr)
                            dst_ap = self.dram[kn].ap()[kr0:kr0 + 128, t0:t0 + TT]
                            P.dma("sp", lambda e, o_t=o_t, dst_ap=dst_ap: e.dma_start(out=dst_ap, in_=o_t[:]),
                                  reads=[b_o], writes=[self.dbuf[kn]], sb=b_o)
                for tb in range(TT // 128):
                    v_t, b_v = vsb[vi % 2]
                    vi += 1
                    for half in range(2):
                        ps, b_ps = banks[4 + half]
                        for c in range(NCH):
                            P.op("pe", lambda e, c=c, half=half, tb=tb, ps=ps, h_t=h_t: e.matmul(
                                ps[:], lhsT=h_t[:, c, tb * 128:(tb + 1) * 128],
                                rhs=w[:, c, 2 * D + half * 512:2 * D + (half + 1) * 512],
                                start=(c == 0), stop=(c == NCH - 1)),
                                reads=[b_w, b_h], writes=[b_ps], pe_accum=(c > 0))
                        P.op("dve", lambda e, half=half, ps=ps, v_t=v_t: e.tensor_copy(
                            out=v_t[:, half * 512:(half + 1) * 512], in_=ps[:]),
                            reads=[b_ps], writes=[b_v])
                    vn, r0 = self.vown(t0 + tb * 128)
                    P.dma("sp", lambda e, v_t=v_t, r0=r0, vn=vn: e.dma_start(
                        out=self.dram[vn].ap()[r0:r0 + 128, :], in_=v_t[:]),
                        reads=[b_v], writes=[self.dbuf[vn]], sb=b_v)
            P.barrier()
        self.exchange_all(64)
        P.barrier()

    def pass_proj_mla(self, w_in, w_uq, w_uk, w_uv, gname, hn_name):
        P, NT = self.P, self.NT
        qT = self.dram["qT96"]
        b_qT = self.dbuf["qT96"]
        cst, b_cst = self.cst, self.b_cst
        with ExitStack() as stack:
            banks = self.psum_banks(stack)
            win, b_win = self.sb(stack, "win", [128, NCH, 672], BF16)
            wuq, b_wuq = self.sb(stack, "wuq", [128, 3, 1536], BF16)
            wuk, b_wuk = self.sb(stack, "wuk", [128, 2, 1024], BF16)
            wuv, b_wuv = self.sb(stack, "wuv", [128, 2, 1024], BF16)
            for (t, b, nm) in ((win, b_win, w_in), (wuq, b_wuq, w_uq), (wuk, b_wuk, w_uk), (wuv, b_wuv, w_uv)):
                P.dma("pool", lambda e, t=t, nm=nm: e.dma_start(out=t[:], in_=self.dram[nm].ap()),
                      reads=[self.dbuf[nm]], writes=[b])
            mg, b_mg = self.sb(stack, "mlag", [128, 5], F32)
            P.dma("sp", lambda e: e.dma_start(out=mg[:], in_=self.dram[gname].ap()),
                  reads=[self.dbuf[gname]], writes=[b_mg])
            (Cq, b_Cq), (Sq, b_Sq) = self.rope_tables(stack, 2, 96)
            (Ck, b_Ck), (Sk, b_Sk) = self.rope_tables(stack, 4, 32)
            hn = [self.sb(stack, "mhn%d" % i, [128, NCH, TT], BF16) for i in range(2)]
            cqf, b_cqf = self.sb(stack, "cqf", [128, 3, TT], F32)
            ckf, b_ckf = self.sb(stack, "ckf", [128, 2, TT], F32)
            sq, b_sq = self.sb(stack, "msq", [128, 3, TT], BF16)
            rstd, b_rstd = self.sb(stack, "mrstd", [128, TT], F32)
            cqn, b_cqn = self.sb(stack, "cqn", [128, 3, TT], BF16)
            ckn, b_ckn = self.sb(stack, "ckn", [128, 2, TT], BF16)
            osb = [self.sb(stack, "mosb%d" % i, [128, TT], BF16) for i in range(4)]
            vsb = [self.sb(stack, "mvsb%d" % i, [128, D], BF16) for i in range(2)]
            qb = [self.sb(stack, "mqb%d" % i, [128, TT], BF16) for i in range(2)]
            t1 = [self.sb(stack, "mt1%d" % i, [128, TT], F32) for i in range(2)]
            t2 = [self.sb(stack, "mt2%d" % i, [128, TT], F32) for i in range(2)]
            hnd = self.dram[hn_name]
            oi = 0
            vi = 0
            pi = 0
            for tt in range(NT // TT):
                t0 = tt * TT
                h_t, b_h = hn[tt % 2]
                P.dma("sp", lambda e, t0=t0, h_t=h_t: e.dma_start(
                    out=h_t[:], in_=hnd.ap()[:, :, t0:t0 + TT].rearrange("c p t -> p c t")),
                    reads=[self.dbuf[hn_name]], writes=[b_h])
                for m in range(5):
                    ps, b_ps = banks[pi % 2]
                    pi += 1
                    for c in range(NCH):
                        P.op("pe", lambda e, c=c, m=m, ps=ps, h_t=h_t: e.matmul(
                            ps[:], lhsT=win[:, c, m * 128:(m + 1) * 128], rhs=h_t[:, c, :],
                            start=(c == 0), stop=(c == NCH - 1)),
                            reads=[b_win, b_h], writes=[b_ps], pe_accum=(c > 0))
                    if m < 3:
                        P.op("act", lambda e, m=m, ps=ps: e.copy(out=cqf[:, m, :], in_=ps[:]),
                             reads=[b_ps], writes=[b_cqf])
                    else:
                        P.op("act", lambda e, m=m, ps=ps: e.copy(out=ckf[:, m - 3, :], in_=ps[:]),
                             reads=[b_ps], writes=[b_ckf])
                ps, b_ps = banks[pi % 2]
                ps2, b_ps2 = banks[2 + pi % 2]
                pi += 1
                for c in range(NCH):
                    P.op("pe", lambda e, c=c, ps=ps, h_t=h_t: e.matmul(
                        ps[0:32, :], lhsT=win[:, c, 640:672], rhs=h_t[:, c, :],
                        start=(c == 0), stop=(c == NCH - 1)),
                        reads=[b_win, b_h], writes=[b_ps], pe_accum=(c > 0))
                kr_t, b_kr = osb[oi % 4]
                oi += 1
                self.apply_rope(ps, b_ps, ps2, b_ps2, 32, C_PSW + 256, Ck, b_Ck, Sk, b_Sk, t0, 1.0, kr_t, b_kr,
                                qb[0][0], qb[0][1], t1[0][0], t1[0][1], t2[0][0], t2[0][1])
                for h in range(H):
                    kn, kr0 = self.kown(96, h)
                    P.dma("sp", lambda e, kn=kn, kr0=kr0, kr_t=kr_t, t0=t0: e.dma_start(
                        out=self.dram[kn].ap()[kr0 + 64:kr0 + 96, t0:t0 + TT], in_=kr_t[0:32, :]),
                        reads=[b_kr], writes=[self.dbuf[kn]], sb=b_kr)
                for (src, b_src, nch, gcol, dst, b_dst, inv) in ((cqf, b_cqf, 3, 0, cqn, b_cqn, 1.0 / 384.0),
                                                                 (ckf, b_ckf, 2, 3, ckn, b_ckn, 1.0 / 256.0)):
                    psn, b_psn = banks[6]
                    P.op("act", lambda e, src=src, nch=nch: e.activation(out=sq[:, 0:nch, :], in_=src[:, 0:nch, :],
                                                                         func=AF.Square),
                         reads=[b_src], writes=[b_sq])
                    for m in range(nch):
                        P.op("pe", lambda e, m=m, nch=nch, psn=psn: e.matmul(
                            psn[:], lhsT=cst[:, C_ONE:C_ONE + 128], rhs=sq[:, m, :], start=(m == 0), stop=(m == nch - 1)),
                            reads=[b_cst, b_sq], writes=[b_psn], pe_accum=(m > 0))
                    P.op("act", lambda e, psn=psn, inv=inv: e.activation(out=rstd[:], in_=psn[:], func=AF.Sqrt,
                                                                         bias=self.epsc[:, 0:1], scale=inv),
                         reads=[b_psn, self.b_epsc], writes=[b_rstd])
                    P.op("dve", lambda e: e.reciprocal(out=rstd[:], in_=rstd[:]), reads=[b_rstd], writes=[b_rstd])
                    for m in range(nch):
                        P.op("dve", lambda e, m=m, src=src, dst=dst, gcol=gcol: e.scalar_tensor_tensor(
                            out=dst[:, m, :], in0=src[:, m, :], scalar=mg[:, gcol + m:gcol + m + 1], in1=rstd[:],
                            op0=ALU.mult, op1=ALU.mult),
                            reads=[b_src, b_rstd, b_mg], writes=[b_dst])
                for h in range(H):
                    ps, b_ps = banks[pi % 2]
                    ps2, b_ps2 = banks[2 + pi % 2]
                    k2 = pi % 2
                    pi += 1
                    for m in range(3):
                        P.op("pe", lambda e, m=m, h=h, ps=ps: e.matmul(
                            ps[0:96, :], lhsT=wuq[:, m, h * 96:(h + 1) * 96], rhs=cqn[:, m, :],
                            start=(m == 0), stop=(m == 2)),
                            reads=[b_wuq, b_cqn], writes=[b_ps], pe_accum=(m > 0))
                    o_t, b_o = osb[oi % 4]
                    oi += 1
                    self.apply_rope(ps, b_ps, ps2, b_ps2, 96, C_PSW + 128, Cq, b_Cq, Sq, b_Sq, t0, 1.0, o_t, b_o,
                                    qb[k2][0], qb[k2][1], t1[k2][0], t1[k2][1], t2[k2][0], t2[k2][1])
                    P.dma("sp", lambda e, h=h, o_t=o_t, t0=t0: e.dma_start(
                        out=qT.ap()[h, :, t0:t0 + TT], in_=o_t[0:96, :]),
                        reads=[b_o], writes=[b_qT], sb=b_o)
                for pr in range(8):
                    ps, b_ps = banks[pi % 2]
                    pi += 1
                    for m in range(2):
                        P.op("pe", lambda e, m=m, pr=pr, ps=ps: e.matmul(
                            ps[:], lhsT=wuk[:, m, pr * 128:(pr + 1) * 128], rhs=ckn[:, m, :],
                            start=(m == 0), stop=(m == 1)),
                            reads=[b_wuk, b_ckn], writes=[b_ps], pe_accum=(m > 0))
                    o_t, b_o = osb[oi % 4]
                    oi += 1
                    P.op("act", lambda e, o_t=o_t, ps=ps: e.copy(out=o_t[:], in_=ps[:]), reads=[b_ps], writes=[b_o])
                    for two in range(2):
                        kn, kr0 = self.kown(96, 2 * pr + two)
                        P.dma("sp", lambda e, kn=kn, kr0=kr0, two=two, o_t=o_t, t0=t0: e.dma_start(
                            out=self.dram[kn].ap()[kr0:kr0 + 64, t0:t0 + TT], in_=o_t[two * 64:(two + 1) * 64, :]),
                            reads=[b_o], writes=[self.dbuf[kn]], sb=b_o)
                for tb in range(TT // 128):
                    v_t, b_v = vsb[vi % 2]
                    vi += 1
                    for half in range(2):
                        ps, b_ps = banks[4 + half]
                        for m in range(2):
                            P.op("pe", lambda e, m=m, half=half, tb=tb, ps=ps: e.matmul(
                                ps[:], lhsT=ckn[:, m, tb * 128:(tb + 1) * 128],
                                rhs=wuv[:, m, half * 512:(half + 1) * 512], start=(m == 0), stop=(m == 1)),
                                reads=[b_wuv, b_ckn], writes=[b_ps], pe_accum=(m > 0))
                        P.op("dve", lambda e, half=half, ps=ps, v_t=v_t: e.tensor_copy(
                            out=v_t[:, half * 512:(half + 1) * 512], in_=ps[:]),
                            reads=[b_ps], writes=[b_v])
                    vn, r0 = self.vown(t0 + tb * 128)
                    P.dma("sp", lambda e, v_t=v_t, r0=r0, vn=vn: e.dma_start(
                        out=self.dram[vn].ap()[r0:r0 + 128, :], in_=v_t[:]),
                        reads=[b_v], writes=[self.dbuf[vn]], sb=b_v)
            P.barrier()
        self.exchange_all(96)
        P.barrier()

    def exchange(self, src, dst):
        P = self.P
        groups = [[0, 1], [2, 3], [4, 5], [6, 7]]
        s_t, d_t = self.dram[src], self.dram[dst]
        P.dma("pool", lambda e: e.collective_compute("AllGather", ALU.bypass, replica_groups=groups,
                                                     ins=[s_t.ap()], outs=[d_t.ap()]),
              reads=[self.dbuf[src]], writes=[self.dbuf[dst]], sb=self.dbuf[dst], inc=1, cls="cc")

    def pass_o(self, wname, h_src, h_dst):
        P, NT = self.P, self.NT
        aT = self.dram["aT"]
        hs, hd = self.dram[h_src], self.dram[h_dst]
        with ExitStack() as stack:
            banks = self.psum_banks(stack)
            w, b_w = self.sb(stack, "wo", [128, NCH, D], BF16)
            P.dma("pool", lambda e: e.dma_start(out=w[:], in_=self.dram[wname].ap()),
                  reads=[self.dbuf[wname]], writes=[b_w])
            at = [self.sb(stack, "oat%d" % i, [128, NCH, TT], BF16) for i in range(2)]
            ht = [self.sb(stack, "oht%d" % i, [128, NCH, TT], F32) for i in range(2)]
            pi = 0
            for tt in range(NT // TT):
                t0 = tt * TT
                a_t, b_a = at[tt % 2]
                h_t, b_h = ht[tt % 2]
                for two in range(2):
                    P.dma("sp", lambda e, t0=t0, a_t=a_t, two=two: e.dma_start(
                        out=a_t[two * 64:(two + 1) * 64, :, :],
                        in_=aT.ap()[:, :, t0:t0 + TT].rearrange("(c two) d t -> two d c t", two=2)[two]),
                        reads=[self.dbuf["aT"]], writes=[b_a])
                P.dma("act", lambda e, t0=t0, h_t=h_t: e.dma_start(
                    out=h_t[:], in_=hs.ap()[:, :, t0:t0 + TT].rearrange("c p t -> p c t")),
                    reads=[self.dbuf[h_src]], writes=[b_h])
                for dc in range(NCH):
                    ps, b_ps = banks[pi % 4]
                    pi += 1
                    for c in range(NCH):
                        P.op("pe", lambda e, c=c, dc=dc, ps=ps, a_t=a_t: e.matmul(
                            ps[:], lhsT=w[:, c, dc * 128:(dc + 1) * 128], rhs=a_t[:, c, :],
                            start=(c == 0), stop=(c == NCH - 1)),
                            reads=[b_w, b_a], writes=[b_ps], pe_accum=(c > 0))
                    P.op("dve", lambda e, dc=dc, ps=ps, h_t=h_t: e.tensor_tensor(
                        out=h_t[:, dc, :], in0=ps[:], in1=h_t[:, dc, :], op=ALU.add),
                        reads=[b_ps, b_h], writes=[b_h])
                P.dma("sp", lambda e, t0=t0, h_t=h_t: e.dma_start(
                    out=hd.ap()[:, :, t0:t0 + TT].rearrange("c p t -> p c t"), in_=h_t[:]),
                    reads=[b_h], writes=[self.dbuf[h_dst]], sb=b_h)
            P.barrier()

    def pass_attn(self, kind):
        P, NT = self.P, self.NT
        KD = 96 if kind == "mla" else 64
        NJ = NT // 128
        NQT = NT // TT
        qT, aT = self.dram["qT%d" % KD], self.dram["aT"]
        b_qT, b_aT = self.dbuf["qT%d" % KD], self.dbuf["aT"]
        cst, b_cst = self.cst, self.b_cst
        sm_scale = float(96 ** -0.5) if kind == "mla" else 1.0
        with ExitStack() as stack:
            banks = self.psum_banks(stack)
            vsb, b_vsb = self.sb(stack, "avsb", [128, 2 * NJ, D], BF16)
            for r in range(2):
                for j0 in range(0, NJ, 4):
                    vn, row0 = self.vall(r, j0 * 128)
                    P.dma("sp" if r == 0 else "pool", lambda e, row0=row0, r=r, j0=j0, vn=vn: e.dma_start(
                        out=vsb[:, r * NJ + j0:r * NJ + j0 + 4, :],
                        in_=self.dram[vn].ap()[row0:row0 + 512, :].rearrange("(j p) f -> p j f", p=128)),
                        reads=[self.dbuf[vn]], writes=[b_vsb])
            mask, b_mask = self.sb(stack, "amask", [128, 8, TT], BF16)
            mk_idx = 0 if kind == "sb" else 1
            P.dma("pool", lambda e: e.dma_start(out=mask[:], in_=self.dram["masks"].ap()[mk_idx]),
                  reads=[self.dbuf["masks"]], writes=[b_mask])
            P.op("dve", lambda e: e.tensor_scalar(out=mask[:], in0=mask[:], scalar1=-1.0, scalar2=BIG,
                                                  op0=ALU.add, op1=ALU.mult), reads=[b_mask], writes=[b_mask])
            ksb = [self.sb(stack, "aksb%d" % i, [128, 2, NT], BF16) for i in range(2)]
            qsb = [self.sb(stack, "aqsb%d" % i, [128, NT], BF16) for i in range(2)]
            for (t_, b_) in ksb + qsb:
                P.op("pool", lambda e, t_=t_: e.memset(t_[64:128], 0.0), writes=[b_])
            e_t = [self.sb(stack, "ae%d" % i, [128, TT], F32) for i in range(2)]
            L_t = [self.sb(stack, "aL%d" % i, [128, TT], BF16) for i in range(2)]
            w_t = [self.sb(stack, "aw%d" % i, [128, TT], BF16) for i in range(3)]
            carry = [self.sb(stack, "acar%d" % i, [64, TT], BF16) for i in range(2)]
            cf, b_cf = self.sb(stack, "acf", [64, TT], F32)
            o_f, b_of = self.sb(stack, "aof", [64, TT], F32)
            rden, b_rden = self.sb(stack, "arden", [64, TT], F32)
            o_sb = [self.sb(stack, "aosb%d" % i, [64, TT], BF16) for i in range(2)]
            if kind == "moba":
                negsel = [self.sb(stack, "ansel%d" % i, [128, NT], BF16) for i in range(2)]
                for (t_, b_) in negsel:
                    P.op("pool", lambda e, t_=t_: e.memset(t_[:], 0.0), writes=[b_])
                km, b_km = self.sb(stack, "akm", [64, 2, NJ], F32)
                kms, b_kms = self.sb(stack, "akms", [64, NJ], F32)
                khi, b_khi = self.sb(stack, "akhi", [128, NJ], BF16)
                klo, b_klo = self.sb(stack, "aklo", [128, NJ], BF16)
                P.op("pool", lambda e: e.memset(khi[:], 0.0), writes=[b_khi])
                P.op("pool", lambda e: e.memset(klo[:], 0.0), writes=[b_klo])
                gm = [self.sb(stack, "agm%d" % i, [128, 16], F32) for i in range(2)]
                m8 = [self.sb(stack, "am8%d" % i, [128, 8], F32) for i in range(2)]
                sel = [self.sb(stack, "asel%d" % i, [128, NJ], F32) for i in range(2)]
                nsb = [self.sb(stack, "ansb%d" % i, [128, NJ], BF16) for i in range(2)]
                pa, b_pa = self.sb(stack, "apa", [128, NJ, NJ], F32)
                p01, b_p01 = self.sb(stack, "ap01", [128, NJ, NJ], F32)
                o01, b_o01 = self.sb(stack, "ao01", [128, NJ, NJ], F32)
                pat = [[1, NJ], [-1, NJ]]
                P.op("pool", lambda e: e.memset(pa[:], 0.0), writes=[b_pa])
                P.op("pool", lambda e: e.affine_select(out=pa[:], in_=pa[:], pattern=pat, compare_op=ALU.is_ge,
                                                       fill=NEG, base=-1, channel_multiplier=0),
                     reads=[b_pa], writes=[b_pa])
                P.op("pool", lambda e: e.memset(p01[:], 1.0), writes=[b_p01])
                P.op("pool", lambda e: e.affine_select(out=p01[:], in_=p01[:], pattern=pat, compare_op=ALU.is_ge,
                                                       fill=0.0, base=-1, channel_multiplier=0),
                     reads=[b_p01], writes=[b_p01])
                P.op("pool", lambda e: e.memset(o01[:], 1.0), writes=[b_o01])
                P.op("pool", lambda e: e.affine_select(out=o01[:], in_=o01[:], pattern=pat, compare_op=ALU.is_equal,
                                                       fill=0.0, base=0, channel_multiplier=0),
                     reads=[b_o01], writes=[b_o01])
                for g_t, b_g in gm:
                    P.op("pool", lambda e, g_t=g_t: e.memset(g_t[:], NEG), writes=[b_g])
            def load_head(h):
                k_t, b_k = ksb[h % 2]
                q_t, b_q = qsb[h % 2]
                for r in range(2):
                    kn, row0 = self.kall(KD, h, r)
                    P.dma("sp" if r == 0 else "pool", lambda e, row0=row0, r=r, k_t=k_t, kn=kn: e.dma_start(
                        out=k_t[0:KD, r, :], in_=self.dram[kn].ap()[row0:row0 + KD, :]),
                        reads=[self.dbuf[kn]], writes=[b_k])
                P.dma("sp", lambda e, h=h, q_t=q_t: e.dma_start(out=q_t[0:KD, :], in_=qT.ap()[h]),
                      reads=[b_qT], writes=[b_q])

            def gate_head(h):
                k_t, b_k = ksb[h % 2]
                q_t, b_q = qsb[h % 2]
                ns_t, b_ns = negsel[h % 2]
                P.op("dve", lambda e, k_t=k_t: e.tensor_reduce(
                    out=km[:], in_=k_t[0:64, :, :].rearrange("d r (j k) -> d r j k", k=128),
                    axis=AX.X, op=ALU.add), reads=[b_k], writes=[b_km])
                P.op("dve", lambda e: e.tensor_tensor(out=kms[:], in0=km[:, 0, :], in1=km[:, 1, :], op=ALU.add),
                     reads=[b_km], writes=[b_kms])
                P.op("dve", lambda e: e.tensor_scalar(out=kms[:], in0=kms[:], scalar1=1.0 / 256.0, scalar2=None,
                                                      op0=ALU.mult), reads=[b_kms], writes=[b_kms])
                P.op("dve", lambda e: e.tensor_copy(out=khi[0:64, :], in_=kms[:]), reads=[b_kms], writes=[b_khi])
                P.op("dve", lambda e: e.tensor_tensor(out=klo[0:64, :], in0=kms[:], in1=khi[0:64, :], op=ALU.subtract),
                     reads=[b_kms, b_khi], writes=[b_klo])
                for js in range(NJ):
                    psG, b_psG = banks[2]
                    psT, b_psT = banks[3]
                    g_t, b_g = gm[js % 2]
                    m_t, b_m = m8[js % 2]
                    s_t, b_s = sel[js % 2]
                    n_t, b_n = nsb[js % 2]
                    P.op("pe", lambda e, js=js, q_t=q_t, psG=psG: e.matmul(
                        psG[:, 0:NJ], lhsT=q_t[:, js * 128:(js + 1) * 128], rhs=khi[:], start=True, stop=False),
                        reads=[b_q, b_khi], writes=[b_psG])
                    P.op("pe", lambda e, js=js, q_t=q_t, psG=psG: e.matmul(
                        psG[:, 0:NJ], lhsT=q_t[:, js * 128:(js + 1) * 128], rhs=klo[:], start=False, stop=True),
                        reads=[b_q, b_klo], writes=[b_psG], pe_accum=True)
                    P.op("dve", lambda e, js=js, g_t=g_t, psG=psG: e.tensor_tensor(
                        out=g_t[:, 0:NJ], in0=psG[:, 0:NJ], in1=pa[:, js, :], op=ALU.add),
                        reads=[b_psG, b_pa], writes=[b_g])
                    P.op("dve", lambda e, g_t=g_t, m_t=m_t: e.max(out=m_t[:], in_=g_t[:]),
                         reads=[b_g], writes=[b_m])
                    P.op("dve", lambda e, g_t=g_t, m_t=m_t, s_t=s_t: e.tensor_scalar(
                        out=s_t[:], in0=g_t[:, 0:NJ], scalar1=m_t[:, 2:3], scalar2=None, op0=ALU.is_ge),
                        reads=[b_g, b_m], writes=[b_s])
                    P.op("dve", lambda e, js=js, s_t=s_t: e.tensor_tensor(
                        out=s_t[:], in0=s_t[:], in1=p01[:, js, :], op=ALU.mult),
                        reads=[b_s, b_p01], writes=[b_s])
                    P.op("dve", lambda e, js=js, s_t=s_t: e.tensor_tensor(
                        out=s_t[:], in0=s_t[:], in1=o01[:, js, :], op=ALU.add),
                        reads=[b_s, b_o01], writes=[b_s])
                    P.op("dve", lambda e, s_t=s_t, n_t=n_t: e.tensor_scalar(
                        out=n_t[:], in0=s_t[:], scalar1=-1.0, scalar2=BIG, op0=ALU.add, op1=ALU.mult),
                        reads=[b_s], writes=[b_n])
                    P.op("pe", lambda e, n_t=n_t, psT=psT: e.matmul(
                        psT[0:NJ, 0:128], lhsT=n_t[:], rhs=cst[:, C_ID:C_ID + 128], start=True, stop=True),
                        reads=[b_n, b_cst], writes=[b_psT])
                    P.op("dve", lambda e, js=js, ns_t=ns_t, psT=psT: e.tensor_copy(
                        out=ns_t[0:NJ, js * 128:(js + 1) * 128], in_=psT[0:NJ, 0:128]),
                        reads=[b_psT], writes=[b_ns])

            tiles = []
            chain = 0
            for h in range(H):
                for qt in range(NQT):
                    nkb = 8 * qt + 8
                    for i in range(nkb):
                        g = (nkb - 1 - i) if kind == "sb" else i
                        tiles.append(dict(h=h, qt=qt, i=i, nkb=nkb, g=g, r=g % 2, j=g // 2, d=g - 8 * qt,
                                          chain=chain, t=len(tiles)))
                    chain += 1
            NTL = len(tiles)
            Lr = [self.sb(stack, "aLr%d" % i, [128, TT], BF16) for i in range(5)]
            wr = [self.sb(stack, "awr%d" % i, [128, TT], BF16) for i in range(4)]
            car = [self.sb(stack, "acr%d" % i, [128, TT], BF16) for i in range(4)]
            if kind == "sb":
                for (t_, b_) in car:
                    P.op("pool", lambda e, t_=t_: e.memset(t_[:], 0.0), writes=[b_])

            def views(T):
                h = T["h"]
                k_t, b_k = ksb[h % 2]
                q_t, b_q = qsb[h % 2]
                kblk = k_t[:, T["r"], T["j"] * 128:(T["j"] + 1) * 128]
                vblk = vsb[:, T["r"] * NJ + T["j"], h * 64:(h + 1) * 64]
                qtile = q_t[:, T["qt"] * TT:(T["qt"] + 1) * TT]
                return kblk, vblk, qtile, b_k, b_q

            def epilogue(T):
                h, qt, ch = T["h"], T["qt"], T["chain"]
                oo, b_oo = o_sb[ch % 2]
                if kind == "sb":
                    psO, b_psO = banks[6 + ch % 2]
                    P.op("dve", lambda e, psO=psO, oo=oo: e.tensor_copy(out=oo[:], in_=psO[0:64, :]),
                         reads=[b_psO], writes=[b_oo])
                else:
                    psO, b_psO = banks[4 + ch % 2]
                    psD, b_psD = banks[6 + ch % 2]
                    P.op("dve", lambda e, psD=psD: e.reciprocal(out=rden[:], in_=psD[0:64, :]),
                         reads=[b_psD], writes=[b_rden])
                    P.op("dve", lambda e, psO=psO, oo=oo: e.tensor_tensor(
                        out=oo[:], in0=psO[0:64, :], in1=rden[:], op=ALU.mult),
                        reads=[b_psO, b_rden], writes=[b_oo])
                P.dma("sp", lambda e, h=h, qt=qt, oo=oo: e.dma_start(
                    out=aT.ap()[h, :, qt * TT:(qt + 1) * TT], in_=oo[:]),
                    reads=[b_oo], writes=[b_aT], sb=b_oo)

            def head_prologue(T):
                h = T["h"]
                if T["qt"] == 0 and T["i"] == 0 and h == 0:
                    load_head(0)
                    if kind == "moba":
                        gate_head(0)
                if T["qt"] == 0 and T["i"] == 4 and h + 1 < H:
                    load_head(h + 1)
                if kind == "moba" and T["qt"] == min(2, NQT - 1) and T["i"] == 0 and h + 1 < H and NQT > 1:
                    gate_head(h + 1)
                if kind == "moba" and NQT == 1 and T["i"] == T["nkb"] - 1 and h + 1 < H:
                    gate_head(h + 1)

            if kind == "sb":
                def s1a(T):
                    head_prologue(T)
                    t = T["t"]
                    kblk, vblk, qtile, b_k, b_q = views(T)
                    psZ, b_psZ = banks[t % 2]
                    ee, b_ee = e_t[t % 2]
                    LL, b_LL = Lr[t % 5]
                    dg = T["d"] >= 0
                    P.op("pe", lambda e: e.matmul(psZ[:], lhsT=kblk, rhs=qtile, start=True, stop=(not dg)),
                         reads=[b_k, b_q], writes=[b_psZ])
                    if dg:
                        d = T["d"]
                        P.op("pe", lambda e: e.matmul(psZ[:], lhsT=cst[:, C_ID:C_ID + 128], rhs=mask[:, d, :],
                                                      start=False, stop=True),
                             reads=[b_cst, b_mask], writes=[b_psZ], pe_accum=True)
                    P.op("act", lambda e: e.activation(out=ee[:], in_=psZ[:], func=AF.Exp),
                         reads=[b_psZ], writes=[b_ee])
                    P.op("act", lambda e: e.activation(out=LL[:], in_=ee[:], func=AF.Ln, bias=1.0),
                         reads=[b_ee], writes=[b_LL])

                def s1b(T):
                    t, i = T["t"], T["i"]
                    if i == T["nkb"] - 1:
                        return
                    LL, b_LL = Lr[t % 5]
                    psC, b_psC = banks[4 + t % 2]
                    cc, b_cc = car[t % 4]
                    P.op("pe", lambda e: e.matmul(psC[0:64, :], lhsT=cst[:, C_ONE:C_ONE + 64], rhs=LL[:],
                                                  start=True, stop=True),
                         reads=[b_cst, b_LL], writes=[b_psC])
                    if i == 0:
                        P.op("dve", lambda e: e.tensor_copy(out=cf[:], in_=psC[0:64, :]),
                             reads=[b_psC], writes=[b_cf])
                    else:
                        P.op("dve", lambda e: e.tensor_tensor(out=cf[:], in0=psC[0:64, :], in1=cf[:], op=ALU.add),
                             reads=[b_psC, b_cf], writes=[b_cf])
                    P.op("pool", lambda e: e.tensor_copy(out=cc[0:64, :], in_=cf[:]), reads=[b_cf], writes=[b_cc])
                    P.op("dve", lambda e: e.tensor_tensor(out=cc[32:64, :], in0=cf[32:64, :], in1=cc[32:64, :],
                                                          op=ALU.subtract),
                         reads=[b_cf, b_cc], writes=[b_cc])

                def s2(T):
                    t, i = T["t"], T["i"]
                    kblk, vblk, qtile, b_k, b_q = views(T)
                    LL, b_LL = Lr[t % 5]
                    psA, b_psA = banks[2 + t % 2]
                    ww, b_ww = wr[t % 4]
                    P.op("pe", lambda e: e.matmul(psA[:], lhsT=kblk, rhs=qtile, start=True, stop=False),
                         reads=[b_k, b_q], writes=[b_psA])
                    P.op("pe", lambda e: e.matmul(psA[:], lhsT=cst[:, C_NTRI:C_NTRI + 128], rhs=LL[:],
                                                  start=False, stop=(i == 0 and T["d"] < 0)),
                         reads=[b_cst, b_LL], writes=[b_psA], pe_accum=True)
                    dg = T["d"] >= 0
                    if i > 0:
                        cc, b_cc = car[(t - 1) % 4]
                        P.op("pe", lambda e: e.matmul(psA[:], lhsT=cst[:, C_SEL:C_SEL + 128], rhs=cc[:],
                                                      start=False, stop=(not dg)),
                             reads=[b_cst, b_cc], writes=[b_psA], pe_accum=True)
                    if dg:
                        d = T["d"]
                        P.op("pe", lambda e: e.matmul(psA[:], lhsT=cst[:, C_ID:C_ID + 128], rhs=mask[:, d, :],
                                                      start=False, stop=True),
                             reads=[b_cst, b_mask], writes=[b_psA], pe_accum=True)
                    P.op("act", lambda e: e.activation(out=ww[:], in_=psA[:], func=AF.Exp),
                         reads=[b_psA], writes=[b_ww])

                def s3(T):
                    t, i, nkb, ch = T["t"], T["i"], T["nkb"], T["chain"]
                    kblk, vblk, qtile, b_k, b_q = views(T)
                    ww, b_ww = wr[t % 4]
                    psO, b_psO = banks[6 + ch % 2]
                    P.op("pe", lambda e: e.matmul(psO[0:64, :], lhsT=vblk, rhs=ww[:], start=(i == 0), stop=(i == nkb - 1)),
                         reads=[b_vsb, b_ww], writes=[b_psO], pe_accum=(i > 0))
                    if i == nkb - 1:
                        epilogue(T)
                stages = [(s1a, 0), (s1b, 1), (s2, 2), (s3, 3)]
            else:
                def s1(T):
                    head_prologue(T)
                    t = T["t"]
                    kblk, vblk, qtile, b_k, b_q = views(T)
                    psZ, b_psZ = banks[t % 2]
                    ww, b_ww = wr[t % 4]
                    dg = T["d"] >= 0
                    P.op("pe", lambda e: e.matmul(psZ[:], lhsT=kblk, rhs=qtile, start=True,
                                                  stop=(kind != "moba" and not dg)),
                         reads=[b_k, b_q], writes=[b_psZ])
                    if kind == "moba":
                        n = T["g"] // 2
                        qt = T["qt"]
                        ns_t, b_ns = negsel[T["h"] % 2]
                        P.op("pe", lambda e: e.matmul(
                            psZ[:], lhsT=cst[:, C_EN + n * 128:C_EN + (n + 1) * 128],
                            rhs=ns_t[:, qt * TT:(qt + 1) * TT], start=False, stop=(not dg)),
                            reads=[b_cst, b_ns], writes=[b_psZ], pe_accum=True)
                    if dg:
                        d = T["d"]
                        P.op("pe", lambda e: e.matmul(psZ[:], lhsT=cst[:, C_ID:C_ID + 128], rhs=mask[:, d, :],
                                                      start=False, stop=True),
                             reads=[b_cst, b_mask], writes=[b_psZ], pe_accum=True)
                    P.op("act", lambda e: e.activation(out=ww[:], in_=psZ[:], func=AF.Exp, scale=sm_scale),
                         reads=[b_psZ], writes=[b_ww])

                def s2(T):
                    t, i, nkb, ch = T["t"], T["i"], T["nkb"], T["chain"]
                    kblk, vblk, qtile, b_k, b_q = views(T)
                    ww, b_ww = wr[t % 4]
                    psO, b_psO = banks[4 + ch % 2]
                    psD, b_psD = banks[6 + ch % 2]
                    P.op("pe", lambda e: e.matmul(psO[0:64, :], lhsT=vblk, rhs=ww[:], start=(i == 0), stop=(i == nkb - 1)),
                         reads=[b_vsb, b_ww], writes=[b_psO], pe_accum=(i > 0))
                    P.op("pe", lambda e: e.matmul(psD[0:64, :], lhsT=cst[:, C_ONE:C_ONE + 64], rhs=ww[:],
                                                  start=(i == 0), stop=(i == nkb - 1)),
                         reads=[b_cst, b_ww], writes=[b_psD], pe_accum=(i > 0))
                    if i == nkb - 1:
                        epilogue(T)
                stages = [(s1, 0), (s2, 2)]
            maxlag = max(l for _, l in stages)
            for step in range(NTL + maxlag):
                for fn, lag in stages:
                    t = step - lag
                    if 0 <= t < NTL:
                        fn(tiles[t])
            P.barrier()


def lay_rows(w):
    k = w.shape[0] // 128
    return np.ascontiguousarray(w.reshape(k, 128, w.shape[1]).transpose(1, 0, 2))


def own_tokens(xb, role):
    S = xb.shape[0]
    return np.ascontiguousarray(xb.reshape(S // 256, 2, 128, -1)[:, role].reshape(S // 2, -1))


def merge_tokens(y0, y1):
    n = y0.shape[0] // 128
    out = np.stack([y0.reshape(n, 128, -1), y1.reshape(n, 128, -1)], axis=1)
    return out.reshape(2 * y0.shape[0], -1)


NT_CORE = 2048
KINDS = ["sb", "moba", "mla", "sb"]


def build_full(NT=NT_CORE, depth=DEPTH):
    m = MKA(NT)
    m.din("xT", [NCH, 128, NT])
    m.din("gains", [128, 3 * depth + 1, NCH])
    m.din("consts", [128, CW]); m.din("ropec", [128, 8]); m.din("masks", [2, 128, 8, 512]); m.din("pos", [1, NT])
    for i in range(depth):
        for f in (1, 2):
            m.din("wgu%d_%d" % (f, i), [NFG, 128, 2, NCH, FG * 128])
            m.din("wd%d_%d" % (f, i), [NCH, 128, NFC, 128])
        kind = KINDS[i]
        if kind == "mla":
            m.din("w_in_%d" % i, [128, NCH, 672]); m.din("w_uq_%d" % i, [128, 3, 1536])
            m.din("w_uk_%d" % i, [128, 2, 1024]); m.din("w_uv_%d" % i, [128, 2, 1024]); m.din("mlag_%d" % i, [128, 5])
        else:
            m.din("wqkv_%d" % i, [128, NCH, 3 * D])
        m.din("wo_%d" % i, [128, NCH, D])
    m.dout("yT", [NCH, 128, NT])
    m.dscr("hA", [NCH, 128, NT], F32); m.dscr("hB", [NCH, 128, NT], F32); m.dscr("hnmix", [NCH, 128, NT], BF16)
    m.alloc_scratch()
    with ExitStack() as cst:
        m.setup_consts(cst, m.dram["gains"], 3 * depth + 1)
        m.setup_attn_consts(cst)
        cur = "xT"
        for i in range(depth):
            kind = KINDS[i]
            m.pass_ffn(cur, "hA", "wgu1_%d" % i, "wd1_%d" % i, 3 * i, epi=("norm", 3 * i + 1, "hnmix"))
            if kind == "mla":
                m.pass_proj_mla("w_in_%d" % i, "w_uq_%d" % i, "w_uk_%d" % i, "w_uv_%d" % i, "mlag_%d" % i, "hnmix")
            else:
                m.pass_proj_qkv(kind, "wqkv_%d" % i, "hnmix")
            m.pass_attn(kind)
            m.pass_o("wo_%d" % i, "hA", "hB")
            last = i == depth - 1
            m.pass_ffn("hB", "hA", "wgu2_%d" % i, "wd2_%d" % i, 3 * i + 2,
                       epi=(("final", 3 * depth, "yT") if last else None))
            cur = "hA"
        nc = m.finish(["yT"])
    return m, nc


def host_inputs(inp, NT=NT_CORE, depth=DEPTH):
    f32 = lambda a: np.asarray(a, dtype=np.float32)
    shared = {}
    gs = []
    for i in range(depth):
        gs += [f32(inp["norm_ffn1"])[i], f32(inp["norm_mix"])[i], f32(inp["norm_ffn2"])[i]]
    gs.append(f32(inp["final_norm"]))
    shared["gains"] = lay_gain(gs)
    shared["consts"] = host_consts()
    shared["ropec"] = host_ropec()
    cnt = {"sb": 0, "moba": 0, "mla": 0}
    for i in range(depth):
        shared["wgu1_%d" % i] = lay_wgu(f32(inp["ffn1_w_gate_up"])[i])
        shared["wd1_%d" % i] = lay_wd(f32(inp["ffn1_w_down"])[i])
        shared["wgu2_%d" % i] = lay_wgu(f32(inp["ffn2_w_gate_up"])[i])
        shared["wd2_%d" % i] = lay_wd(f32(inp["ffn2_w_down"])[i])
        kind = KINDS[i]
        j = cnt[kind]
        cnt[kind] += 1
        if kind == "mla":
            shared["w_in_%d" % i] = lay_rows(f32(inp["mla_w_in"])[j])
            shared["w_uq_%d" % i] = lay_rows(f32(inp["mla_w_uq"])[j])
            wukv = f32(inp["mla_w_ukv"])[j].reshape(256, 16, 2, 64)
            shared["w_uk_%d" % i] = lay_rows(np.ascontiguousarray(wukv[:, :, 0].reshape(256, 1024)))
            shared["w_uv_%d" % i] = lay_rows(np.ascontiguousarray(wukv[:, :, 1].reshape(256, 1024)))
            shared["mlag_%d" % i] = np.ascontiguousarray(np.concatenate(
                [f32(inp["mla_q_norm"])[j].reshape(3, 128), f32(inp["mla_kv_norm"])[j].reshape(2, 128)], 0).T)
            shared["wo_%d" % i] = lay_rows(f32(inp["mla_w_o"])[j])
        elif kind == "moba":
            shared["wqkv_%d" % i] = lay_rows(f32(inp["moba_w_qkv"])[j])
            shared["wo_%d" % i] = lay_rows(f32(inp["moba_w_o"])[j])
        else:
            shared["wqkv_%d" % i] = lay_rows(f32(inp["sb_w_qkv"])[j])
            shared["wo_%d" % i] = lay_rows(f32(inp["sb_w_o"])[j])
    x = f32(inp["x"])
    in_maps = []
    for c in range(8):
        b, r = c // 2, c % 2
        d = dict(shared)
        d["xT"] = lay_xT(own_tokens(x[b], r))
        d["masks"] = host_masks(r)
        d["pos"] = host_pos(r, NT)
        in_maps.append(d)
    return in_maps


_CACHE = {}


def kernel(**inputs):
    if "nc" not in _CACHE:
        _CACHE["nc"] = build_full()[1]
    nc = _CACHE["nc"]
    in_maps = host_inputs(inputs)
    res = run_bass_kernel_spmd(nc, in_maps, core_ids=list(range(8)))
    B = 4
    out = np.empty((B, 2 * NT_CORE, D), np.float32)
    for b in range(B):
        out[b] = merge_tokens(unlay_xT(res.results[2 * b]["yT"]), unlay_xT(res.results[2 * b + 1]["yT"]))
    return out
```

```python
import numpy as np
import concourse.bass as bass
import concourse.mybir as mybir

F32 = mybir.dt.float32
BF16 = mybir.dt.bfloat16
AF = mybir.ActivationFunctionType
ALU = mybir.AluOpType
AX = mybir.AxisListType

ENGS = ("pe", "act", "dve", "pool", "sp")


class Buf:
    __slots__ = ("name", "writers", "readers", "sem", "semval", "excl", "sw", "preaders")

    def __init__(self, name, excl=False):
        self.name = name
        self.excl = excl
        self.writers = {}
        self.readers = {}
        self.preaders = {}
        self.sem = None
        self.semval = 0
        self.sw = None


class Prog:
    def __init__(self, nc):
        self.nc = nc
        self.streams = {e: [] for e in ENGS}
        self.cnt = {e: 0 for e in ENGS}
        self.sem = {}
        self.seen = {e: {} for e in ENGS}
        self.snaps = {}
        self._ctx = []
        for e in ENGS:
            cm = nc.semaphore("s_" + e)
            self.sem[e] = cm.__enter__()
            self._ctx.append(cm)
        self.nwaits = 0
        self.nops = 0
        self.dmabufs = []
        self.free = {"hw": [], "sw": [], "cc": []}
        self._issw = {}

    def new_sem(self, name):
        self._semuid = getattr(self, "_semuid", 0) + 1
        cm = self.nc.semaphore("%s_%d" % (name, self._semuid))
        s = cm.__enter__()
        self._ctx.append(cm)
        return s

    def _need(self, eng, tok, waits):
        sem, val, _ = tok
        if self.seen[eng].get(sem.num, 0) >= val:
            return
        waits[sem.num] = (sem, max(val, waits.get(sem.num, (sem, 0))[1]))

    def _deps(self, eng, reads, writes, pe_accum=False, nowaw=False):
        waits = {}
        for b in reads:
            for k, tok in b.writers.items():
                self._need(eng, tok, waits)
            if b.excl:
                for k, tok in b.readers.items():
                    if k != eng:
                        self._need(eng, tok, waits)
        for b in writes:
            for k, tok in b.readers.items():
                self._need(eng, tok, waits)
            for k, tok in b.preaders.items():
                self._need(eng, tok, waits)
            for k, tok in b.writers.items():
                if pe_accum and k == "pe":
                    continue
                if nowaw and isinstance(k, tuple):
                    continue
                self._need(eng, tok, waits)
        sn = self.seen[eng]
        for num, (sem, val) in waits.items():
            sn[num] = max(sn.get(num, 0), val)
            snap = self.snaps.get((num, val))
            if snap:
                for k2, v2 in snap.items():
                    if sn.get(k2, 0) < v2:
                        sn[k2] = v2
        self.nwaits += len(waits)
        return list(waits.values())

    def _post(self, key, tok, reads, writes):
        for b in reads:
            old = b.readers.get(key)
            if old is None or old[1] < tok[1] or old[0].num != tok[0].num:
                b.readers[key] = tok
        for b in writes:
            if b.readers:
                b.preaders = b.readers
                b.readers = {}
                b.writers = {key: tok}
            else:
                b.writers[key] = tok

    def op(self, eng, fn, reads=(), writes=(), pe_accum=False):
        waits = self._deps(eng, reads, writes, pe_accum=pe_accum)
        self.cnt[eng] += 1
        n = self.cnt[eng]
        sem = self.sem[eng]
        self.snaps[(sem.num, n)] = dict(self.seen[eng])
        tok = (sem, n, None)
        self._post(eng, tok, reads, writes)
        self.streams[eng].append((waits, fn, (sem, 1)))
        self.nops += 1

    def dma(self, eng, fn, reads=(), writes=(), sb=None, inc=16, cls=None):
        dst = sb if sb is not None else writes[0]
        if cls is None:
            cls = "sw" if eng == "pool" else "hw"
        if cls == "sw":
            if dst.sw is None:
                dst.sw = Buf(dst.name + "_sw")
            dst = dst.sw
        if dst.sem is None:
            pool = self.free[cls]
            if pool:
                dst.sem, dst.semval = pool.pop()
            else:
                dst.sem = self.new_sem("d_" + dst.name)
                dst.semval = 0
            self.dmabufs.append(dst)
            self._issw[id(dst)] = cls
        waits = self._deps(eng, reads, writes, nowaw=True)
        dst.semval += inc
        tok = (dst.sem, dst.semval, None)
        self.snaps[(dst.sem.num, dst.semval)] = dict(self.seen[eng])
        self._post(("dma", dst.sem.num), tok, reads, writes)
        self.streams[eng].append((waits, fn, (dst.sem, inc)))
        self.nops += 1

    def barrier(self):
        for eng in ENGS:
            waits = {}
            for e2 in ENGS:
                if e2 != eng and self.cnt[e2] > 0:
                    self._need(eng, (self.sem[e2], self.cnt[e2], None), waits)
            if self.cnt[eng] > 0:
                self._need(eng, (self.sem[eng], self.cnt[eng], None), waits)
            for b in self.dmabufs:
                if b.semval > 0:
                    self._need(eng, (b.sem, b.semval, None), waits)
            sn = self.seen[eng]
            for num, (sem, val) in waits.items():
                sn[num] = max(sn.get(num, 0), val)
            self.nwaits += len(waits)
            self.streams[eng].append((list(waits.values()), None, None))
        self.recycle()

    def recycle(self):
        for b in self.dmabufs:
            self.free[self._issw[id(b)]].append((b.sem, b.semval))
            b.sem = None
            b.semval = 0
        self.dmabufs = []
        self._issw = {}

    def wait_all(self, eng, bufs):
        waits = self._deps(eng, bufs, ())
        self.streams[eng].append((waits, None, None))

    def emit(self):
        nc = self.nc
        engmap = {"pe": "tensor", "act": "scalar", "dve": "vector", "pool": "gpsimd", "sp": "sync"}
        with nc.Block() as block:
            for e in ENGS:
                stream = self.streams[e]

                def body(eng, stream=stream):
                    for waits, fn, inc in stream:
                        for sem, val in waits:
                            eng.wait_ge(sem, val)
                        if fn is not None:
                            ins = fn(eng)
                            ins.then_inc(inc[0], inc[1])
                getattr(block, engmap[e])(body)

    def close(self):
        for cm in reversed(self._ctx):
            cm.__exit__(None, None, None)
from contextlib import ExitStack
from concourse.bass_utils import run_bass_kernel_spmd

D = 1024
F = 2816
NCH = D // 128
NFC = F // 128
FG = 2
NFG = NFC // FG
H = 16
HD = 64
EPS = 1e-6
DEPTH = 4
ST = 1024
TT = 512


class MK:
    def __init__(self, NT, n_layers=DEPTH, cfg=None):
        self.NT = NT
        self.cfg = cfg or {}
        nc = self.nc = bass.Bass("TRN2", target_bir_lowering=False)
        self.P = Prog(nc)
        self.dram = {}
        self.dbuf = {}

    def din(self, name, shape, dt=F32):
        t = self.nc.dram_tensor(name, list(shape), dt, kind="ExternalInput")
        self.dram[name] = t
        self.dbuf[name] = Buf(name)
        return t

    def dout(self, name, shape, dt=F32):
        t = self.nc.dram_tensor(name, list(shape), dt, kind="ExternalOutput")
        self.dram[name] = t
        self.dbuf[name] = Buf(name)
        return t

    def dscr(self, name, shape, dt):
        t = self.nc.dram_tensor(name, list(shape), dt)
        self.dram[name] = t
        self.dbuf[name] = Buf(name)
        return t

    _uid = 0

    def uname(self, name):
        MK._uid += 1
        return "%s_%d" % (name, MK._uid)

    def sb(self, stack, name, shape, dt):
        name = self.uname(name)
        t = stack.enter_context(self.nc.sbuf_tensor(name, list(shape), dt))
        return t, Buf(name)

    def psum_banks(self, stack):
        banks = []
        for i in range(8):
            nm = self.uname("psb%d" % i)
            t = stack.enter_context(self.nc.psum_tensor(nm, [128, 512], F32))
            banks.append((t, Buf(nm, excl=True)))
        return banks

    def setup_consts(self, stack, gains_dram, ngain):
        P = self.P
        self.ones_mean, self.b_ones_mean = self.sb(stack, "ones_mean", [128, 128], BF16)
        P.op("pool", lambda e: e.memset(self.ones_mean[:], 1.0 / D), writes=[self.b_ones_mean])
        self.epsc, self.b_epsc = self.sb(stack, "epsc", [128, 1], F32)
        P.op("pool", lambda e: e.memset(self.epsc[:], EPS), writes=[self.b_epsc])
        self.gains, self.b_gains = self.sb(stack, "gains_sb", [128, ngain, NCH], F32)
        P.dma("sp", lambda e: e.dma_start(out=self.gains[:], in_=gains_dram.ap()),
              reads=[self.dbuf[gains_dram.name]], writes=[self.b_gains])

    def rmsnorm_tile(self, ht, b_ht, ntok, gi, out, b_out, sq, b_sq, rstd, b_rstd, bank):
        P = self.P
        ps, b_ps = bank
        for t0 in range(0, ntok, TT):
            P.op("act", lambda e, t0=t0: e.activation(out=sq[:, :, :], in_=ht[:, :, t0:t0 + TT], func=AF.Square),
                 reads=[b_ht], writes=[b_sq])
            for c in range(NCH):
                P.op("pe", lambda e, c=c: e.matmul(ps[:], lhsT=self.ones_mean[:], rhs=sq[:, c, :],
                                                   start=(c == 0), stop=(c == NCH - 1)),
                     reads=[self.b_ones_mean, b_sq], writes=[b_ps], pe_accum=(c > 0))
            P.op("act", lambda e: e.activation(out=rstd[:], in_=ps[:], func=AF.Sqrt, bias=self.epsc[:, 0:1]),
                 reads=[b_ps, self.b_epsc], writes=[b_rstd])
            P.op("dve", lambda e: e.reciprocal(out=rstd[:], in_=rstd[:]),
                 reads=[b_rstd], writes=[b_rstd])
            for c in range(NCH):
                P.op("dve", lambda e, c=c, t0=t0: e.scalar_tensor_tensor(
                    out=out[:, c, t0:t0 + TT], in0=ht[:, c, t0:t0 + TT], scalar=self.gains[:, gi, c:c + 1],
                    in1=rstd[:], op0=ALU.mult, op1=ALU.mult),
                     reads=[b_ht, b_rstd, self.b_gains], writes=[b_out])

    def pass_ffn(self, h_src, h_dst, wgu, wd, gi, epi=None):
        P, nc, NT = self.P, self.nc, self.NT
        ST = min(1024, NT)
        with ExitStack() as stack:
            banks = self.psum_banks(stack)
            ht = [self.sb(stack, "ht%d" % i, [128, NCH, ST], F32) for i in range(1)]
            hn, b_hn = self.sb(stack, "hn", [128, NCH, ST], BF16)
            act, b_act = self.sb(stack, "act", [128, NFC, ST], BF16)
            sq, b_sq = self.sb(stack, "sq", [128, NCH, TT], BF16)
            rstd, b_rstd = self.sb(stack, "rstd", [128, TT], F32)
            sil = [self.sb(stack, "sil%d" % i, [128, TT], F32) for i in range(2)]
            wg_s = [self.sb(stack, "wgu%d" % i, [128, 2, NCH, FG * 128], BF16) for i in range(2)]
            wd_s = [self.sb(stack, "wd%d" % i, [128, NFC, 128], BF16) for i in range(2)]
            if epi is not None:
                eo_dt = BF16 if epi[0] == "norm" else F32
                eo, b_eo = self.sb(stack, "eo", [128, NCH, ST], eo_dt)
            hsrc, hdst = self.dram[h_src], self.dram[h_dst]
            wgu_t, wd_t = self.dram[wgu], self.dram[wd]
            nst = NT // ST
            wi = 0
            di = 0
            for st in range(nst):
                t_lo = st * ST
                h_t, b_h = ht[0]
                P.dma("sp", lambda e, t_lo=t_lo: e.dma_start(
                    out=h_t[:], in_=hsrc.ap()[:, :, t_lo:t_lo + ST].rearrange("c p t -> p c t")),
                    reads=[self.dbuf[h_src]], writes=[b_h])
                self.rmsnorm_tile(h_t, b_h, ST, gi, hn, b_hn, sq, b_sq, rstd, b_rstd, banks[6])
                for fg in range(NFG):
                    w_t, b_w = wg_s[wi % 2]
                    wi += 1
                    P.dma("pool", lambda e, fg=fg, w_t=w_t: e.dma_start(out=w_t[:], in_=wgu_t.ap()[fg]),
                          reads=[self.dbuf[wgu]], writes=[b_w])
                    for f2 in range(FG):
                        f = fg * FG + f2
                        for ti in range(ST // TT):
                            t0 = ti * TT
                            k = (f * 2 + ti) % 2
                            pg, b_pg = banks[k]
                            pu, b_pu = banks[2 + k]
                            for c in range(NCH):
                                P.op("pe", lambda e, c=c, w_t=w_t, f2=f2, t0=t0, pg=pg: e.matmul(
                                    pg[:], lhsT=w_t[:, 0, c, f2 * 128:(f2 + 1) * 128], rhs=hn[:, c, t0:t0 + TT],
                                    start=(c == 0), stop=(c == NCH - 1)),
                                    reads=[b_w, b_hn], writes=[b_pg], pe_accum=(c > 0))
                            for c in range(NCH):
                                P.op("pe", lambda e, c=c, w_t=w_t, f2=f2, t0=t0, pu=pu: e.matmul(
                                    pu[:], lhsT=w_t[:, 1, c, f2 * 128:(f2 + 1) * 128], rhs=hn[:, c, t0:t0 + TT],
                                    start=(c == 0), stop=(c == NCH - 1)),
                                    reads=[b_w, b_hn], writes=[b_pu], pe_accum=(c > 0))
                            s_t, b_s = sil[k]
                            P.op("act", lambda e, s_t=s_t, pg=pg: e.activation(out=s_t[:], in_=pg[:], func=AF.Silu),
                                 reads=[b_pg], writes=[b_s])
                            P.op("dve", lambda e, s_t=s_t, pu=pu, f=f, t0=t0: e.tensor_tensor(
                                out=act[:, f, t0:t0 + TT], in0=pu[:], in1=s_t[:], op=ALU.mult),
                                reads=[b_pu, b_s], writes=[b_act])
                for dc in range(NCH):
                    w_t, b_w = wd_s[di % 2]
                    di += 1
                    P.dma("pool", lambda e, dc=dc, w_t=w_t: e.dma_start(out=w_t[:], in_=wd_t.ap()[dc]),
                          reads=[self.dbuf[wd]], writes=[b_w])
                    for ti in range(ST // TT):
                        t0 = ti * TT
                        po, b_po = banks[4 + (dc * 2 + ti) % 2]
                        for f in range(NFC):
                            P.op("pe", lambda e, f=f, w_t=w_t, t0=t0, po=po: e.matmul(
                                po[:], lhsT=w_t[:, f, :], rhs=act[:, f, t0:t0 + TT],
                                start=(f == 0), stop=(f == NFC - 1)),
                                reads=[b_w, b_act], writes=[b_po], pe_accum=(f > 0))
                        P.op("dve", lambda e, dc=dc, t0=t0, po=po: e.scalar_tensor_tensor(
                            out=h_t[:, dc, t0:t0 + TT], in0=po[:], scalar=0.5, in1=h_t[:, dc, t0:t0 + TT],
                            op0=ALU.mult, op1=ALU.add),
                            reads=[b_po, b_h], writes=[b_h])
                if epi is None or epi[0] == "norm":
                    P.dma("sp", lambda e, t_lo=t_lo: e.dma_start(
                        out=hdst.ap()[:, :, t_lo:t_lo + ST].rearrange("c p t -> p c t"), in_=h_t[:]),
                        reads=[b_h], writes=[self.dbuf[h_dst]], sb=b_h)
                if epi is not None:
                    self.rmsnorm_tile(h_t, b_h, ST, epi[1], eo, b_eo, sq, b_sq, rstd, b_rstd, banks[6])
                    dst = self.dram[epi[2]]
                    P.dma("sp", lambda e, t_lo=t_lo, dst=dst: e.dma_start(
                        out=dst.ap()[:, :, t_lo:t_lo + ST].rearrange("c p t -> p c t"), in_=eo[:]),
                        reads=[b_eo], writes=[self.dbuf[epi[2]]], sb=b_eo)
            P.barrier()

    def finish(self, out_names):
        P = self.P
        P.barrier()
        P.emit()
        P.close()
        return self.nc


def lay_wgu(w):
    g = w.reshape(NCH, 128, 2, NFG, FG * 128)
    return np.ascontiguousarray(g.transpose(3, 1, 2, 0, 4))


def lay_wd(w):
    g = w.reshape(NFC, 128, NCH, 128)
    return np.ascontiguousarray(g.transpose(2, 1, 0, 3))


def lay_gain(gs):
    a = np.stack(gs, 0).reshape(len(gs), NCH, 128)
    return np.ascontiguousarray(a.transpose(2, 0, 1))


def lay_xT(x):
    return np.ascontiguousarray(x.T.reshape(NCH, 128, x.shape[0]))


def unlay_xT(y):
    return np.ascontiguousarray(y.reshape(D, y.shape[-1]).T)


BIG = 30000.0
NEG = -1e30
THETA = 500000.0
C_NTRI = 0
C_ID = 128
C_PSW = 256
C_SEL = 640
C_ONE = 768
C_EN = 896
CW = 896 + 16 * 128


def host_consts():
    c = np.zeros((128, CW), np.float32)
    s = np.arange(128)[:, None]
    j = np.arange(128)[None, :]
    c[:, C_NTRI:C_NTRI + 128] = -(s >= j).astype(np.float32)
    c[:, C_ID:C_ID + 128] = (s == j)
    pm = np.zeros((128, 128), np.float32)
    for p in range(128):
        d = p % 64
        if d < 8:
            pm[p, p + 8] = 1
        elif d < 16:
            pm[p, p - 8] = 1
    c[:, C_PSW:C_PSW + 128] = pm
    pq = np.zeros((128, 128), np.float32)
    for p in range(64, 96):
        dd = p - 64
        pq[p, p + 16 if dd < 16 else p - 16] = 1
    c[:, C_PSW + 128:C_PSW + 256] = pq
    pk = np.zeros((128, 128), np.float32)
    for p in range(32):
        pk[p, p + 16 if p < 16 else p - 16] = 1
    c[:, C_PSW + 256:C_PSW + 384] = pk
    c[0, C_SEL:C_SEL + 128] = -1
    c[32, C_SEL:C_SEL + 128] = -1
    c[:, C_ONE:C_ONE + 128] = 1
    for n in range(16):
        c[n, C_EN + n * 128:C_EN + (n + 1) * 128] = 1
    return c


def host_ropec():
    r = np.zeros((128, 8), np.float32)
    for p in range(128):
        d = p % 64
        if d < 16:
            i = d % 8
            r[p, 0] = THETA ** (-(2.0 * i) / 16.0) / (2 * np.pi)
            r[p, 1] = -1.0 if d < 8 else 1.0
        if 64 <= p < 96:
            dd = p - 64
            i = dd % 16
            r[p, 2] = THETA ** (-(2.0 * i) / 32.0) / (2 * np.pi)
            r[p, 3] = -1.0 if dd < 16 else 1.0
        if p < 32:
            i = p % 16
            r[p, 4] = THETA ** (-(2.0 * i) / 32.0) / (2 * np.pi)
            r[p, 5] = -1.0 if p < 16 else 1.0
    r[:, 6] = -np.pi
    return r


def host_masks(role, NQS=4):
    m = np.zeros((2, 128, 8, 512), np.float32)
    k = np.arange(128)[:, None]
    q = np.arange(128)[None, :]
    for d in range(8):
        for s in range(NQS):
            qb = 2 * s + role
            if d < qb:
                m[:, :, d, s * 128:(s + 1) * 128] = 1
            elif d == qb:
                m[0, :, d, s * 128:(s + 1) * 128] = (k < q)
                m[1, :, d, s * 128:(s + 1) * 128] = (k <= q)
    return m


def host_pos(role, NT):
    j = np.arange(NT // 128)[:, None]
    ql = np.arange(128)[None, :]
    return ((2 * j + role) * 128 + ql).reshape(1, NT).astype(np.float32)


class MKA(MK):
    HPC = {64: 8, 96: 4}

    def alloc_scratch(self, kds=(64, 96)):
        NT = self.NT
        self.VR = min(1024, NT)
        for KD in kds:
            self.dscr("qT%d" % KD, [16, KD, NT], BF16)
            hpc = self.HPC[KD]
            for c in range(16 // hpc):
                self.dscr("kT%d_own_c%d" % (KD, c), [hpc * KD, NT], BF16)
                self.dscr("kT%d_all_c%d" % (KD, c), [2 * hpc * KD, NT], BF16)
        for c in range(NT // self.VR):
            self.dscr("v_own_c%d" % c, [self.VR, D], BF16)
            self.dscr("v_all_c%d" % c, [2 * self.VR, D], BF16)
        self.dscr("aT", [16, 64, NT], BF16)

    def kown(self, KD, h):
        hpc = self.HPC[KD]
        return "kT%d_own_c%d" % (KD, h // hpc), (h % hpc) * KD

    def kall(self, KD, h, r):
        hpc = self.HPC[KD]
        return "kT%d_all_c%d" % (KD, h // hpc), r * hpc * KD + (h % hpc) * KD

    def vown(self, row):
        return "v_own_c%d" % (row // self.VR), row % self.VR

    def vall(self, r, row):
        return "v_all_c%d" % (row // self.VR), r * self.VR + row % self.VR

    def exchange_all(self, KD):
        for c in range(16 // self.HPC[KD]):
            self.exchange("kT%d_own_c%d" % (KD, c), "kT%d_all_c%d" % (KD, c))
        for c in range(self.NT // self.VR):
            self.exchange("v_own_c%d" % c, "v_all_c%d" % c)

    def setup_attn_consts(self, stack):
        P = self.P
        self.cst, self.b_cst = self.sb(stack, "cst", [128, CW], BF16)
        P.dma("pool", lambda e: e.dma_start(out=self.cst[:], in_=self.dram["consts"].ap()),
              reads=[self.dbuf["consts"]], writes=[self.b_cst])
        self.ropec, self.b_ropec = self.sb(stack, "ropec_sb", [128, 8], F32)
        P.dma("sp", lambda e: e.dma_start(out=self.ropec[:], in_=self.dram["ropec"].ap()),
              reads=[self.dbuf["ropec"]], writes=[self.b_ropec])

    def rope_tables(self, stack, col, nrows):
        P, NT = self.P, self.NT
        posb, b_posb = self.sb(stack, "posb", [128, NT], F32)
        P.dma("sp", lambda e: e.dma_start(out=posb[:], in_=self.dram["pos"].ap().partition_broadcast(128)),
              reads=[self.dbuf["pos"]], writes=[b_posb])
        C, b_C = self.sb(stack, "ropeC", [128, NT], F32)
        S, b_S = self.sb(stack, "ropeS", [128, NT], F32)
        tmp, b_tmp = self.sb(stack, "ropeT", [128, NT], F32)
        rc = self.ropec
        R = slice(0, nrows)
        two_pi = float(2 * np.pi)
        MAGIC = 12582912.0
        rnd, b_rnd = self.sb(stack, "ropeR", [128, NT], F32)
        for (dst, b_dst, shift) in ((S, b_S, 0.0), (C, b_C, 0.25)):
            P.op("dve", lambda e, shift=shift: e.tensor_scalar(out=tmp[R, :], in0=posb[R, :], scalar1=rc[R, col:col + 1],
                                                               scalar2=shift, op0=ALU.mult, op1=ALU.add),
                 reads=[b_posb, self.b_ropec], writes=[b_tmp])
            P.op("dve", lambda e: e.tensor_scalar(out=rnd[R, :], in0=tmp[R, :], scalar1=MAGIC, scalar2=None,
                                                  op0=ALU.add), reads=[b_tmp], writes=[b_rnd])
            P.op("dve", lambda e: e.tensor_scalar(out=rnd[R, :], in0=rnd[R, :], scalar1=-MAGIC, scalar2=None,
                                                  op0=ALU.add), reads=[b_rnd], writes=[b_rnd])
            P.op("dve", lambda e: e.tensor_tensor(out=tmp[R, :], in0=tmp[R, :], in1=rnd[R, :], op=ALU.subtract),
                 reads=[b_tmp, b_rnd], writes=[b_tmp])
            P.op("act", lambda e, dst=dst: e.activation(out=dst[R, :], in_=tmp[R, :], func=AF.Sin, scale=two_pi),
                 reads=[b_tmp], writes=[b_dst])
        P.op("dve", lambda e: e.tensor_scalar(out=S[R, :], in0=S[R, :], scalar1=rc[R, col + 1:col + 2], scalar2=None,
                                              op0=ALU.mult),
             reads=[b_S, self.b_ropec], writes=[b_S])
        return (C, b_C), (S, b_S)

    def apply_rope(self, ps, b_ps, ps2, b_ps2, nrows, pswcol, C, b_C, S, b_S, t0, scale, out, b_out,
                   qb, b_qb, t1, b_t1, t2, b_t2):
        P = self.P
        R = slice(0, nrows)
        P.op("act", lambda e: e.copy(out=qb[R, :], in_=ps[R, :]), reads=[b_ps], writes=[b_qb])
        P.op("pe", lambda e: e.matmul(ps2[R, :], lhsT=self.cst[R, pswcol:pswcol + nrows], rhs=qb[R, :],
                                      start=True, stop=True),
             reads=[self.b_cst, b_qb], writes=[b_ps2])
        P.op("dve", lambda e: e.scalar_tensor_tensor(out=t1[R, :], in0=ps[R, :], scalar=scale,
                                                     in1=C[R, t0:t0 + TT], op0=ALU.mult, op1=ALU.mult),
             reads=[b_ps, b_C], writes=[b_t1])
        P.op("dve", lambda e: e.scalar_tensor_tensor(out=t2[R, :], in0=ps2[R, :], scalar=scale,
                                                     in1=S[R, t0:t0 + TT], op0=ALU.mult, op1=ALU.mult),
             reads=[b_ps2, b_S], writes=[b_t2])
        P.op("pool", lambda e: e.tensor_tensor(out=out[R, :], in0=t1[R, :], in1=t2[R, :], op=ALU.add),
             reads=[b_t1, b_t2], writes=[b_out])

    def pass_proj_qkv(self, kind, wname, hn_name):
        P, NT = self.P, self.NT
        qT = self.dram["qT64"]
        with ExitStack() as stack:
            banks = self.psum_banks(stack)
            w, b_w = self.sb(stack, "wqkv", [128, NCH, 3 * D], BF16)
            wd = self.dram[wname]
            for c in range(NCH):
                P.dma("pool", lambda e, c=c: e.dma_start(out=w[:, c, :], in_=wd.ap()[:, c, :]),
                      reads=[self.dbuf[wname]], writes=[b_w])
            hn = [self.sb(stack, "phn%d" % i, [128, NCH, TT], BF16) for i in range(2)]
            osb = [self.sb(stack, "posb%d" % i, [128, TT], BF16) for i in range(4)]
            vsb = [self.sb(stack, "pvsb%d" % i, [128, D], BF16) for i in range(2)]
            rope = kind == "moba"
            if rope:
                (C, b_C), (S, b_S) = self.rope_tables(stack, 0, 128)
                qb = [self.sb(stack, "rqb%d" % i, [128, TT], BF16) for i in range(2)]
                t1 = [self.sb(stack, "rt1%d" % i, [128, TT], F32) for i in range(2)]
                t2 = [self.sb(stack, "rt2%d" % i, [128, TT], F32) for i in range(2)]
            hnd = self.dram[hn_name]
            oi = 0
            vi = 0
            pi = 0
            for tt in range(NT // TT):
                t0 = tt * TT
                h_t, b_h = hn[tt % 2]
                P.dma("sp", lambda e, t0=t0, h_t=h_t: e.dma_start(
                    out=h_t[:], in_=hnd.ap()[:, :, t0:t0 + TT].rearrange("c p t -> p c t")),
                    reads=[self.dbuf[hn_name]], writes=[b_h])
                for which in range(2):
                    for pr in range(8):
                        ps, b_ps = banks[pi % 2]
                        ps2, b_ps2 = banks[2 + pi % 2]
                        pi += 1
                        col = which * D + pr * 128
                        for c in range(NCH):
                            P.op("pe", lambda e, c=c, col=col, ps=ps, h_t=h_t: e.matmul(
                                ps[:], lhsT=w[:, c, col:col + 128], rhs=h_t[:, c, :],
                                start=(c == 0), stop=(c == NCH - 1)),
                                reads=[b_w, b_h], writes=[b_ps], pe_accum=(c > 0))
                        o_t, b_o = osb[oi % 4]
                        oi += 1
                        scale = 0.125 if which == 0 else 1.0
                        if rope:
                            k2 = pi % 2
                            self.apply_rope(ps, b_ps, ps2, b_ps2, 128, C_PSW, C, b_C, S, b_S, t0, scale, o_t, b_o,
                                            qb[k2][0], qb[k2][1], t1[k2][0], t1[k2][1], t2[k2][0], t2[k2][1])
                        else:
                            P.op("act", lambda e, o_t=o_t, ps=ps, scale=scale: e.activation(
                                out=o_t[:], in_=ps[:], func=AF.Copy, scale=scale),
                                reads=[b_ps], writes=[b_o])
                        if which == 0:
                            dst_ap = qT.ap()[2 * pr:2 * pr + 2, :, t0:t0 + TT].rearrange("h d t -> (h d) t")
                            P.dma("sp", lambda e, o_t=o_t, dst_ap=dst_ap: e.dma_start(out=dst_ap, in_=o_t[:]),
                                  reads=[b_o], writes=[self.dbuf["qT64"]], sb=b_o)
                        else:
                            kn, kr0 = self.kown(64, 2 * pr)
                            dst_ap = self.dram[kn].ap()[kr0:kr0 + 128, t0:t0 + TT]
                            P.dma("sp", lambda e, o_t=o_t, dst_ap=dst_ap: e.dma_start(out=dst_ap, in_=o_t[:]),
                                  reads=[b_o], writes=[self.dbuf[kn]], sb=b_o)
                for tb in range(TT // 128):
                    v_t, b_v = vsb[vi % 2]
                    vi += 1
                    for half in range(2):
                        ps, b_ps = banks[4 + half]
                        for c in range(NCH):
                            P.op("pe", lambda e, c=c, half=half, tb=tb, ps=ps, h_t=h_t: e.matmul(
                                ps[:], lhsT=h_t[:, c, tb * 128:(tb + 1) * 128],
                                rhs=w[:, c, 2 * D + half * 512:2 * D + (half + 1) * 512],
                                start=(c == 0), stop=(c == NCH - 1)),
                                reads=[b_w, b_h], writes=[b_ps], pe_accum=(c > 0))
                        P.op("dve", lambda e, half=half, ps=ps, v_t=v_t: e.tensor_copy(
                            out=v_t[:, half * 512:(half + 1) * 512], in_=ps[:]),
                            reads=[b_ps], writes=[b_v])
                    vn, r0 = self.vown(t0 + tb * 128)
                    P.dma("sp", lambda e, v_t=v_t, r0=r0, vn=vn: e.dma_start(
                        out=self.dram[vn].ap()[r0:r0 + 128, :], in_=v_t[:]),
                        reads=[b_v], writes=[self.dbuf[vn]], sb=b_v)
            P.barrier()
        self.exchange_all(64)
        P.barrier()

    def pass_proj_mla(self, w_in, w_uq, w_uk, w_uv, gname, hn_name):
        P, NT = self.P, self.NT
        qT = self.dram["qT96"]
        b_qT = self.dbuf["qT96"]
        cst, b_cst = self.cst, self.b_cst
        with ExitStack() as stack:
            banks = self.psum_banks(stack)
            win, b_win = self.sb(stack, "win", [128, NCH, 672], BF16)
            wuq, b_wuq = self.sb(stack, "wuq", [128, 3, 1536], BF16)
            wuk, b_wuk = self.sb(stack, "wuk", [128, 2, 1024], BF16)
            wuv, b_wuv = self.sb(stack, "wuv", [128, 2, 1024], BF16)
            for (t, b, nm) in ((win, b_win, w_in), (wuq, b_wuq, w_uq), (wuk, b_wuk, w_uk), (wuv, b_wuv, w_uv)):
                P.dma("pool", lambda e, t=t, nm=nm: e.dma_start(out=t[:], in_=self.dram[nm].ap()),
                      reads=[self.dbuf[nm]], writes=[b])
            mg, b_mg = self.sb(stack, "mlag", [128, 5], F32)
            P.dma("sp", lambda e: e.dma_start(out=mg[:], in_=self.dram[gname].ap()),
                  reads=[self.dbuf[gname]], writes=[b_mg])
            (Cq, b_Cq), (Sq, b_Sq) = self.rope_tables(stack, 2, 96)
            (Ck, b_Ck), (Sk, b_Sk) = self.rope_tables(stack, 4, 32)
            hn = [self.sb(stack, "mhn%d" % i, [128, NCH, TT], BF16) for i in range(2)]
            cqf, b_cqf = self.sb(stack, "cqf", [128, 3, TT], F32)
            ckf, b_ckf = self.sb(stack, "ckf", [128, 2, TT], F32)
            sq, b_sq = self.sb(stack, "msq", [128, 3, TT], BF16)
            rstd, b_rstd = self.sb(stack, "mrstd", [128, TT], F32)
            cqn, b_cqn = self.sb(stack, "cqn", [128, 3, TT], BF16)
            ckn, b_ckn = self.sb(stack, "ckn", [128, 2, TT], BF16)
            osb = [self.sb(stack, "mosb%d" % i, [128, TT], BF16) for i in range(4)]
            vsb = [self.sb(stack, "mvsb%d" % i, [128, D], BF16) for i in range(2)]
            qb = [self.sb(stack, "mqb%d" % i, [128, TT], BF16) for i in range(2)]
            t1 = [self.sb(stack, "mt1%d" % i, [128, TT], F32) for i in range(2)]
            t2 = [self.sb(stack, "mt2%d" % i, [128, TT], F32) for i in range(2)]
            hnd = self.dram[hn_name]
            oi = 0
            vi = 0
            pi = 0
            for tt in range(NT // TT):
                t0 = tt * TT
                h_t, b_h = hn[tt % 2]
                P.dma("sp", lambda e, t0=t0, h_t=h_t: e.dma_start(
                    out=h_t[:], in_=hnd.ap()[:, :, t0:t0 + TT].rearrange("c p t -> p c t")),
                    reads=[self.dbuf[hn_name]], writes=[b_h])
                for m in range(5):
                    ps, b_ps = banks[pi % 2]
                    pi += 1
                    for c in range(NCH):
                        P.op("pe", lambda e, c=c, m=m, ps=ps, h_t=h_t: e.matmul(
                            ps[:], lhsT=win[:, c, m * 128:(m + 1) * 128], rhs=h_t[:, c, :],
                            start=(c == 0), stop=(c == NCH - 1)),
                            reads=[b_win, b_h], writes=[b_ps], pe_accum=(c > 0))
                    if m < 3:
                        P.op("act", lambda e, m=m, ps=ps: e.copy(out=cqf[:, m, :], in_=ps[:]),
                             reads=[b_ps], writes=[b_cqf])
                    else:
                        P.op("act", lambda e, m=m, ps=ps: e.copy(out=ckf[:, m - 3, :], in_=ps[:]),
                             reads=[b_ps], writes=[b_ckf])
                ps, b_ps = banks[pi % 2]
                ps2, b_ps2 = banks[2 + pi % 2]
                pi += 1
                for c in range(NCH):
                    P.op("pe", lambda e, c=c, ps=ps, h_t=h_t: e.matmul(
                        ps[0:32, :], lhsT=win[:, c, 640:672], rhs=h_t[:, c, :],
                        start=(c == 0), stop=(c == NCH - 1)),
                        reads=[b_win, b_h], writes=[b_ps], pe_accum=(c > 0))
                kr_t, b_kr = osb[oi % 4]
                oi += 1
                self.apply_rope(ps, b_ps, ps2, b_ps2, 32, C_PSW + 256, Ck, b_Ck, Sk, b_Sk, t0, 1.0, kr_t, b_kr,
                                qb[0][0], qb[0][1], t1[0][0], t1[0][1], t2[0][0], t2[0][1])
                for h in range(H):
                    kn, kr0 = self.kown(96, h)
                    P.dma("sp", lambda e, kn=kn, kr0=kr0, kr_t=kr_t, t0=t0: e.dma_start(
                        out=self.dram[kn].ap()[kr0 + 64:kr0 + 96, t0:t0 + TT], in_=kr_t[0:32, :]),
                        reads=[b_kr], writes=[self.dbuf[kn]], sb=b_kr)
                for (src, b_src, nch, gcol, dst, b_dst, inv) in ((cqf, b_cqf, 3, 0, cqn, b_cqn, 1.0 / 384.0),
                                                                 (ckf, b_ckf, 2, 3, ckn, b_ckn, 1.0 / 256.0)):
                    psn, b_psn = banks[6]
                    P.op("act", lambda e, src=src, nch=nch: e.activation(out=sq[:, 0:nch, :], in_=src[:, 0:nch, :],
                                                                         func=AF.Square),
                         reads=[b_src], writes=[b_sq])
                    for m in range(nch):
                        P.op("pe", lambda e, m=m, nch=nch, psn=psn: e.matmul(
                            psn[:], lhsT=cst[:, C_ONE:C_ONE + 128], rhs=sq[:, m, :], start=(m == 0), stop=(m == nch - 1)),
                            reads=[b_cst, b_sq], writes=[b_psn], pe_accum=(m > 0))
                    P.op("act", lambda e, psn=psn, inv=inv: e.activation(out=rstd[:], in_=psn[:], func=AF.Sqrt,
                                                                         bias=self.epsc[:, 0:1], scale=inv),
                         reads=[b_psn, self.b_epsc], writes=[b_rstd])
                    P.op("dve", lambda e: e.reciprocal(out=rstd[:], in_=rstd[:]), reads=[b_rstd], writes=[b_rstd])
                    for m in range(nch):
                        P.op("dve", lambda e, m=m, src=src, dst=dst, gcol=gcol: e.scalar_tensor_tensor(
                            out=dst[:, m, :], in0=src[:, m, :], scalar=mg[:, gcol + m:gcol + m + 1], in1=rstd[:],
                            op0=ALU.mult, op1=ALU.mult),
                            reads=[b_src, b_rstd, b_mg], writes=[b_dst])
                for h in range(H):
                    ps, b_ps = banks[pi % 2]
                    ps2, b_ps2 = banks[2 + pi % 2]
                    k2 = pi % 2
                    pi += 1
                    for m in range(3):
                        P.op("pe", lambda e, m=m, h=h, ps=ps: e.matmul(
                            ps[0:96, :], lhsT=wuq[:, m, h * 96:(h + 1) * 96], rhs=cqn[:, m, :],
                            start=(m == 0), stop=(m == 2)),
                            reads=[b_wuq, b_cqn], writes=[b_ps], pe_accum=(m > 0))
                    o_t, b_o = osb[oi % 4]
                    oi += 1
                    self.apply_rope(ps, b_ps, ps2, b_ps2, 96, C_PSW + 128, Cq, b_Cq, Sq, b_Sq, t0, 1.0, o_t, b_o,
                                    qb[k2][0], qb[k2][1], t1[k2][0], t1[k2][1], t2[k2][0], t2[k2][1])
                    P.dma("sp", lambda e, h=h, o_t=o_t, t0=t0: e.dma_start(
                        out=qT.ap()[h, :, t0:t0 + TT], in_=o_t[0:96, :]),
                        reads=[b_o], writes=[b_qT], sb=b_o)
                for pr in range(8):
                    ps, b_ps = banks[pi % 2]
                    pi += 1
                    for m in range(2):
                        P.op("pe", lambda e, m=m, pr=pr, ps=ps: e.matmul(
                            ps[:], lhsT=wuk[:, m, pr * 128:(pr + 1) * 128], rhs=ckn[:, m, :],
                            start=(m == 0), stop=(m == 1)),
                            reads=[b_wuk, b_ckn], writes=[b_ps], pe_accum=(m > 0))
                    o_t, b_o = osb[oi % 4]
                    oi += 1
                    P.op("act", lambda e, o_t=o_t, ps=ps: e.copy(out=o_t[:], in_=ps[:]), reads=[b_ps], writes=[b_o])
                    for two in range(2):
                        kn, kr0 = self.kown(96, 2 * pr + two)
                        P.dma("sp", lambda e, kn=kn, kr0=kr0, two=two, o_t=o_t, t0=t0: e.dma_start(
                            out=self.dram[kn].ap()[kr0:kr0 + 64, t0:t0 + TT], in_=o_t[two * 64:(two + 1) * 64, :]),
                            reads=[b_o], writes=[self.dbuf[kn]], sb=b_o)
                for tb in range(TT // 128):
                    v_t, b_v = vsb[vi % 2]
                    vi += 1
                    for half in range(2):
                        ps, b_ps = banks[4 + half]
                        for m in range(2):
                            P.op("pe", lambda e, m=m, half=half, tb=tb, ps=ps: e.matmul(
                                ps[:], lhsT=ckn[:, m, tb * 128:(tb + 1) * 128],
                                rhs=wuv[:, m, half * 512:(half + 1) * 512], start=(m == 0), stop=(m == 1)),
                                reads=[b_wuv, b_ckn], writes=[b_ps], pe_accum=(m > 0))
                        P.op("dve", lambda e, half=half, ps=ps, v_t=v_t: e.tensor_copy(
                            out=v_t[:, half * 512:(half + 1) * 512], in_=ps[:]),
                            reads=[b_ps], writes=[b_v])
                    vn, r0 = self.vown(t0 + tb * 128)
                    P.dma("sp", lambda e, v_t=v_t, r0=r0, vn=vn: e.dma_start(
                        out=self.dram[vn].ap()[r0:r0 + 128, :], in_=v_t[:]),
                        reads=[b_v], writes=[self.dbuf[vn]], sb=b_v)
            P.barrier()
        self.exchange_all(96)
        P.barrier()

    def exchange(self, src, dst):
        P = self.P
        groups = [[0, 1], [2, 3], [4, 5], [6, 7]]
        s_t, d_t = self.dram[src], self.dram[dst]
        P.dma("pool", lambda e: e.collective_compute("AllGather", ALU.bypass, replica_groups=groups,
                                                     ins=[s_t.ap()], outs=[d_t.ap()]),
              reads=[self.dbuf[src]], writes=[self.dbuf[dst]], sb=self.dbuf[dst], inc=1, cls="cc")

    def pass_o(self, wname, h_src, h_dst):
        P, NT = self.P, self.NT
        aT = self.dram["aT"]
        hs, hd = self.dram[h_src], self.dram[h_dst]
        with ExitStack() as stack:
            banks = self.psum_banks(stack)
            w, b_w = self.sb(stack, "wo", [128, NCH, D], BF16)
            P.dma("pool", lambda e: e.dma_start(out=w[:], in_=self.dram[wname].ap()),
                  reads=[self.dbuf[wname]], writes=[b_w])
            at = [self.sb(stack, "oat%d" % i, [128, NCH, TT], BF16) for i in range(2)]
            ht = [self.sb(stack, "oht%d" % i, [128, NCH, TT], F32) for i in range(2)]
            pi = 0
            for tt in range(NT // TT):
                t0 = tt * TT
                a_t, b_a = at[tt % 2]
                h_t, b_h = ht[tt % 2]
                for two in range(2):
                    P.dma("sp", lambda e, t0=t0, a_t=a_t, two=two: e.dma_start(
                        out=a_t[two * 64:(two + 1) * 64, :, :],
                        in_=aT.ap()[:, :, t0:t0 + TT].rearrange("(c two) d t -> two d c t", two=2)[two]),
                        reads=[self.dbuf["aT"]], writes=[b_a])
                P.dma("act", lambda e, t0=t0, h_t=h_t: e.dma_start(
                    out=h_t[:], in_=hs.ap()[:, :, t0:t0 + TT].rearrange("c p t -> p c t")),
                    reads=[self.dbuf[h_src]], writes=[b_h])
                for dc in range(NCH):
                    ps, b_ps = banks[pi % 4]
                    pi += 1
                    for c in range(NCH):
                        P.op("pe", lambda e, c=c, dc=dc, ps=ps, a_t=a_t: e.matmul(
                            ps[:], lhsT=w[:, c, dc * 128:(dc + 1) * 128], rhs=a_t[:, c, :],
                            start=(c == 0), stop=(c == NCH - 1)),
                            reads=[b_w, b_a], writes=[b_ps], pe_accum=(c > 0))
                    P.op("dve", lambda e, dc=dc, ps=ps, h_t=h_t: e.tensor_tensor(
                        out=h_t[:, dc, :], in0=ps[:], in1=h_t[:, dc, :], op=ALU.add),
                        reads=[b_ps, b_h], writes=[b_h])
                P.dma("sp", lambda e, t0=t0, h_t=h_t: e.dma_start(
                    out=hd.ap()[:, :, t0:t0 + TT].rearrange("c p t -> p c t"), in_=h_t[:]),
                    reads=[b_h], writes=[self.dbuf[h_dst]], sb=b_h)
            P.barrier()

    def pass_attn(self, kind):
        P, NT = self.P, self.NT
        KD = 96 if kind == "mla" else 64
        NJ = NT // 128
        NQT = NT // TT
        qT, aT = self.dram["qT%d" % KD], self.dram["aT"]
        b_qT, b_aT = self.dbuf["qT%d" % KD], self.dbuf["aT"]
        cst, b_cst = self.cst, self.b_cst
        sm_scale = float(96 ** -0.5) if kind == "mla" else 1.0
        with ExitStack() as stack:
            banks = self.psum_banks(stack)
            vsb, b_vsb = self.sb(stack, "avsb", [128, 2 * NJ, D], BF16)
            for r in range(2):
                for j0 in range(0, NJ, 4):
                    vn, row0 = self.vall(r, j0 * 128)
                    P.dma("sp" if r == 0 else "pool", lambda e, row0=row0, r=r, j0=j0, vn=vn: e.dma_start(
                        out=vsb[:, r * NJ + j0:r * NJ + j0 + 4, :],
                        in_=self.dram[vn].ap()[row0:row0 + 512, :].rearrange("(j p) f -> p j f", p=128)),
                        reads=[self.dbuf[vn]], writes=[b_vsb])
            mask, b_mask = self.sb(stack, "amask", [128, 8, TT], BF16)
            mk_idx = 0 if kind == "sb" else 1
            P.dma("pool", lambda e: e.dma_start(out=mask[:], in_=self.dram["masks"].ap()[mk_idx]),
                  reads=[self.dbuf["masks"]], writes=[b_mask])
            P.op("dve", lambda e: e.tensor_scalar(out=mask[:], in0=mask[:], scalar1=-1.0, scalar2=BIG,
                                                  op0=ALU.add, op1=ALU.mult), reads=[b_mask], writes=[b_mask])
            ksb = [self.sb(stack, "aksb%d" % i, [128, 2, NT], BF16) for i in range(2)]
            qsb = [self.sb(stack, "aqsb%d" % i, [128, NT], BF16) for i in range(2)]
            for (t_, b_) in ksb + qsb:
                P.op("pool", lambda e, t_=t_: e.memset(t_[64:128], 0.0), writes=[b_])
            e_t = [self.sb(stack, "ae%d" % i, [128, TT], F32) for i in range(2)]
            L_t = [self.sb(stack, "aL%d" % i, [128, TT], BF16) for i in range(2)]
            w_t = [self.sb(stack, "aw%d" % i, [128, TT], BF16) for i in range(3)]
            carry = [self.sb(stack, "acar%d" % i, [64, TT], BF16) for i in range(2)]
            cf, b_cf = self.sb(stack, "acf", [64, TT], F32)
            o_f, b_of = self.sb(stack, "aof", [64, TT], F32)
            rden, b_rden = self.sb(stack, "arden", [64, TT], F32)
            o_sb = [self.sb(stack, "aosb%d" % i, [64, TT], BF16) for i in range(2)]
            if kind == "moba":
                negsel = [self.sb(stack, "ansel%d" % i, [128, NT], BF16) for i in range(2)]
                for (t_, b_) in negsel:
                    P.op("pool", lambda e, t_=t_: e.memset(t_[:], 0.0), writes=[b_])
                km, b_km = self.sb(stack, "akm", [64, 2, NJ], F32)
                kms, b_kms = self.sb(stack, "akms", [64, NJ], F32)
                khi, b_khi = self.sb(stack, "akhi", [128, NJ], BF16)
                klo, b_klo = self.sb(stack, "aklo", [128, NJ], BF16)
                P.op("pool", lambda e: e.memset(khi[:], 0.0), writes=[b_khi])
                P.op("pool", lambda e: e.memset(klo[:], 0.0), writes=[b_klo])
                gm = [self.sb(stack, "agm%d" % i, [128, 16], F32) for i in range(2)]
                m8 = [self.sb(stack, "am8%d" % i, [128, 8], F32) for i in range(2)]
                sel = [self.sb(stack, "asel%d" % i, [128, NJ], F32) for i in range(2)]
                nsb = [self.sb(stack, "ansb%d" % i, [128, NJ], BF16) for i in range(2)]
                pa, b_pa = self.sb(stack, "apa", [128, NJ, NJ], F32)
                p01, b_p01 = self.sb(stack, "ap01", [128, NJ, NJ], F32)
                o01, b_o01 = self.sb(stack, "ao01", [128, NJ, NJ], F32)
                pat = [[1, NJ], [-1, NJ]]
                P.op("pool", lambda e: e.memset(pa[:], 0.0), writes=[b_pa])
                P.op("pool", lambda e: e.affine_select(out=pa[:], in_=pa[:], pattern=pat, compare_op=ALU.is_ge,
                                                       fill=NEG, base=-1, channel_multiplier=0),
                     reads=[b_pa], writes=[b_pa])
                P.op("pool", lambda e: e.memset(p01[:], 1.0), writes=[b_p01])
                P.op("pool", lambda e: e.affine_select(out=p01[:], in_=p01[:], pattern=pat, compare_op=ALU.is_ge,
                                                       fill=0.0, base=-1, channel_multiplier=0),
                     reads=[b_p01], writes=[b_p01])
                P.op("pool", lambda e: e.memset(o01[:], 1.0), writes=[b_o01])
                P.op("pool", lambda e: e.affine_select(out=o01[:], in_=o01[:], pattern=pat, compare_op=ALU.is_equal,
                                                       fill=0.0, base=0, channel_multiplier=0),
                     reads=[b_o01], writes=[b_o01])
                for g_t, b_g in gm:
                    P.op("pool", lambda e, g_t=g_t: e.memset(g_t[:], NEG), writes=[b_g])
            def load_head(h):
                k_t, b_k = ksb[h % 2]
                q_t, b_q = qsb[h % 2]
                for r in range(2):
                    kn, row0 = self.kall(KD, h, r)
                    P.dma("sp" if r == 0 else "pool", lambda e, row0=row0, r=r, k_t=k_t, kn=kn: e.dma_start(
                        out=k_t[0:KD, r, :], in_=self.dram[kn].ap()[row0:row0 + KD, :]),
                        reads=[self.dbuf[kn]], writes=[b_k])
                P.dma("sp", lambda e, h=h, q_t=q_t: e.dma_start(out=q_t[0:KD, :], in_=qT.ap()[h]),
                      reads=[b_qT], writes=[b_q])

            def gate_head(h):
                k_t, b_k = ksb[h % 2]
                q_t, b_q = qsb[h % 2]
                ns_t, b_ns = negsel[h % 2]
                P.op("dve", lambda e, k_t=k_t: e.tensor_reduce(
                    out=km[:], in_=k_t[0:64, :, :].rearrange("d r (j k) -> d r j k", k=128),
                    axis=AX.X, op=ALU.add), reads=[b_k], writes=[b_km])
                P.op("dve", lambda e: e.tensor_tensor(out=kms[:], in0=km[:, 0, :], in1=km[:, 1, :], op=ALU.add),
                     reads=[b_km], writes=[b_kms])
                P.op("dve", lambda e: e.tensor_scalar(out=kms[:], in0=kms[:], scalar1=1.0 / 256.0, scalar2=None,
                                                      op0=ALU.mult), reads=[b_kms], writes=[b_kms])
                P.op("dve", lambda e: e.tensor_copy(out=khi[0:64, :], in_=kms[:]), reads=[b_kms], writes=[b_khi])
                P.op("dve", lambda e: e.tensor_tensor(out=klo[0:64, :], in0=kms[:], in1=khi[0:64, :], op=ALU.subtract),
                     reads=[b_kms, b_khi], writes=[b_klo])
                for js in range(NJ):
                    psG, b_psG = banks[2]
                    psT, b_psT = banks[3]
                    g_t, b_g = gm[js % 2]
                    m_t, b_m = m8[js % 2]
                    s_t, b_s = sel[js % 2]
                    n_t, b_n = nsb[js % 2]
                    P.op("pe", lambda e, js=js, q_t=q_t, psG=psG: e.matmul(
                        psG[:, 0:NJ], lhsT=q_t[:, js * 128:(js + 1) * 128], rhs=khi[:], start=True, stop=False),
                        reads=[b_q, b_khi], writes=[b_psG])
                    P.op("pe", lambda e, js=js, q_t=q_t, psG=psG: e.matmul(
                        psG[:, 0:NJ], lhsT=q_t[:, js * 128:(js + 1) * 128], rhs=klo[:], start=False, stop=True),
                        reads=[b_q, b_klo], writes=[b_psG], pe_accum=True)
                    P.op("dve", lambda e, js=js, g_t=g_t, psG=psG: e.tensor_tensor(
                        out=g_t[:, 0:NJ], in0=psG[:, 0:NJ], in1=pa[:, js, :], op=ALU.add),
                        reads=[b_psG, b_pa], writes=[b_g])
                    P.op("dve", lambda e, g_t=g_t, m_t=m_t: e.max(out=m_t[:], in_=g_t[:]),
                         reads=[b_g], writes=[b_m])
                    P.op("dve", lambda e, g_t=g_t, m_t=m_t, s_t=s_t: e.tensor_scalar(
                        out=s_t[:], in0=g_t[:, 0:NJ], scalar1=m_t[:, 2:3], scalar2=None, op0=ALU.is_ge),
                        reads=[b_g, b_m], writes=[b_s])
                    P.op("dve", lambda e, js=js, s_t=s_t: e.tensor_tensor(
                        out=s_t[:], in0=s_t[:], in1=p01[:, js, :], op=ALU.mult),
                        reads=[b_s, b_p01], writes=[b_s])
                    P.op("dve", lambda e, js=js, s_t=s_t: e.tensor_tensor(
                        out=s_t[:], in0=s_t[:], in1=o01[:, js, :], op=ALU.add),
                        reads=[b_s, b_o01], writes=[b_s])
                    P.op("dve", lambda e, s_t=s_t, n_t=n_t: e.tensor_scalar(
                        out=n_t[:], in0=s_t[:], scalar1=-1.0, scalar2=BIG, op0=ALU.add, op1=ALU.mult),
                        reads=[b_s], writes=[b_n])
                    P.op("pe", lambda e, n_t=n_t, psT=psT: e.matmul(
                        psT[0:NJ, 0:128], lhsT=n_t[:], rhs=cst[:, C_ID:C_ID + 128], start=True, stop=True),
                        reads=[b_n, b_cst], writes=[b_psT])
                    P.op("dve", lambda e, js=js, ns_t=ns_t, psT=psT: e.tensor_copy(
                        out=ns_t[0:NJ, js * 128:(js + 1) * 128], in_=psT[0:NJ, 0:128]),
                        reads=[b_psT], writes=[b_ns])

            tiles = []
            chain = 0
            for h in range(H):
                for qt in range(NQT):
                    nkb = 8 * qt + 8
                    for i in range(nkb):
                        g = (nkb - 1 - i) if kind == "sb" else i
                        tiles.append(dict(h=h, qt=qt, i=i, nkb=nkb, g=g, r=g % 2, j=g // 2, d=g - 8 * qt,
                                          chain=chain, t=len(tiles)))
                    chain += 1
            NTL = len(tiles)
            Lr = [self.sb(stack, "aLr%d" % i, [128, TT], BF16) for i in range(6)]
            cfr = [self.sb(stack, "acfr%d" % i, [64, TT], F32) for i in range(2)]
            wr = [self.sb(stack, "awr%d" % i, [128, TT], BF16) for i in range(4)]
            car = [self.sb(stack, "acr%d" % i, [128, TT], BF16) for i in range(4)]
            if kind == "sb":
                for (t_, b_) in car:
                    P.op("pool", lambda e, t_=t_: e.memset(t_[:], 0.0), writes=[b_])

            def views(T):
                h = T["h"]
                k_t, b_k = ksb[h % 2]
                q_t, b_q = qsb[h % 2]
                kblk = k_t[:, T["r"], T["j"] * 128:(T["j"] + 1) * 128]
                vblk = vsb[:, T["r"] * NJ + T["j"], h * 64:(h + 1) * 64]
                qtile = q_t[:, T["qt"] * TT:(T["qt"] + 1) * TT]
                return kblk, vblk, qtile, b_k, b_q

            def epilogue(T):
                h, qt, ch = T["h"], T["qt"], T["chain"]
                oo, b_oo = o_sb[ch % 2]
                if kind == "sb":
                    psO, b_psO = banks[6 + ch % 2]
                    P.op("dve", lambda e, psO=psO, oo=oo: e.tensor_copy(out=oo[:], in_=psO[0:64, :]),
                         reads=[b_psO], writes=[b_oo])
                else:
                    psO, b_psO = banks[4 + ch % 2]
                    psD, b_psD = banks[6 + ch % 2]
                    P.op("dve", lambda e, psD=psD: e.reciprocal(out=rden[:], in_=psD[0:64, :]),
                         reads=[b_psD], writes=[b_rden])
                    P.op("dve", lambda e, psO=psO, oo=oo: e.tensor_tensor(
                        out=oo[:], in0=psO[0:64, :], in1=rden[:], op=ALU.mult),
                        reads=[b_psO, b_rden], writes=[b_oo])
                P.dma("sp", lambda e, h=h, qt=qt, oo=oo: e.dma_start(
                    out=aT.ap()[h, :, qt * TT:(qt + 1) * TT], in_=oo[:]),
                    reads=[b_oo], writes=[b_aT], sb=b_oo)

            def head_prologue(T):
                h = T["h"]
                if T["qt"] == 0 and T["i"] == 0 and h == 0:
                    load_head(0)
                    if kind == "moba":
                        gate_head(0)
                if T["qt"] == 0 and T["i"] == 6 and h + 1 < H:
                    load_head(h + 1)
                if kind == "moba" and T["qt"] == min(2, NQT - 1) and T["i"] == 0 and h + 1 < H and NQT > 1:
                    gate_head(h + 1)
                if kind == "moba" and NQT == 1 and T["i"] == T["nkb"] - 1 and h + 1 < H:
                    gate_head(h + 1)

            if kind == "sb":
                def s1a(T):
                    head_prologue(T)
                    t = T["t"]
                    kblk, vblk, qtile, b_k, b_q = views(T)
                    psZ, b_psZ = banks[t % 2]
                    ee, b_ee = e_t[t % 2]
                    LL, b_LL = Lr[t % 6]
                    dg = T["d"] >= 0
                    P.op("pe", lambda e: e.matmul(psZ[:], lhsT=kblk, rhs=qtile, start=True, stop=(not dg)),
                         reads=[b_k, b_q], writes=[b_psZ])
                    if dg:
                        d = T["d"]
                        P.op("pe", lambda e: e.matmul(psZ[:], lhsT=cst[:, C_ID:C_ID + 128], rhs=mask[:, d, :],
                                                      start=False, stop=True),
                             reads=[b_cst, b_mask], writes=[b_psZ], pe_accum=True)
                    P.op("act", lambda e: e.activation(out=ee[:], in_=psZ[:], func=AF.Exp),
                         reads=[b_psZ], writes=[b_ee])
                    P.op("act", lambda e: e.activation(out=LL[:], in_=ee[:], func=AF.Ln, bias=1.0),
                         reads=[b_ee], writes=[b_LL])

                def s1b(T):
                    t, i = T["t"], T["i"]
                    if i == T["nkb"] - 1:
                        return
                    LL, b_LL = Lr[t % 6]
                    psC, b_psC = banks[4 + t % 2]
                    cc, b_cc = car[t % 4]
                    cfn, b_cfn = cfr[t % 2]
                    cfo, b_cfo = cfr[(t - 1) % 2]
                    P.op("pe", lambda e: e.matmul(psC[0:64, :], lhsT=cst[:, C_ONE:C_ONE + 64], rhs=LL[:],
                                                  start=True, stop=True),
                         reads=[b_cst, b_LL], writes=[b_psC])
                    if i == 0:
                        P.op("dve", lambda e: e.tensor_copy(out=cfn[:], in_=psC[0:64, :]),
                             reads=[b_psC], writes=[b_cfn])
                    else:
                        P.op("dve", lambda e: e.tensor_tensor(out=cfn[:], in0=psC[0:64, :], in1=cfo[:], op=ALU.add),
                             reads=[b_psC, b_cfo], writes=[b_cfn])
                    P.op("pool", lambda e: e.tensor_copy(out=cc[0:64, :], in_=cfn[:]), reads=[b_cfn], writes=[b_cc])

                def s1c(T):
                    t, i = T["t"], T["i"]
                    if i == T["nkb"] - 1:
                        return
                    cc, b_cc = car[t % 4]
                    cfn, b_cfn = cfr[t % 2]
                    P.op("dve", lambda e: e.tensor_tensor(out=cc[32:64, :], in0=cfn[32:64, :], in1=cc[32:64, :],
                                                          op=ALU.subtract),
                         reads=[b_cfn, b_cc], writes=[b_cc])

                def s2(T):
                    t, i = T["t"], T["i"]
                    kblk, vblk, qtile, b_k, b_q = views(T)
                    LL, b_LL = Lr[t % 6]
                    psA, b_psA = banks[2 + t % 2]
                    ww, b_ww = wr[t % 4]
                    P.op("pe", lambda e: e.matmul(psA[:], lhsT=kblk, rhs=qtile, start=True, stop=False),
                         reads=[b_k, b_q], writes=[b_psA])
                    P.op("pe", lambda e: e.matmul(psA[:], lhsT=cst[:, C_NTRI:C_NTRI + 128], rhs=LL[:],
                                                  start=False, stop=(i == 0 and T["d"] < 0)),
                         reads=[b_cst, b_LL], writes=[b_psA], pe_accum=True)
                    dg = T["d"] >= 0
                    if i > 0:
                        cc, b_cc = car[(t - 1) % 4]
                        P.op("pe", lambda e: e.matmul(psA[:], lhsT=cst[:, C_SEL:C_SEL + 128], rhs=cc[:],
                                                      start=False, stop=(not dg)),
                             reads=[b_cst, b_cc], writes=[b_psA], pe_accum=True)
                    if dg:
                        d = T["d"]
                        P.op("pe", lambda e: e.matmul(psA[:], lhsT=cst[:, C_ID:C_ID + 128], rhs=mask[:, d, :],
                                                      start=False, stop=True),
                             reads=[b_cst, b_mask], writes=[b_psA], pe_accum=True)
                    P.op("act", lambda e: e.activation(out=ww[:], in_=psA[:], func=AF.Exp),
                         reads=[b_psA], writes=[b_ww])

                def s3(T):
                    t, i, nkb, ch = T["t"], T["i"], T["nkb"], T["chain"]
                    kblk, vblk, qtile, b_k, b_q = views(T)
                    ww, b_ww = wr[t % 4]
                    psO, b_psO = banks[6 + ch % 2]
                    P.op("pe", lambda e: e.matmul(psO[0:64, :], lhsT=vblk, rhs=ww[:], start=(i == 0), stop=(i == nkb - 1)),
                         reads=[b_vsb, b_ww], writes=[b_psO], pe_accum=(i > 0))
                    if i == nkb - 1:
                        epilogue(T)
                stages = [(s1a, 0), (s1b, 1), (s1c, 2), (s2, 3), (s3, 4)]
            else:
                def s1(T):
                    head_prologue(T)
                    t = T["t"]
                    kblk, vblk, qtile, b_k, b_q = views(T)
                    psZ, b_psZ = banks[t % 2]
                    ww, b_ww = wr[t % 4]
                    dg = T["d"] >= 0
                    P.op("pe", lambda e: e.matmul(psZ[:], lhsT=kblk, rhs=qtile, start=True,
                                                  stop=(kind != "moba" and not dg)),
                         reads=[b_k, b_q], writes=[b_psZ])
                    if kind == "moba":
                        n = T["g"] // 2
                        qt = T["qt"]
                        ns_t, b_ns = negsel[T["h"] % 2]
                        P.op("pe", lambda e: e.matmul(
                            psZ[:], lhsT=cst[:, C_EN + n * 128:C_EN + (n + 1) * 128],
                            rhs=ns_t[:, qt * TT:(qt + 1) * TT], start=False, stop=(not dg)),
                            reads=[b_cst, b_ns], writes=[b_psZ], pe_accum=True)
                    if dg:
                        d = T["d"]
                        P.op("pe", lambda e: e.matmul(psZ[:], lhsT=cst[:, C_ID:C_ID + 128], rhs=mask[:, d, :],
                                                      start=False, stop=True),
                             reads=[b_cst, b_mask], writes=[b_psZ], pe_accum=True)
                    P.op("act", lambda e: e.activation(out=ww[:], in_=psZ[:], func=AF.Exp, scale=sm_scale),
                         reads=[b_psZ], writes=[b_ww])

                def s2(T):
                    t, i, nkb, ch = T["t"], T["i"], T["nkb"], T["chain"]
                    kblk, vblk, qtile, b_k, b_q = views(T)
                    ww, b_ww = wr[t % 4]
                    psO, b_psO = banks[4 + ch % 2]
                    psD, b_psD = banks[6 + ch % 2]
                    P.op("pe", lambda e: e.matmul(psO[0:64, :], lhsT=vblk, rhs=ww[:], start=(i == 0), stop=(i == nkb - 1)),
                         reads=[b_vsb, b_ww], writes=[b_psO], pe_accum=(i > 0))
                    P.op("pe", lambda e: e.matmul(psD[0:64, :], lhsT=cst[:, C_ONE:C_ONE + 64], rhs=ww[:],
                                                  start=(i == 0), stop=(i == nkb - 1)),
                         reads=[b_cst, b_ww], writes=[b_psD], pe_accum=(i > 0))
                    if i == nkb - 1:
                        epilogue(T)
                stages = [(s1, 0), (s2, 2)]
            maxlag = max(l for _, l in stages)
            for step in range(NTL + maxlag):
                for fn, lag in stages:
                    t = step - lag
                    if 0 <= t < NTL:
                        fn(tiles[t])
            P.barrier()


def lay_rows(w):
    k = w.shape[0] // 128
    return np.ascontiguousarray(w.reshape(k, 128, w.shape[1]).transpose(1, 0, 2))


def own_tokens(xb, role):
    S = xb.shape[0]
    return np.ascontiguousarray(xb.reshape(S // 256, 2, 128, -1)[:, role].reshape(S // 2, -1))


def merge_tokens(y0, y1):
    n = y0.shape[0] // 128
    out = np.stack([y0.reshape(n, 128, -1), y1.reshape(n, 128, -1)], axis=1)
    return out.reshape(2 * y0.shape[0], -1)


NT_CORE = 2048
KINDS = ["sb", "moba", "mla", "sb"]


def build_full(NT=NT_CORE, depth=DEPTH):
    m = MKA(NT)
    m.din("xT", [NCH, 128, NT])
    m.din("gains", [128, 3 * depth + 1, NCH])
    m.din("consts", [128, CW]); m.din("ropec", [128, 8]); m.din("masks", [2, 128, 8, 512]); m.din("pos", [1, NT])
    for i in range(depth):
        for f in (1, 2):
            m.din("wgu%d_%d" % (f, i), [NFG, 128, 2, NCH, FG * 128])
            m.din("wd%d_%d" % (f, i), [NCH, 128, NFC, 128])
        kind = KINDS[i]
        if kind == "mla":
            m.din("w_in_%d" % i, [128, NCH, 672]); m.din("w_uq_%d" % i, [128, 3, 1536])
            m.din("w_uk_%d" % i, [128, 2, 1024]); m.din("w_uv_%d" % i, [128, 2, 1024]); m.din("mlag_%d" % i, [128, 5])
        else:
            m.din("wqkv_%d" % i, [128, NCH, 3 * D])
        m.din("wo_%d" % i, [128, NCH, D])
    m.dout("yT", [NCH, 128, NT])
    m.dscr("hA", [NCH, 128, NT], F32); m.dscr("hB", [NCH, 128, NT], F32); m.dscr("hnmix", [NCH, 128, NT], BF16)
    m.alloc_scratch()
    with ExitStack() as cst:
        m.setup_consts(cst, m.dram["gains"], 3 * depth + 1)
        m.setup_attn_consts(cst)
        cur = "xT"
        for i in range(depth):
            kind = KINDS[i]
            m.pass_ffn(cur, "hA", "wgu1_%d" % i, "wd1_%d" % i, 3 * i, epi=("norm", 3 * i + 1, "hnmix"))
            if kind == "mla":
                m.pass_proj_mla("w_in_%d" % i, "w_uq_%d" % i, "w_uk_%d" % i, "w_uv_%d" % i, "mlag_%d" % i, "hnmix")
            else:
                m.pass_proj_qkv(kind, "wqkv_%d" % i, "hnmix")
            m.pass_attn(kind)
            m.pass_o("wo_%d" % i, "hA", "hB")
            last = i == depth - 1
            m.pass_ffn("hB", "hA", "wgu2_%d" % i, "wd2_%d" % i, 3 * i + 2,
                       epi=(("final", 3 * depth, "yT") if last else None))
            cur = "hA"
        nc = m.finish(["yT"])
    return m, nc


def host_inputs(inp, NT=NT_CORE, depth=DEPTH):
    f32 = lambda a: np.asarray(a, dtype=np.float32)
    shared = {}
    gs = []
    for i in range(depth):
        gs += [f32(inp["norm_ffn1"])[i], f32(inp["norm_mix"])[i], f32(inp["norm_ffn2"])[i]]
    gs.append(f32(inp["final_norm"]))
    shared["gains"] = lay_gain(gs)
    shared["consts"] = host_consts()
    shared["ropec"] = host_ropec()
    cnt = {"sb": 0, "moba": 0, "mla": 0}
    for i in range(depth):
        shared["wgu1_%d" % i] = lay_wgu(f32(inp["ffn1_w_gate_up"])[i])
        shared["wd1_%d" % i] = lay_wd(f32(inp["ffn1_w_down"])[i])
        shared["wgu2_%d" % i] = lay_wgu(f32(inp["ffn2_w_gate_up"])[i])
        shared["wd2_%d" % i] = lay_wd(f32(inp["ffn2_w_down"])[i])
        kind = KINDS[i]
        j = cnt[kind]
        cnt[kind] += 1
        if kind == "mla":
            shared["w_in_%d" % i] = lay_rows(f32(inp["mla_w_in"])[j])
            shared["w_uq_%d" % i] = lay_rows(f32(inp["mla_w_uq"])[j])
            wukv = f32(inp["mla_w_ukv"])[j].reshape(256, 16, 2, 64)
            shared["w_uk_%d" % i] = lay_rows(np.ascontiguousarray(wukv[:, :, 0].reshape(256, 1024)))
            shared["w_uv_%d" % i] = lay_rows(np.ascontiguousarray(wukv[:, :, 1].reshape(256, 1024)))
            shared["mlag_%d" % i] = np.ascontiguousarray(np.concatenate(
                [f32(inp["mla_q_norm"])[j].reshape(3, 128), f32(inp["mla_kv_norm"])[j].reshape(2, 128)], 0).T)
            shared["wo_%d" % i] = lay_rows(f32(inp["mla_w_o"])[j])
        elif kind == "moba":
            shared["wqkv_%d" % i] = lay_rows(f32(inp["moba_w_qkv"])[j])
            shared["wo_%d" % i] = lay_rows(f32(inp["moba_w_o"])[j])
        else:
            shared["wqkv_%d" % i] = lay_rows(f32(inp["sb_w_qkv"])[j])
            shared["wo_%d" % i] = lay_rows(f32(inp["sb_w_o"])[j])
    x = f32(inp["x"])
    in_maps = []
    for c in range(8):
        b, r = c // 2, c % 2
        d = dict(shared)
        d["xT"] = lay_xT(own_tokens(x[b], r))
        d["masks"] = host_masks(r)
        d["pos"] = host_pos(r, NT)
        in_maps.append(d)
    return in_maps


_CACHE = {}


def kernel(**inputs):
    if "nc" not in _CACHE:
        _CACHE["nc"] = build_full()[1]
    nc = _CACHE["nc"]
    in_maps = host_inputs(inputs)
    res = run_bass_kernel_spmd(nc, in_maps, core_ids=list(range(8)))
    B = 4
    out = np.empty((B, 2 * NT_CORE, D), np.float32)
    for b in range(B):
        out[b] = merge_tokens(unlay_xT(res.results[2 * b]["yT"]), unlay_xT(res.results[2 * b + 1]["yT"]))
    return out
```

```python
import numpy as np
import concourse.bass as bass
import concourse.mybir as mybir

F32 = mybir.dt.float32
BF16 = mybir.dt.bfloat16
AF = mybir.ActivationFunctionType
ALU = mybir.AluOpType
AX = mybir.AxisListType

ENGS = ("pe", "act", "dve", "pool", "sp")


class Buf:
    __slots__ = ("name", "writers", "readers", "sem", "semval", "excl", "sw", "preaders")

    def __init__(self, name, excl=False):
        self.name = name
        self.excl = excl
        self.writers = {}
        self.readers = {}
        self.preaders = {}
        self.sem = None
        self.semval = 0
        self.sw = None


class Prog:
    def __init__(self, nc):
        self.nc = nc
        self.streams = {e: [] for e in ENGS}
        self.cnt = {e: 0 for e in ENGS}
        self.sem = {}
        self.seen = {e: {} for e in ENGS}
        self.snaps = {}
        self._ctx = []
        for e in ENGS:
            cm = nc.semaphore("s_" + e)
            self.sem[e] = cm.__enter__()
            self._ctx.append(cm)
        self.nwaits = 0
        self.nops = 0
        self.dmabufs = []
        self.free = {"hw": [], "sw": [], "cc": []}
        self._issw = {}

    def new_sem(self, name):
        self._semuid = getattr(self, "_semuid", 0) + 1
        cm = self.nc.semaphore("%s_%d" % (name, self._semuid))
        s = cm.__enter__()
        self._ctx.append(cm)
        return s

    def _need(self, eng, tok, waits):
        sem, val, _ = tok
        if self.seen[eng].get(sem.num, 0) >= val:
            return
        waits[sem.num] = (sem, max(val, waits.get(sem.num, (sem, 0))[1]))

    def _deps(self, eng, reads, writes, pe_accum=False, nowaw=False):
        waits = {}
        for b in reads:
            for k, tok in b.writers.items():
                self._need(eng, tok, waits)
            if b.excl:
                for k, tok in b.readers.items():
                    if k != eng:
                        self._need(eng, tok, waits)
        for b in writes:
            for k, tok in b.readers.items():
                self._need(eng, tok, waits)
            for k, tok in b.preaders.items():
                self._need(eng, tok, waits)
            for k, tok in b.writers.items():
                if pe_accum and k == "pe":
                    continue
                if nowaw and isinstance(k, tuple):
                    continue
                self._need(eng, tok, waits)
        sn = self.seen[eng]
        for num, (sem, val) in waits.items():
            sn[num] = max(sn.get(num, 0), val)
            snap = self.snaps.get((num, val))
            if snap:
                for k2, v2 in snap.items():
                    if sn.get(k2, 0) < v2:
                        sn[k2] = v2
        self.nwaits += len(waits)
        return list(waits.values())

    def _post(self, key, tok, reads, writes):
        for b in reads:
            old = b.readers.get(key)
            if old is None or old[1] < tok[1] or old[0].num != tok[0].num:
                b.readers[key] = tok
        for b in writes:
            if b.readers:
                b.preaders = b.readers
                b.readers = {}
                b.writers = {key: tok}
            else:
                b.writers[key] = tok

    def op(self, eng, fn, reads=(), writes=(), pe_accum=False):
        waits = self._deps(eng, reads, writes, pe_accum=pe_accum)
        self.cnt[eng] += 1
        n = self.cnt[eng]
        sem = self.sem[eng]
        self.snaps[(sem.num, n)] = dict(self.seen[eng])
        tok = (sem, n, None)
        self._post(eng, tok, reads, writes)
        self.streams[eng].append((waits, fn, (sem, 1)))
        self.nops += 1

    def dma(self, eng, fn, reads=(), writes=(), sb=None, inc=16, cls=None):
        dst = sb if sb is not None else writes[0]
        if cls is None:
            cls = "sw" if eng == "pool" else "hw"
        if cls == "sw":
            if dst.sw is None:
                dst.sw = Buf(dst.name + "_sw")
            dst = dst.sw
        if dst.sem is None:
            pool = self.free[cls]
            if pool:
                dst.sem, dst.semval = pool.pop()
            else:
                dst.sem = self.new_sem("d_" + dst.name)
                dst.semval = 0
            self.dmabufs.append(dst)
            self._issw[id(dst)] = cls
        waits = self._deps(eng, reads, writes, nowaw=True)
        dst.semval += inc
        tok = (dst.sem, dst.semval, None)
        self.snaps[(dst.sem.num, dst.semval)] = dict(self.seen[eng])
        self._post(("dma", dst.sem.num), tok, reads, writes)
        self.streams[eng].append((waits, fn, (dst.sem, inc)))
        self.nops += 1

    def barrier(self):
        for eng in ENGS:
            waits = {}
            for e2 in ENGS:
                if e2 != eng and self.cnt[e2] > 0:
                    self._need(eng, (self.sem[e2], self.cnt[e2], None), waits)
            if self.cnt[eng] > 0:
                self._need(eng, (self.sem[eng], self.cnt[eng], None), waits)
            for b in self.dmabufs:
                if b.semval > 0:
                    self._need(eng, (b.sem, b.semval, None), waits)
            sn = self.seen[eng]
            for num, (sem, val) in waits.items():
                sn[num] = max(sn.get(num, 0), val)
            self.nwaits += len(waits)
            self.streams[eng].append((list(waits.values()), None, None))
        self.recycle()

    def recycle(self):
        for b in self.dmabufs:
            self.free[self._issw[id(b)]].append((b.sem, b.semval))
            b.sem = None
            b.semval = 0
        self.dmabufs = []
        self._issw = {}

    def wait_all(self, eng, bufs):
        waits = self._deps(eng, bufs, ())
        self.streams[eng].append((waits, None, None))

    def emit(self):
        nc = self.nc
        engmap = {"pe": "tensor", "act": "scalar", "dve": "vector", "pool": "gpsimd", "sp": "sync"}
        with nc.Block() as block:
            for e in ENGS:
                stream = self.streams[e]

                def body(eng, stream=stream):
                    for waits, fn, inc in stream:
                        for sem, val in waits:
                            eng.wait_ge(sem, val)
                        if fn is not None:
                            ins = fn(eng)
                            ins.then_inc(inc[0], inc[1])
                getattr(block, engmap[e])(body)

    def close(self):
        for cm in reversed(self._ctx):
            cm.__exit__(None, None, None)
from contextlib import ExitStack
from concourse.bass_utils import run_bass_kernel_spmd

D = 1024
F = 2816
NCH = D // 128
NFC = F // 128
FG = 2
NFG = NFC // FG
H = 16
HD = 64
EPS = 1e-6
DEPTH = 4
ST = 1024
TT = 512


class MK:
    def __init__(self, NT, n_layers=DEPTH, cfg=None):
        self.NT = NT
        self.cfg = cfg or {}
        nc = self.nc = bass.Bass("TRN2", target_bir_lowering=False)
        self.P = Prog(nc)
        self.dram = {}
        self.dbuf = {}

    def din(self, name, shape, dt=F32):
        t = self.nc.dram_tensor(name, list(shape), dt, kind="ExternalInput")
        self.dram[name] = t
        self.dbuf[name] = Buf(name)
        return t

    def dout(self, name, shape, dt=F32):
        t = self.nc.dram_tensor(name, list(shape), dt, kind="ExternalOutput")
        self.dram[name] = t
        self.dbuf[name] = Buf(name)
        return t

    def dscr(self, name, shape, dt):
        t = self.nc.dram_tensor(name, list(shape), dt)
        self.dram[name] = t
        self.dbuf[name] = Buf(name)
        return t

    _uid = 0

    def uname(self, name):
        MK._uid += 1
        return "%s_%d" % (name, MK._uid)

    def sb(self, stack, name, shape, dt):
        name = self.uname(name)
        t = stack.enter_context(self.nc.sbuf_tensor(name, list(shape), dt))
        return t, Buf(name)

    def psum_banks(self, stack):
        banks = []
        for i in range(8):
            nm = self.uname("psb%d" % i)
            t = stack.enter_context(self.nc.psum_tensor(nm, [128, 512], F32))
            banks.append((t, Buf(nm, excl=True)))
        return banks

    def setup_consts(self, stack, gains_dram, ngain):
        P = self.P
        self.ones_mean, self.b_ones_mean = self.sb(stack, "ones_mean", [128, 128], BF16)
        P.op("pool", lambda e: e.memset(self.ones_mean[:], 1.0 / D), writes=[self.b_ones_mean])
        self.epsc, self.b_epsc = self.sb(stack, "epsc", [128, 1], F32)
        P.op("pool", lambda e: e.memset(self.epsc[:], EPS), writes=[self.b_epsc])
        self.gains, self.b_gains = self.sb(stack, "gains_sb", [128, ngain, NCH], F32)
        P.dma("sp", lambda e: e.dma_start(out=self.gains[:], in_=gains_dram.ap()),
              reads=[self.dbuf[gains_dram.name]], writes=[self.b_gains])

    def rmsnorm_tile(self, ht, b_ht, ntok, gi, out, b_out, sq, b_sq, rstd, b_rstd, bank):
        P = self.P
        ps, b_ps = bank
        for t0 in range(0, ntok, TT):
            P.op("act", lambda e, t0=t0: e.activation(out=sq[:, :, :], in_=ht[:, :, t0:t0 + TT], func=AF.Square),
                 reads=[b_ht], writes=[b_sq])
            for c in range(NCH):
                P.op("pe", lambda e, c=c: e.matmul(ps[:], lhsT=self.ones_mean[:], rhs=sq[:, c, :],
                                                   start=(c == 0), stop=(c == NCH - 1)),
                     reads=[self.b_ones_mean, b_sq], writes=[b_ps], pe_accum=(c > 0))
            P.op("act", lambda e: e.activation(out=rstd[:], in_=ps[:], func=AF.Sqrt, bias=self.epsc[:, 0:1]),
                 reads=[b_ps, self.b_epsc], writes=[b_rstd])
            P.op("dve", lambda e: e.reciprocal(out=rstd[:], in_=rstd[:]),
                 reads=[b_rstd], writes=[b_rstd])
            for c in range(NCH):
                P.op("dve", lambda e, c=c, t0=t0: e.scalar_tensor_tensor(
                    out=out[:, c, t0:t0 + TT], in0=ht[:, c, t0:t0 + TT], scalar=self.gains[:, gi, c:c + 1],
                    in1=rstd[:], op0=ALU.mult, op1=ALU.mult),
                     reads=[b_ht, b_rstd, self.b_gains], writes=[b_out])

    def pass_ffn(self, h_src, h_dst, wgu, wd, gi, epi=None):
        P, nc, NT = self.P, self.nc, self.NT
        ST = min(1024, NT)
        with ExitStack() as stack:
            banks = self.psum_banks(stack)
            ht = [self.sb(stack, "ht%d" % i, [128, NCH, ST], F32) for i in range(1)]
            hn, b_hn = self.sb(stack, "hn", [128, NCH, ST], BF16)
            act, b_act = self.sb(stack, "act", [128, NFC, ST], BF16)
            sq, b_sq = self.sb(stack, "sq", [128, NCH, TT], BF16)
            rstd, b_rstd = self.sb(stack, "rstd", [128, TT], F32)
            sil = [self.sb(stack, "sil%d" % i, [128, TT], F32) for i in range(2)]
            wg_s = [self.sb(stack, "wgu%d" % i, [128, 2, NCH, FG * 128], BF16) for i in range(2)]
            wd_s = [self.sb(stack, "wd%d" % i, [128, NFC, 128], BF16) for i in range(2)]
            if epi is not None:
                eo_dt = BF16 if epi[0] == "norm" else F32
                eo, b_eo = self.sb(stack, "eo", [128, NCH, ST], eo_dt)
            hsrc, hdst = self.dram[h_src], self.dram[h_dst]
            wgu_t, wd_t = self.dram[wgu], self.dram[wd]
            nst = NT // ST
            wi = 0
            di = 0
            for st in range(nst):
                t_lo = st * ST
                h_t, b_h = ht[0]
                P.dma("sp", lambda e, t_lo=t_lo: e.dma_start(
                    out=h_t[:], in_=hsrc.ap()[:, :, t_lo:t_lo + ST].rearrange("c p t -> p c t")),
                    reads=[self.dbuf[h_src]], writes=[b_h])
                self.rmsnorm_tile(h_t, b_h, ST, gi, hn, b_hn, sq, b_sq, rstd, b_rstd, banks[6])
                for fg in range(NFG):
                    w_t, b_w = wg_s[wi % 2]
                    wi += 1
                    if not (self.cfg.get("skipw") and wi > 2):
                        P.dma("pool", lambda e, fg=fg, w_t=w_t: e.dma_start(out=w_t[:], in_=wgu_t.ap()[fg]),
                              reads=[self.dbuf[wgu]], writes=[b_w])
                    for f2 in range(FG):
                        f = fg * FG + f2
                        for ti in range(ST // TT):
                            t0 = ti * TT
                            k = (f * 2 + ti) % 2
                            pg, b_pg = banks[k]
                            pu, b_pu = banks[2 + k]
                            for c in range(NCH):
                                P.op("pe", lambda e, c=c, w_t=w_t, f2=f2, t0=t0, pg=pg: e.matmul(
                                    pg[:], lhsT=w_t[:, 0, c, f2 * 128:(f2 + 1) * 128], rhs=hn[:, c, t0:t0 + TT],
                                    start=(c == 0), stop=(c == NCH - 1)),
                                    reads=[b_w, b_hn], writes=[b_pg], pe_accum=(c > 0))
                            for c in range(NCH):
                                P.op("pe", lambda e, c=c, w_t=w_t, f2=f2, t0=t0, pu=pu: e.matmul(
                                    pu[:], lhsT=w_t[:, 1, c, f2 * 128:(f2 + 1) * 128], rhs=hn[:, c, t0:t0 + TT],
                                    start=(c == 0), stop=(c == NCH - 1)),
                                    reads=[b_w, b_hn], writes=[b_pu], pe_accum=(c > 0))
                            s_t, b_s = sil[k]
                            P.op("act", lambda e, s_t=s_t, pg=pg: e.activation(out=s_t[:], in_=pg[:], func=AF.Silu),
                                 reads=[b_pg], writes=[b_s])
                            P.op("dve", lambda e, s_t=s_t, pu=pu, f=f, t0=t0: e.tensor_tensor(
                                out=act[:, f, t0:t0 + TT], in0=pu[:], in1=s_t[:], op=ALU.mult),
                                reads=[b_pu, b_s], writes=[b_act])
                for dc in range(NCH):
                    w_t, b_w = wd_s[di % 2]
                    di += 1
                    if not (self.cfg.get("skipw") and di > 2):
                        P.dma("pool", lambda e, dc=dc, w_t=w_t: e.dma_start(out=w_t[:], in_=wd_t.ap()[dc]),
                              reads=[self.dbuf[wd]], writes=[b_w])
                    for ti in range(ST // TT):
                        t0 = ti * TT
                        po, b_po = banks[4 + (dc * 2 + ti) % 2]
                        for f in range(NFC):
                            P.op("pe", lambda e, f=f, w_t=w_t, t0=t0, po=po: e.matmul(
                                po[:], lhsT=w_t[:, f, :], rhs=act[:, f, t0:t0 + TT],
                                start=(f == 0), stop=(f == NFC - 1)),
                                reads=[b_w, b_act], writes=[b_po], pe_accum=(f > 0))
                        P.op("dve", lambda e, dc=dc, t0=t0, po=po: e.scalar_tensor_tensor(
                            out=h_t[:, dc, t0:t0 + TT], in0=po[:], scalar=0.5, in1=h_t[:, dc, t0:t0 + TT],
                            op0=ALU.mult, op1=ALU.add),
                            reads=[b_po, b_h], writes=[b_h])
                if epi is None or epi[0] == "norm":
                    P.dma("sp", lambda e, t_lo=t_lo: e.dma_start(
                        out=hdst.ap()[:, :, t_lo:t_lo + ST].rearrange("c p t -> p c t"), in_=h_t[:]),
                        reads=[b_h], writes=[self.dbuf[h_dst]], sb=b_h)
                if epi is not None:
                    self.rmsnorm_tile(h_t, b_h, ST, epi[1], eo, b_eo, sq, b_sq, rstd, b_rstd, banks[6])
                    dst = self.dram[epi[2]]
                    P.dma("sp", lambda e, t_lo=t_lo, dst=dst: e.dma_start(
                        out=dst.ap()[:, :, t_lo:t_lo + ST].rearrange("c p t -> p c t"), in_=eo[:]),
                        reads=[b_eo], writes=[self.dbuf[epi[2]]], sb=b_eo)
            P.barrier()

    def finish(self, out_names):
        P = self.P
        P.barrier()
        P.emit()
        P.close()
        return self.nc


def lay_wgu(w):
    g = w.reshape(NCH, 128, 2, NFG, FG * 128)
    return np.ascontiguousarray(g.transpose(3, 1, 2, 0, 4))


def lay_wd(w):
    g = w.reshape(NFC, 128, NCH, 128)
    return np.ascontiguousarray(g.transpose(2, 1, 0, 3))


def lay_gain(gs):
    a = np.stack(gs, 0).reshape(len(gs), NCH, 128)
    return np.ascontiguousarray(a.transpose(2, 0, 1))


def lay_xT(x):
    return np.ascontiguousarray(x.T.reshape(NCH, 128, x.shape[0]))


def unlay_xT(y):
    return np.ascontiguousarray(y.reshape(D, y.shape[-1]).T)


BIG = 30000.0
NEG = -1e30
THETA = 500000.0
C_NTRI = 0
C_ID = 128
C_PSW = 256
C_SEL = 640
C_ONE = 768
C_EN = 896
CW = 896 + 16 * 128


def host_consts():
    c = np.zeros((128, CW), np.float32)
    s = np.arange(128)[:, None]
    j = np.arange(128)[None, :]
    c[:, C_NTRI:C_NTRI + 128] = -(s >= j).astype(np.float32)
    c[:, C_ID:C_ID + 128] = (s == j)
    pm = np.zeros((128, 128), np.float32)
    for p in range(128):
        d = p % 64
        if d < 8:
            pm[p, p + 8] = 1
        elif d < 16:
            pm[p, p - 8] = 1
    c[:, C_PSW:C_PSW + 128] = pm
    pq = np.zeros((128, 128), np.float32)
    for p in range(64, 96):
        dd = p - 64
        pq[p, p + 16 if dd < 16 else p - 16] = 1
    c[:, C_PSW + 128:C_PSW + 256] = pq
    pk = np.zeros((128, 128), np.float32)
    for p in range(32):
        pk[p, p + 16 if p < 16 else p - 16] = 1
    c[:, C_PSW + 256:C_PSW + 384] = pk
    c[0, C_SEL:C_SEL + 128] = -1
    c[32, C_SEL:C_SEL + 128] = -1
    c[:, C_ONE:C_ONE + 128] = 1
    for n in range(16):
        c[n, C_EN + n * 128:C_EN + (n + 1) * 128] = 1
    return c


def host_ropec():
    r = np.zeros((128, 8), np.float32)
    for p in range(128):
        d = p % 64
        if d < 16:
            i = d % 8
            r[p, 0] = THETA ** (-(2.0 * i) / 16.0) / (2 * np.pi)
            r[p, 1] = -1.0 if d < 8 else 1.0
        if 64 <= p < 96:
            dd = p - 64
            i = dd % 16
            r[p, 2] = THETA ** (-(2.0 * i) / 32.0) / (2 * np.pi)
            r[p, 3] = -1.0 if dd < 16 else 1.0
        if p < 32:
            i = p % 16
            r[p, 4] = THETA ** (-(2.0 * i) / 32.0) / (2 * np.pi)
            r[p, 5] = -1.0 if p < 16 else 1.0
    r[:, 6] = -np.pi
    return r


def host_masks(role, NQS=4):
    m = np.zeros((2, 128, 8, 512), np.float32)
    k = np.arange(128)[:, None]
    q = np.arange(128)[None, :]
    for d in range(8):
        for s in range(NQS):
            qb = 2 * s + role
            if d < qb:
                m[:, :, d, s * 128:(s + 1) * 128] = 1
            elif d == qb:
                m[0, :, d, s * 128:(s + 1) * 128] = (k < q)
                m[1, :, d, s * 128:(s + 1) * 128] = (k <= q)
    return m


def host_pos(role, NT):
    j = np.arange(NT // 128)[:, None]
    ql = np.arange(128)[None, :]
    return ((2 * j + role) * 128 + ql).reshape(1, NT).astype(np.float32)


class MKA(MK):
    HPC = {64: 8, 96: 4}

    def alloc_scratch(self, kds=(64, 96)):
        NT = self.NT
        self.VR = min(1024, NT)
        for KD in kds:
            self.dscr("qT%d" % KD, [16, KD, NT], BF16)
            hpc = self.HPC[KD]
            for c in range(16 // hpc):
                self.dscr("kT%d_own_c%d" % (KD, c), [hpc * KD, NT], BF16)
                self.dscr("kT%d_all_c%d" % (KD, c), [2 * hpc * KD, NT], BF16)
        for c in range(NT // self.VR):
            self.dscr("v_own_c%d" % c, [self.VR, D], BF16)
            self.dscr("v_all_c%d" % c, [2 * self.VR, D], BF16)
        self.dscr("aT", [16, 64, NT], BF16)

    def kown(self, KD, h):
        hpc = self.HPC[KD]
        return "kT%d_own_c%d" % (KD, h // hpc), (h % hpc) * KD

    def kall(self, KD, h, r):
        hpc = self.HPC[KD]
        return "kT%d_all_c%d" % (KD, h // hpc), r * hpc * KD + (h % hpc) * KD

    def vown(self, row):
        return "v_own_c%d" % (row // self.VR), row % self.VR

    def vall(self, r, row):
        return "v_all_c%d" % (row // self.VR), r * self.VR + row % self.VR

    def exchange_all(self, KD):
        for c in range(16 // self.HPC[KD]):
            self.exchange("kT%d_own_c%d" % (KD, c), "kT%d_all_c%d" % (KD, c))
        for c in range(self.NT // self.VR):
            self.exchange("v_own_c%d" % c, "v_all_c%d" % c)

    def setup_attn_consts(self, stack):
        P = self.P
        self.cst, self.b_cst = self.sb(stack, "cst", [128, CW], BF16)
        P.dma("pool", lambda e: e.dma_start(out=self.cst[:], in_=self.dram["consts"].ap()),
              reads=[self.dbuf["consts"]], writes=[self.b_cst])
        self.ropec, self.b_ropec = self.sb(stack, "ropec_sb", [128, 8], F32)
        P.dma("sp", lambda e: e.dma_start(out=self.ropec[:], in_=self.dram["ropec"].ap()),
              reads=[self.dbuf["ropec"]], writes=[self.b_ropec])

    def rope_tables(self, stack, col, nrows):
        P, NT = self.P, self.NT
        posb, b_posb = self.sb(stack, "posb", [128, NT], F32)
        P.dma("sp", lambda e: e.dma_start(out=posb[:], in_=self.dram["pos"].ap().partition_broadcast(128)),
              reads=[self.dbuf["pos"]], writes=[b_posb])
        C, b_C = self.sb(stack, "ropeC", [128, NT], F32)
        S, b_S = self.sb(stack, "ropeS", [128, NT], F32)
        tmp, b_tmp = self.sb(stack, "ropeT", [128, NT], F32)
        rc = self.ropec
        R = slice(0, nrows)
        two_pi = float(2 * np.pi)
        MAGIC = 12582912.0
        rnd, b_rnd = self.sb(stack, "ropeR", [128, NT], F32)
        for (dst, b_dst, shift) in ((S, b_S, 0.0), (C, b_C, 0.25)):
            P.op("dve", lambda e, shift=shift: e.tensor_scalar(out=tmp[R, :], in0=posb[R, :], scalar1=rc[R, col:col + 1],
                                                               scalar2=shift, op0=ALU.mult, op1=ALU.add),
                 reads=[b_posb, self.b_ropec], writes=[b_tmp])
            P.op("dve", lambda e: e.tensor_scalar(out=rnd[R, :], in0=tmp[R, :], scalar1=MAGIC, scalar2=None,
                                                  op0=ALU.add), reads=[b_tmp], writes=[b_rnd])
            P.op("dve", lambda e: e.tensor_scalar(out=rnd[R, :], in0=rnd[R, :], scalar1=-MAGIC, scalar2=None,
                                                  op0=ALU.add), reads=[b_rnd], writes=[b_rnd])
            P.op("dve", lambda e: e.tensor_tensor(out=tmp[R, :], in0=tmp[R, :], in1=rnd[R, :], op=ALU.subtract),
                 reads=[b_tmp, b_rnd], writes=[b_tmp])
            P.op("act", lambda e, dst=dst: e.activation(out=dst[R, :], in_=tmp[R, :], func=AF.Sin, scale=two_pi),
                 reads=[b_tmp], writes=[b_dst])
        P.op("dve", lambda e: e.tensor_scalar(out=S[R, :], in0=S[R, :], scalar1=rc[R, col + 1:col + 2], scalar2=None,
                                              op0=ALU.mult),
             reads=[b_S, self.b_ropec], writes=[b_S])
        return (C, b_C), (S, b_S)

    def apply_rope(self, ps, b_ps, ps2, b_ps2, nrows, pswcol, C, b_C, S, b_S, t0, scale, out, b_out,
                   qb, b_qb, t1, b_t1, t2, b_t2):
        P = self.P
        R = slice(0, nrows)
        P.op("act", lambda e: e.copy(out=qb[R, :], in_=ps[R, :]), reads=[b_ps], writes=[b_qb])
        P.op("pe", lambda e: e.matmul(ps2[R, :], lhsT=self.cst[R, pswcol:pswcol + nrows], rhs=qb[R, :],
                                      start=True, stop=True),
             reads=[self.b_cst, b_qb], writes=[b_ps2])
        P.op("dve", lambda e: e.scalar_tensor_tensor(out=t1[R, :], in0=ps[R, :], scalar=scale,
                                                     in1=C[R, t0:t0 + TT], op0=ALU.mult, op1=ALU.mult),
             reads=[b_ps, b_C], writes=[b_t1])
        P.op("dve", lambda e: e.scalar_tensor_tensor(out=t2[R, :], in0=ps2[R, :], scalar=scale,
                                                     in1=S[R, t0:t0 + TT], op0=ALU.mult, op1=ALU.mult),
             reads=[b_ps2, b_S], writes=[b_t2])
        P.op("pool", lambda e: e.tensor_tensor(out=out[R, :], in0=t1[R, :], in1=t2[R, :], op=ALU.add),
             reads=[b_t1, b_t2], writes=[b_out])

    def pass_proj_qkv(self, kind, wname, hn_name):
        P, NT = self.P, self.NT
        qT = self.dram["qT64"]
        with ExitStack() as stack:
            banks = self.psum_banks(stack)
            w, b_w = self.sb(stack, "wqkv", [128, NCH, 3 * D], BF16)
            wd = self.dram[wname]
            for c in range(NCH):
                P.dma("pool", lambda e, c=c: e.dma_start(out=w[:, c, :], in_=wd.ap()[:, c, :]),
                      reads=[self.dbuf[wname]], writes=[b_w])
            hn = [self.sb(stack, "phn%d" % i, [128, NCH, TT], BF16) for i in range(2)]
            osb = [self.sb(stack, "posb%d" % i, [128, TT], BF16) for i in range(4)]
            vsb = [self.sb(stack, "pvsb%d" % i, [128, D], BF16) for i in range(2)]
            rope = kind == "moba"
            if rope:
                (C, b_C), (S, b_S) = self.rope_tables(stack, 0, 128)
                qb = [self.sb(stack, "rqb%d" % i, [128, TT], BF16) for i in range(2)]
                t1 = [self.sb(stack, "rt1%d" % i, [128, TT], F32) for i in range(2)]
                t2 = [self.sb(stack, "rt2%d" % i, [128, TT], F32) for i in range(2)]
            hnd = self.dram[hn_name]
            oi = 0
            vi = 0
            pi = 0
            for tt in range(NT // TT):
                t0 = tt * TT
                h_t, b_h = hn[tt % 2]
                P.dma("sp", lambda e, t0=t0, h_t=h_t: e.dma_start(
                    out=h_t[:], in_=hnd.ap()[:, :, t0:t0 + TT].rearrange("c p t -> p c t")),
                    reads=[self.dbuf[hn_name]], writes=[b_h])
                for which in range(2):
                    for pr in range(8):
                        ps, b_ps = banks[pi % 2]
                        ps2, b_ps2 = banks[2 + pi % 2]
                        pi += 1
                        col = which * D + pr * 128
                        for c in range(NCH):
                            P.op("pe", lambda e, c=c, col=col, ps=ps, h_t=h_t: e.matmul(
                                ps[:], lhsT=w[:, c, col:col + 128], rhs=h_t[:, c, :],
                                start=(c == 0), stop=(c == NCH - 1)),
                                reads=[b_w, b_h], writes=[b_ps], pe_accum=(c > 0))
                        o_t, b_o = osb[oi % 4]
                        oi += 1
                        scale = 0.125 if which == 0 else 1.0
                        if rope:
                            k2 = pi % 2
                            self.apply_rope(ps, b_ps, ps2, b_ps2, 128, C_PSW, C, b_C, S, b_S, t0, scale, o_t, b_o,
                                            qb[k2][0], qb[k2][1], t1[k2][0], t1[k2][1], t2[k2][0], t2[k2][1])
                        else:
                            P.op("act", lambda e, o_t=o_t, ps=ps, scale=scale: e.activation(
                                out=o_t[:], in_=ps[:], func=AF.Copy, scale=scale),
                                reads=[b_ps], writes=[b_o])
                        if which == 0:
                            dst_ap = qT.ap()[2 * pr:2 * pr + 2, :, t0:t0 + TT].rearrange("h d t -> (h d) t")
                            P.dma("sp", lambda e, o_t=o_t, dst_ap=dst_ap: e.dma_start(out=dst_ap, in_=o_t[:]),
                                  reads=[b_o], writes=[self.dbuf["qT64"]], sb=b_o)
                        else:
                            kn, kr0 = self.kown(64, 2 * pr)
                            dst_ap = self.dram[kn].ap()[kr0:kr0 + 128, t0:t0 + TT]
                            P.dma("sp", lambda e, o_t=o_t, dst_ap=dst_ap: e.dma_start(out=dst_ap, in_=o_t[:]),
                                  reads=[b_o], writes=[self.dbuf[kn]], sb=b_o)
                for tb in range(TT // 128):
                    v_t, b_v = vsb[vi % 2]
                    vi += 1
                    for half in range(2):
                        ps, b_ps = banks[4 + half]
                        for c in range(NCH):
                            P.op("pe", lambda e, c=c, half=half, tb=tb, ps=ps, h_t=h_t: e.matmul(
                                ps[:], lhsT=h_t[:, c, tb * 128:(tb + 1) * 128],
                                rhs=w[:, c, 2 * D + half * 512:2 * D + (half + 1) * 512],
                                start=(c == 0), stop=(c == NCH - 1)),
                                reads=[b_w, b_h], writes=[b_ps], pe_accum=(c > 0))
                        P.op("dve", lambda e, half=half, ps=ps, v_t=v_t: e.tensor_copy(
                            out=v_t[:, half * 512:(half + 1) * 512], in_=ps[:]),
                            reads=[b_ps], writes=[b_v])
                    vn, r0 = self.vown(t0 + tb * 128)
                    P.dma("sp", lambda e, v_t=v_t, r0=r0, vn=vn: e.dma_start(
                        out=self.dram[vn].ap()[r0:r0 + 128, :], in_=v_t[:]),
                        reads=[b_v], writes=[self.dbuf[vn]], sb=b_v)
            P.barrier()
        self.exchange_all(64)
        P.barrier()

    def pass_proj_mla(self, w_in, w_uq, w_uk, w_uv, gname, hn_name):
        P, NT = self.P, self.NT
        qT = self.dram["qT96"]
        b_qT = self.dbuf["qT96"]
        cst, b_cst = self.cst, self.b_cst
        with ExitStack() as stack:
            banks = self.psum_banks(stack)
            win, b_win = self.sb(stack, "win", [128, NCH, 672], BF16)
            wuq, b_wuq = self.sb(stack, "wuq", [128, 3, 1536], BF16)
            wuk, b_wuk = self.sb(stack, "wuk", [128, 2, 1024], BF16)
            wuv, b_wuv = self.sb(stack, "wuv", [128, 2, 1024], BF16)
            for (t, b, nm) in ((win, b_win, w_in), (wuq, b_wuq, w_uq), (wuk, b_wuk, w_uk), (wuv, b_wuv, w_uv)):
                P.dma("pool", lambda e, t=t, nm=nm: e.dma_start(out=t[:], in_=self.dram[nm].ap()),
                      reads=[self.dbuf[nm]], writes=[b])
            mg, b_mg = self.sb(stack, "mlag", [128, 5], F32)
            P.dma("sp", lambda e: e.dma_start(out=mg[:], in_=self.dram[gname].ap()),
                  reads=[self.dbuf[gname]], writes=[b_mg])
            (Cq, b_Cq), (Sq, b_Sq) = self.rope_tables(stack, 2, 96)
            (Ck, b_Ck), (Sk, b_Sk) = self.rope_tables(stack, 4, 32)
            hn = [self.sb(stack, "mhn%d" % i, [128, NCH, TT], BF16) for i in range(2)]
            cqf, b_cqf = self.sb(stack, "cqf", [128, 3, TT], F32)
            ckf, b_ckf = self.sb(stack, "ckf", [128, 2, TT], F32)
            sq, b_sq = self.sb(stack, "msq", [128, 3, TT], BF16)
            rstd, b_rstd = self.sb(stack, "mrstd", [128, TT], F32)
            cqn, b_cqn = self.sb(stack, "cqn", [128, 3, TT], BF16)
            ckn, b_ckn = self.sb(stack, "ckn", [128, 2, TT], BF16)
            osb = [self.sb(stack, "mosb%d" % i, [128, TT], BF16) for i in range(4)]
            vsb = [self.sb(stack, "mvsb%d" % i, [128, D], BF16) for i in range(2)]
            qb = [self.sb(stack, "mqb%d" % i, [128, TT], BF16) for i in range(2)]
            t1 = [self.sb(stack, "mt1%d" % i, [128, TT], F32) for i in range(2)]
            t2 = [self.sb(stack, "mt2%d" % i, [128, TT], F32) for i in range(2)]
            hnd = self.dram[hn_name]
            oi = 0
            vi = 0
            pi = 0
            for tt in range(NT // TT):
                t0 = tt * TT
                h_t, b_h = hn[tt % 2]
                P.dma("sp", lambda e, t0=t0, h_t=h_t: e.dma_start(
                    out=h_t[:], in_=hnd.ap()[:, :, t0:t0 + TT].rearrange("c p t -> p c t")),
                    reads=[self.dbuf[hn_name]], writes=[b_h])
                for m in range(5):
                    ps, b_ps = banks[pi % 2]
                    pi += 1
                    for c in range(NCH):
                        P.op("pe", lambda e, c=c, m=m, ps=ps, h_t=h_t: e.matmul(
                            ps[:], lhsT=win[:, c, m * 128:(m + 1) * 128], rhs=h_t[:, c, :],
                            start=(c == 0), stop=(c == NCH - 1)),
                            reads=[b_win, b_h], writes=[b_ps], pe_accum=(c > 0))
                    if m < 3:
                        P.op("act", lambda e, m=m, ps=ps: e.copy(out=cqf[:, m, :], in_=ps[:]),
                             reads=[b_ps], writes=[b_cqf])
                    else:
                        P.op("act", lambda e, m=m, ps=ps: e.copy(out=ckf[:, m - 3, :], in_=ps[:]),
                             reads=[b_ps], writes=[b_ckf])
                ps, b_ps = banks[pi % 2]
                ps2, b_ps2 = banks[2 + pi % 2]
                pi += 1
                for c in range(NCH):
                    P.op("pe", lambda e, c=c, ps=ps, h_t=h_t: e.matmul(
                        ps[0:32, :], lhsT=win[:, c, 640:672], rhs=h_t[:, c, :],
                        start=(c == 0), stop=(c == NCH - 1)),
                        reads=[b_win, b_h], writes=[b_ps], pe_accum=(c > 0))
                kr_t, b_kr = osb[oi % 4]
                oi += 1
                self.apply_rope(ps, b_ps, ps2, b_ps2, 32, C_PSW + 256, Ck, b_Ck, Sk, b_Sk, t0, 1.0, kr_t, b_kr,
                                qb[0][0], qb[0][1], t1[0][0], t1[0][1], t2[0][0], t2[0][1])
                for h in range(H):
                    kn, kr0 = self.kown(96, h)
                    P.dma("sp", lambda e, kn=kn, kr0=kr0, kr_t=kr_t, t0=t0: e.dma_start(
                        out=self.dram[kn].ap()[kr0 + 64:kr0 + 96, t0:t0 + TT], in_=kr_t[0:32, :]),
                        reads=[b_kr], writes=[self.dbuf[kn]], sb=b_kr)
                for (src, b_src, nch, gcol, dst, b_dst, inv) in ((cqf, b_cqf, 3, 0, cqn, b_cqn, 1.0 / 384.0),
                                                                 (ckf, b_ckf, 2, 3, ckn, b_ckn, 1.0 / 256.0)):
                    psn, b_psn = banks[6]
                    P.op("act", lambda e, src=src, nch=nch: e.activation(out=sq[:, 0:nch, :], in_=src[:, 0:nch, :],
                                                                         func=AF.Square),
                         reads=[b_src], writes=[b_sq])
                    for m in range(nch):
                        P.op("pe", lambda e, m=m, nch=nch, psn=psn: e.matmul(
                            psn[:], lhsT=cst[:, C_ONE:C_ONE + 128], rhs=sq[:, m, :], start=(m == 0), stop=(m == nch - 1)),
                            reads=[b_cst, b_sq], writes=[b_psn], pe_accum=(m > 0))
                    P.op("act", lambda e, psn=psn, inv=inv: e.activation(out=rstd[:], in_=psn[:], func=AF.Sqrt,
                                                                         bias=self.epsc[:, 0:1], scale=inv),
                         reads=[b_psn, self.b_epsc], writes=[b_rstd])
                    P.op("dve", lambda e: e.reciprocal(out=rstd[:], in_=rstd[:]), reads=[b_rstd], writes=[b_rstd])
                    for m in range(nch):
                        P.op("dve", lambda e, m=m, src=src, dst=dst, gcol=gcol: e.scalar_tensor_tensor(
                            out=dst[:, m, :], in0=src[:, m, :], scalar=mg[:, gcol + m:gcol + m + 1], in1=rstd[:],
                            op0=ALU.mult, op1=ALU.mult),
                            reads=[b_src, b_rstd, b_mg], writes=[b_dst])
                for h in range(H):
                    ps, b_ps = banks[pi % 2]
                    ps2, b_ps2 = banks[2 + pi % 2]
                    k2 = pi % 2
                    pi += 1
                    for m in range(3):
                        P.op("pe", lambda e, m=m, h=h, ps=ps: e.matmul(
                            ps[0:96, :], lhsT=wuq[:, m, h * 96:(h + 1) * 96], rhs=cqn[:, m, :],
                            start=(m == 0), stop=(m == 2)),
                            reads=[b_wuq, b_cqn], writes=[b_ps], pe_accum=(m > 0))
                    o_t, b_o = osb[oi % 4]
                    oi += 1
                    self.apply_rope(ps, b_ps, ps2, b_ps2, 96, C_PSW + 128, Cq, b_Cq, Sq, b_Sq, t0, 1.0, o_t, b_o,
                                    qb[k2][0], qb[k2][1], t1[k2][0], t1[k2][1], t2[k2][0], t2[k2][1])
                    P.dma("sp", lambda e, h=h, o_t=o_t, t0=t0: e.dma_start(
                        out=qT.ap()[h, :, t0:t0 + TT], in_=o_t[0:96, :]),
                        reads=[b_o], writes=[b_qT], sb=b_o)
                for pr in range(8):
                    ps, b_ps = banks[pi % 2]
                    pi += 1
                    for m in range(2):
                        P.op("pe", lambda e, m=m, pr=pr, ps=ps: e.matmul(
                            ps[:], lhsT=wuk[:, m, pr * 128:(pr + 1) * 128], rhs=ckn[:, m, :],
                            start=(m == 0), stop=(m == 1)),
                            reads=[b_wuk, b_ckn], writes=[b_ps], pe_accum=(m > 0))
                    o_t, b_o = osb[oi % 4]
                    oi += 1
                    P.op("act", lambda e, o_t=o_t, ps=ps: e.copy(out=o_t[:], in_=ps[:]), reads=[b_ps], writes=[b_o])
                    for two in range(2):
                        kn, kr0 = self.kown(96, 2 * pr + two)
                        P.dma("sp", lambda e, kn=kn, kr0=kr0, two=two, o_t=o_t, t0=t0: e.dma_start(
                            out=self.dram[kn].ap()[kr0:kr0 + 64, t0:t0 + TT], in_=o_t[two * 64:(two + 1) * 64, :]),
                            reads=[b_o], writes=[self.dbuf[kn]], sb=b_o)
                for tb in range(TT // 128):
                    v_t, b_v = vsb[vi % 2]
                    vi += 1
                    for half in range(2):
                        ps, b_ps = banks[4 + half]
                        for m in range(2):
                            P.op("pe", lambda e, m=m, half=half, tb=tb, ps=ps: e.matmul(
                                ps[:], lhsT=ckn[:, m, tb * 128:(tb + 1) * 128],
                                rhs=wuv[:, m, half * 512:(half + 1) * 512], start=(m == 0), stop=(m == 1)),
                                reads=[b_wuv, b_ckn], writes=[b_ps], pe_accum=(m > 0))
                        P.op("dve", lambda e, half=half, ps=ps, v_t=v_t: e.tensor_copy(
                            out=v_t[:, half * 512:(half + 1) * 512], in_=ps[:]),
                            reads=[b_ps], writes=[b_v])
                    vn, r0 = self.vown(t0 + tb * 128)
                    P.dma("sp", lambda e, v_t=v_t, r0=r0, vn=vn: e.dma_start(
                        out=self.dram[vn].ap()[r0:r0 + 128, :], in_=v_t[:]),
                        reads=[b_v], writes=[self.dbuf[vn]], sb=b_v)
            P.barrier()
        self.exchange_all(96)
        P.barrier()

    def exchange(self, src, dst):
        P = self.P
        groups = [[0, 1], [2, 3], [4, 5], [6, 7]]
        s_t, d_t = self.dram[src], self.dram[dst]
        P.dma("pool", lambda e: e.collective_compute("AllGather", ALU.bypass, replica_groups=groups,
                                                     ins=[s_t.ap()], outs=[d_t.ap()]),
              reads=[self.dbuf[src]], writes=[self.dbuf[dst]], sb=self.dbuf[dst], inc=1, cls="cc")

    def pass_o(self, wname, h_src, h_dst):
        P, NT = self.P, self.NT
        aT = self.dram["aT"]
        hs, hd = self.dram[h_src], self.dram[h_dst]
        with ExitStack() as stack:
            banks = self.psum_banks(stack)
            w, b_w = self.sb(stack, "wo", [128, NCH, D], BF16)
            P.dma("pool", lambda e: e.dma_start(out=w[:], in_=self.dram[wname].ap()),
                  reads=[self.dbuf[wname]], writes=[b_w])
            at = [self.sb(stack, "oat%d" % i, [128, NCH, TT], BF16) for i in range(2)]
            ht = [self.sb(stack, "oht%d" % i, [128, NCH, TT], F32) for i in range(2)]
            pi = 0
            for tt in range(NT // TT):
                t0 = tt * TT
                a_t, b_a = at[tt % 2]
                h_t, b_h = ht[tt % 2]
                for two in range(2):
                    P.dma("sp", lambda e, t0=t0, a_t=a_t, two=two: e.dma_start(
                        out=a_t[two * 64:(two + 1) * 64, :, :],
                        in_=aT.ap()[:, :, t0:t0 + TT].rearrange("(c two) d t -> two d c t", two=2)[two]),
                        reads=[self.dbuf["aT"]], writes=[b_a])
                P.dma("act", lambda e, t0=t0, h_t=h_t: e.dma_start(
                    out=h_t[:], in_=hs.ap()[:, :, t0:t0 + TT].rearrange("c p t -> p c t")),
                    reads=[self.dbuf[h_src]], writes=[b_h])
                for dc in range(NCH):
                    ps, b_ps = banks[pi % 4]
                    pi += 1
                    for c in range(NCH):
                        P.op("pe", lambda e, c=c, dc=dc, ps=ps, a_t=a_t: e.matmul(
                            ps[:], lhsT=w[:, c, dc * 128:(dc + 1) * 128], rhs=a_t[:, c, :],
                            start=(c == 0), stop=(c == NCH - 1)),
                            reads=[b_w, b_a], writes=[b_ps], pe_accum=(c > 0))
                    P.op("dve", lambda e, dc=dc, ps=ps, h_t=h_t: e.tensor_tensor(
                        out=h_t[:, dc, :], in0=ps[:], in1=h_t[:, dc, :], op=ALU.add),
                        reads=[b_ps, b_h], writes=[b_h])
                P.dma("sp", lambda e, t0=t0, h_t=h_t: e.dma_start(
                    out=hd.ap()[:, :, t0:t0 + TT].rearrange("c p t -> p c t"), in_=h_t[:]),
                    reads=[b_h], writes=[self.dbuf[h_dst]], sb=b_h)
            P.barrier()

    def pass_attn(self, kind):
        P, NT = self.P, self.NT
        KD = 96 if kind == "mla" else 64
        NJ = NT // 128
        NQT = NT // TT
        qT, aT = self.dram["qT%d" % KD], self.dram["aT"]
        b_qT, b_aT = self.dbuf["qT%d" % KD], self.dbuf["aT"]
        cst, b_cst = self.cst, self.b_cst
        sm_scale = float(96 ** -0.5) if kind == "mla" else 1.0
        with ExitStack() as stack:
            banks = self.psum_banks(stack)
            vsb, b_vsb = self.sb(stack, "avsb", [128, 2 * NJ, D], BF16)
            for r in range(2):
                for j0 in range(0, NJ, 4):
                    vn, row0 = self.vall(r, j0 * 128)
                    P.dma("sp" if r == 0 else "pool", lambda e, row0=row0, r=r, j0=j0, vn=vn: e.dma_start(
                        out=vsb[:, r * NJ + j0:r * NJ + j0 + 4, :],
                        in_=self.dram[vn].ap()[row0:row0 + 512, :].rearrange("(j p) f -> p j f", p=128)),
                        reads=[self.dbuf[vn]], writes=[b_vsb])
            mask, b_mask = self.sb(stack, "amask", [128, 8, TT], BF16)
            mk_idx = 0 if kind == "sb" else 1
            P.dma("pool", lambda e: e.dma_start(out=mask[:], in_=self.dram["masks"].ap()[mk_idx]),
                  reads=[self.dbuf["masks"]], writes=[b_mask])
            P.op("dve", lambda e: e.tensor_scalar(out=mask[:], in0=mask[:], scalar1=-1.0, scalar2=BIG,
                                                  op0=ALU.add, op1=ALU.mult), reads=[b_mask], writes=[b_mask])
            ksb = [self.sb(stack, "aksb%d" % i, [128, 2, NT], BF16) for i in range(2)]
            qsb = [self.sb(stack, "aqsb%d" % i, [128, NT], BF16) for i in range(2)]
            for (t_, b_) in ksb + qsb:
                P.op("pool", lambda e, t_=t_: e.memset(t_[64:128], 0.0), writes=[b_])
            e_t = [self.sb(stack, "ae%d" % i, [128, TT], F32) for i in range(2)]
            L_t = [self.sb(stack, "aL%d" % i, [128, TT], BF16) for i in range(2)]
            w_t = [self.sb(stack, "aw%d" % i, [128, TT], BF16) for i in range(3)]
            carry = [self.sb(stack, "acar%d" % i, [64, TT], BF16) for i in range(2)]
            cf, b_cf = self.sb(stack, "acf", [64, TT], F32)
            o_f, b_of = self.sb(stack, "aof", [64, TT], F32)
            rden, b_rden = self.sb(stack, "arden", [64, TT], F32)
            o_sb = [self.sb(stack, "aosb%d" % i, [64, TT], BF16) for i in range(2)]
            if kind == "moba":
                negsel = [self.sb(stack, "ansel%d" % i, [128, NT], BF16) for i in range(2)]
                for (t_, b_) in negsel:
                    P.op("pool", lambda e, t_=t_: e.memset(t_[:], 0.0), writes=[b_])
                km, b_km = self.sb(stack, "akm", [64, 2, NJ], F32)
                kms, b_kms = self.sb(stack, "akms", [64, NJ], F32)
                khi, b_khi = self.sb(stack, "akhi", [128, NJ], BF16)
                klo, b_klo = self.sb(stack, "aklo", [128, NJ], BF16)
                P.op("pool", lambda e: e.memset(khi[:], 0.0), writes=[b_khi])
                P.op("pool", lambda e: e.memset(klo[:], 0.0), writes=[b_klo])
                gm = [self.sb(stack, "agm%d" % i, [128, 16], F32) for i in range(2)]
                m8 = [self.sb(stack, "am8%d" % i, [128, 8], F32) for i in range(2)]
                sel = [self.sb(stack, "asel%d" % i, [128, NJ], F32) for i in range(2)]
                nsb = [self.sb(stack, "ansb%d" % i, [128, NJ], BF16) for i in range(2)]
                pa, b_pa = self.sb(stack, "apa", [128, NJ, NJ], F32)
                p01, b_p01 = self.sb(stack, "ap01", [128, NJ, NJ], F32)
                o01, b_o01 = self.sb(stack, "ao01", [128, NJ, NJ], F32)
                pat = [[1, NJ], [-1, NJ]]
                P.op("pool", lambda e: e.memset(pa[:], 0.0), writes=[b_pa])
                P.op("pool", lambda e: e.affine_select(out=pa[:], in_=pa[:], pattern=pat, compare_op=ALU.is_ge,
                                                       fill=NEG, base=-1, channel_multiplier=0),
                     reads=[b_pa], writes=[b_pa])
                P.op("pool", lambda e: e.memset(p01[:], 1.0), writes=[b_p01])
                P.op("pool", lambda e: e.affine_select(out=p01[:], in_=p01[:], pattern=pat, compare_op=ALU.is_ge,
                                                       fill=0.0, base=-1, channel_multiplier=0),
                     reads=[b_p01], writes=[b_p01])
                P.op("pool", lambda e: e.memset(o01[:], 1.0), writes=[b_o01])
                P.op("pool", lambda e: e.affine_select(out=o01[:], in_=o01[:], pattern=pat, compare_op=ALU.is_equal,
                                                       fill=0.0, base=0, channel_multiplier=0),
                     reads=[b_o01], writes=[b_o01])
                for g_t, b_g in gm:
                    P.op("pool", lambda e, g_t=g_t: e.memset(g_t[:], NEG), writes=[b_g])
            def load_head(h):
                k_t, b_k = ksb[h % 2]
                q_t, b_q = qsb[h % 2]
                for r in range(2):
                    kn, row0 = self.kall(KD, h, r)
                    P.dma("sp" if r == 0 else "pool", lambda e, row0=row0, r=r, k_t=k_t, kn=kn: e.dma_start(
                        out=k_t[0:KD, r, :], in_=self.dram[kn].ap()[row0:row0 + KD, :]),
                        reads=[self.dbuf[kn]], writes=[b_k])
                P.dma("sp", lambda e, h=h, q_t=q_t: e.dma_start(out=q_t[0:KD, :], in_=qT.ap()[h]),
                      reads=[b_qT], writes=[b_q])

            def gate_head(h):
                k_t, b_k = ksb[h % 2]
                q_t, b_q = qsb[h % 2]
                ns_t, b_ns = negsel[h % 2]
                P.op("dve", lambda e, k_t=k_t: e.tensor_reduce(
                    out=km[:], in_=k_t[0:64, :, :].rearrange("d r (j k) -> d r j k", k=128),
                    axis=AX.X, op=ALU.add), reads=[b_k], writes=[b_km])
                P.op("dve", lambda e: e.tensor_tensor(out=kms[:], in0=km[:, 0, :], in1=km[:, 1, :], op=ALU.add),
                     reads=[b_km], writes=[b_kms])
                P.op("dve", lambda e: e.tensor_scalar(out=kms[:], in0=kms[:], scalar1=1.0 / 256.0, scalar2=None,
                                                      op0=ALU.mult), reads=[b_kms], writes=[b_kms])
                P.op("dve", lambda e: e.tensor_copy(out=khi[0:64, :], in_=kms[:]), reads=[b_kms], writes=[b_khi])
                P.op("dve", lambda e: e.tensor_tensor(out=klo[0:64, :], in0=kms[:], in1=khi[0:64, :], op=ALU.subtract),
                     reads=[b_kms, b_khi], writes=[b_klo])
                for js in range(NJ):
                    psG, b_psG = banks[2]
                    psT, b_psT = banks[3]
                    g_t, b_g = gm[js % 2]
                    m_t, b_m = m8[js % 2]
                    s_t, b_s = sel[js % 2]
                    n_t, b_n = nsb[js % 2]
                    P.op("pe", lambda e, js=js, q_t=q_t, psG=psG: e.matmul(
                        psG[:, 0:NJ], lhsT=q_t[:, js * 128:(js + 1) * 128], rhs=khi[:], start=True, stop=False),
                        reads=[b_q, b_khi], writes=[b_psG])
                    P.op("pe", lambda e, js=js, q_t=q_t, psG=psG: e.matmul(
                        psG[:, 0:NJ], lhsT=q_t[:, js * 128:(js + 1) * 128], rhs=klo[:], start=False, stop=True),
                        reads=[b_q, b_klo], writes=[b_psG], pe_accum=True)
                    P.op("dve", lambda e, js=js, g_t=g_t, psG=psG: e.tensor_tensor(
                        out=g_t[:, 0:NJ], in0=psG[:, 0:NJ], in1=pa[:, js, :], op=ALU.add),
                        reads=[b_psG, b_pa], writes=[b_g])
                    P.op("dve", lambda e, g_t=g_t, m_t=m_t: e.max(out=m_t[:], in_=g_t[:]),
                         reads=[b_g], writes=[b_m])
                    P.op("dve", lambda e, g_t=g_t, m_t=m_t, s_t=s_t: e.tensor_scalar(
                        out=s_t[:], in0=g_t[:, 0:NJ], scalar1=m_t[:, 2:3], scalar2=None, op0=ALU.is_ge),
                        reads=[b_g, b_m], writes=[b_s])
                    P.op("dve", lambda e, js=js, s_t=s_t: e.tensor_tensor(
                        out=s_t[:], in0=s_t[:], in1=p01[:, js, :], op=ALU.mult),
                        reads=[b_s, b_p01], writes=[b_s])
                    P.op("dve", lambda e, js=js, s_t=s_t: e.tensor_tensor(
                        out=s_t[:], in0=s_t[:], in1=o01[:, js, :], op=ALU.add),
                        reads=[b_s, b_o01], writes=[b_s])
                    P.op("dve", lambda e, s_t=s_t, n_t=n_t: e.tensor_scalar(
                        out=n_t[:], in0=s_t[:], scalar1=-1.0, scalar2=BIG, op0=ALU.add, op1=ALU.mult),
                        reads=[b_s], writes=[b_n])
                    P.op("pe", lambda e, n_t=n_t, psT=psT: e.matmul(
                        psT[0:NJ, 0:128], lhsT=n_t[:], rhs=cst[:, C_ID:C_ID + 128], start=True, stop=True),
                        reads=[b_n, b_cst], writes=[b_psT])
                    P.op("dve", lambda e, js=js, ns_t=ns_t, psT=psT: e.tensor_copy(
                        out=ns_t[0:NJ, js * 128:(js + 1) * 128], in_=psT[0:NJ, 0:128]),
                        reads=[b_psT], writes=[b_ns])

            tiles = []
            chain = 0
            TRIM = True
            for h in range(H):
                for qt in range(NQT):
                    nkb = 8 * qt + 8
                    for i in range(nkb):
                        g = (nkb - 1 - i) if kind == "sb" else i
                        d_ = g - 8 * qt
                        c0_ = 128 * max(0, d_ // 2) if (d_ >= 0 and TRIM) else 0
                        if kind == "sb" and i == 0:
                            c0_ = 0
                        tiles.append(dict(h=h, qt=qt, i=i, nkb=nkb, g=g, r=g % 2, j=g // 2, d=d_, c0=c0_,
                                          chain=chain, t=len(tiles)))
                    chain += 1
            NTL = len(tiles)

            def c0_of(t, same_chain_as):
                if 0 <= t < NTL and tiles[t]["chain"] == same_chain_as["chain"]:
                    return tiles[t]["c0"]
                return None
            Lr = [self.sb(stack, "aLr%d" % i, [128, TT], BF16) for i in range(6)]
            cfr = [self.sb(stack, "acfr%d" % i, [64, TT], F32) for i in range(2)]
            wr = [self.sb(stack, "awr%d" % i, [128, TT], BF16) for i in range(4)]
            car = [self.sb(stack, "acr%d" % i, [128, TT], BF16) for i in range(4)]
            if kind == "sb":
                for (t_, b_) in car:
                    P.op("pool", lambda e, t_=t_: e.memset(t_[:], 0.0), writes=[b_])

            def views(T):
                h = T["h"]
                k_t, b_k = ksb[h % 2]
                q_t, b_q = qsb[h % 2]
                kblk = k_t[:, T["r"], T["j"] * 128:(T["j"] + 1) * 128]
                vblk = vsb[:, T["r"] * NJ + T["j"], h * 64:(h + 1) * 64]
                qtile = q_t[:, T["qt"] * TT:(T["qt"] + 1) * TT]
                return kblk, vblk, qtile, b_k, b_q

            def epilogue(T):
                h, qt, ch = T["h"], T["qt"], T["chain"]
                oo, b_oo = o_sb[ch % 2]
                if kind == "sb":
                    psO, b_psO = banks[6 + ch % 2]
                    P.op("dve", lambda e, psO=psO, oo=oo: e.tensor_copy(out=oo[:], in_=psO[0:64, :]),
                         reads=[b_psO], writes=[b_oo])
                else:
                    psO, b_psO = banks[4 + ch % 2]
                    psD, b_psD = banks[6 + ch % 2]
                    P.op("dve", lambda e, psD=psD: e.reciprocal(out=rden[:], in_=psD[0:64, :]),
                         reads=[b_psD], writes=[b_rden])
                    P.op("dve", lambda e, psO=psO, oo=oo: e.tensor_tensor(
                        out=oo[:], in0=psO[0:64, :], in1=rden[:], op=ALU.mult),
                        reads=[b_psO, b_rden], writes=[b_oo])
                P.dma("sp", lambda e, h=h, qt=qt, oo=oo: e.dma_start(
                    out=aT.ap()[h, :, qt * TT:(qt + 1) * TT], in_=oo[:]),
                    reads=[b_oo], writes=[b_aT], sb=b_oo)

            def head_prologue(T):
                h = T["h"]
                if T["qt"] == 0 and T["i"] == 0 and h == 0:
                    load_head(0)
                    if kind == "moba":
                        gate_head(0)
                if T["qt"] == 0 and T["i"] == 6 and h + 1 < H:
                    load_head(h + 1)
                if kind == "moba" and T["qt"] == min(2, NQT - 1) and T["i"] == 0 and h + 1 < H and NQT > 1:
                    gate_head(h + 1)
                if kind == "moba" and NQT == 1 and T["i"] == T["nkb"] - 1 and h + 1 < H:
                    gate_head(h + 1)

            if kind == "sb":
                def s1a(T):
                    head_prologue(T)
                    t = T["t"]
                    kblk, vblk, qtile, b_k, b_q = views(T)
                    psZ, b_psZ = banks[t % 2]
                    ee, b_ee = e_t[t % 2]
                    LL, b_LL = Lr[t % 6]
                    dg = T["d"] >= 0
                    R = slice(T["c0"], TT)
                    P.op("pe", lambda e: e.matmul(psZ[:, R], lhsT=kblk, rhs=qtile[:, R], start=True, stop=(not dg)),
                         reads=[b_k, b_q], writes=[b_psZ])
                    if dg:
                        d = T["d"]
                        P.op("pe", lambda e: e.matmul(psZ[:, R], lhsT=cst[:, C_ID:C_ID + 128], rhs=mask[:, d, R],
                                                      start=False, stop=True),
                             reads=[b_cst, b_mask], writes=[b_psZ], pe_accum=True)
                    P.op("act", lambda e: e.activation(out=ee[:, R], in_=psZ[:, R], func=AF.Exp),
                         reads=[b_psZ], writes=[b_ee])
                    P.op("act", lambda e: e.activation(out=LL[:, R], in_=ee[:, R], func=AF.Ln, bias=1.0),
                         reads=[b_ee], writes=[b_LL])

                def s1b(T):
                    t, i = T["t"], T["i"]
                    if i == T["nkb"] - 1:
                        return
                    LL, b_LL = Lr[t % 6]
                    psC, b_psC = banks[4 + t % 2]
                    cc, b_cc = car[t % 4]
                    cfn, b_cfn = cfr[t % 2]
                    cfo, b_cfo = cfr[(t - 1) % 2]
                    R = slice(T["c0"], TT)
                    cn = c0_of(t + 1, T)
                    Rn = slice(min(cn, T["c0"]), TT)
                    P.op("pe", lambda e: e.matmul(psC[0:64, R], lhsT=cst[:, C_ONE:C_ONE + 64], rhs=LL[:, R],
                                                  start=True, stop=True),
                         reads=[b_cst, b_LL], writes=[b_psC])
                    if i == 0:
                        P.op("dve", lambda e: e.tensor_copy(out=cfn[:, R], in_=psC[0:64, R]),
                             reads=[b_psC], writes=[b_cfn])
                    else:
                        P.op("dve", lambda e: e.tensor_tensor(out=cfn[:, R], in0=psC[0:64, R], in1=cfo[:, R], op=ALU.add),
                             reads=[b_psC, b_cfo], writes=[b_cfn])
                    if cn < T["c0"]:
                        Rz = slice(cn, T["c0"])
                        P.op("pool", lambda e: e.memset(cfn[:, Rz], 0.0), writes=[b_cfn])
                    P.op("pool", lambda e: e.tensor_copy(out=cc[0:64, Rn], in_=cfn[:, Rn]), reads=[b_cfn], writes=[b_cc])

                def s1c(T):
                    t, i = T["t"], T["i"]
                    if i == T["nkb"] - 1:
                        return
                    cc, b_cc = car[t % 4]
                    cfn, b_cfn = cfr[t % 2]
                    cn = c0_of(t + 1, T)
                    Rn = slice(min(cn, T["c0"]), TT)
                    P.op("dve", lambda e: e.tensor_tensor(out=cc[32:64, Rn], in0=cfn[32:64, Rn], in1=cc[32:64, Rn],
                                                          op=ALU.subtract),
                         reads=[b_cfn, b_cc], writes=[b_cc])

                def s2(T):
                    t, i = T["t"], T["i"]
                    kblk, vblk, qtile, b_k, b_q = views(T)
                    LL, b_LL = Lr[t % 6]
                    psA, b_psA = banks[2 + t % 2]
                    ww, b_ww = wr[t % 4]
                    R = slice(T["c0"], TT)
                    P.op("pe", lambda e: e.matmul(psA[:, R], lhsT=kblk, rhs=qtile[:, R], start=True, stop=False),
                         reads=[b_k, b_q], writes=[b_psA])
                    P.op("pe", lambda e: e.matmul(psA[:, R], lhsT=cst[:, C_NTRI:C_NTRI + 128], rhs=LL[:, R],
                                                  start=False, stop=(i == 0 and T["d"] < 0)),
                         reads=[b_cst, b_LL], writes=[b_psA], pe_accum=True)
                    dg = T["d"] >= 0
                    if i > 0:
                        cc, b_cc = car[(t - 1) % 4]
                        P.op("pe", lambda e: e.matmul(psA[:, R], lhsT=cst[:, C_SEL:C_SEL + 128], rhs=cc[:, R],
                                                      start=False, stop=(not dg)),
                             reads=[b_cst, b_cc], writes=[b_psA], pe_accum=True)
                    if dg:
                        d = T["d"]
                        P.op("pe", lambda e: e.matmul(psA[:, R], lhsT=cst[:, C_ID:C_ID + 128], rhs=mask[:, d, R],
                                                      start=False, stop=True),
                             reads=[b_cst, b_mask], writes=[b_psA], pe_accum=True)
                    P.op("act", lambda e: e.activation(out=ww[:, R], in_=psA[:, R], func=AF.Exp),
                         reads=[b_psA], writes=[b_ww])

                def s3(T):
                    t, i, nkb, ch = T["t"], T["i"], T["nkb"], T["chain"]
                    kblk, vblk, qtile, b_k, b_q = views(T)
                    ww, b_ww = wr[t % 4]
                    psO, b_psO = banks[6 + ch % 2]
                    R = slice(T["c0"], TT)
                    P.op("pe", lambda e: e.matmul(psO[0:64, R], lhsT=vblk, rhs=ww[:, R], start=(i == 0), stop=(i == nkb - 1)),
                         reads=[b_vsb, b_ww], writes=[b_psO], pe_accum=(i > 0))
                    if i == nkb - 1:
                        epilogue(T)
                stages = [(s1a, 0), (s1b, 1), (s1c, 2), (s2, 3), (s3, 4)]
            else:
                def s1(T):
                    head_prologue(T)
                    t = T["t"]
                    kblk, vblk, qtile, b_k, b_q = views(T)
                    psZ, b_psZ = banks[t % 2]
                    ww, b_ww = wr[t % 4]
                    dg = T["d"] >= 0
                    R = slice(T["c0"], TT)
                    c0 = T["c0"]
                    P.op("pe", lambda e: e.matmul(psZ[:, R], lhsT=kblk, rhs=qtile[:, R], start=True,
                                                  stop=(kind != "moba" and not dg)),
                         reads=[b_k, b_q], writes=[b_psZ])
                    if kind == "moba":
                        n = T["g"] // 2
                        qt = T["qt"]
                        ns_t, b_ns = negsel[T["h"] % 2]
                        P.op("pe", lambda e: e.matmul(
                            psZ[:, R], lhsT=cst[:, C_EN + n * 128:C_EN + (n + 1) * 128],
                            rhs=ns_t[:, qt * TT + c0:(qt + 1) * TT], start=False, stop=(not dg)),
                            reads=[b_cst, b_ns], writes=[b_psZ], pe_accum=True)
                    if dg:
                        d = T["d"]
                        P.op("pe", lambda e: e.matmul(psZ[:, R], lhsT=cst[:, C_ID:C_ID + 128], rhs=mask[:, d, R],
                                                      start=False, stop=True),
                             reads=[b_cst, b_mask], writes=[b_psZ], pe_accum=True)
                    P.op("act", lambda e: e.activation(out=ww[:, R], in_=psZ[:, R], func=AF.Exp, scale=sm_scale),
                         reads=[b_psZ], writes=[b_ww])

                def s2(T):
                    t, i, nkb, ch = T["t"], T["i"], T["nkb"], T["chain"]
                    kblk, vblk, qtile, b_k, b_q = views(T)
                    ww, b_ww = wr[t % 4]
                    psO, b_psO = banks[4 + ch % 2]
                    psD, b_psD = banks[6 + ch % 2]
                    R = slice(T["c0"], TT)
                    P.op("pe", lambda e: e.matmul(psO[0:64, R], lhsT=vblk, rhs=ww[:, R], start=(i == 0), stop=(i == nkb - 1)),
                         reads=[b_vsb, b_ww], writes=[b_psO], pe_accum=(i > 0))
                    P.op("pe", lambda e: e.matmul(psD[0:64, R], lhsT=cst[:, C_ONE:C_ONE + 64], rhs=ww[:, R],
                                                  start=(i == 0), stop=(i == nkb - 1)),
                         reads=[b_cst, b_ww], writes=[b_psD], pe_accum=(i > 0))
                    if i == nkb - 1:
                        epilogue(T)
                stages = [(s1, 0), (s2, 2)]
            maxlag = max(l for _, l in stages)
            for step in range(NTL + maxlag):
                for fn, lag in stages:
                    t = step - lag
                    if 0 <= t < NTL:
                        fn(tiles[t])
            P.barrier()


def lay_rows(w):
    k = w.shape[0] // 128
    return np.ascontiguousarray(w.reshape(k, 128, w.shape[1]).transpose(1, 0, 2))


def own_tokens(xb, role):
    S = xb.shape[0]
    return np.ascontiguousarray(xb.reshape(S // 256, 2, 128, -1)[:, role].reshape(S // 2, -1))


def merge_tokens(y0, y1):
    n = y0.shape[0] // 128
    out = np.stack([y0.reshape(n, 128, -1), y1.reshape(n, 128, -1)], axis=1)
    return out.reshape(2 * y0.shape[0], -1)


NT_CORE = 2048
KINDS = ["sb", "moba", "mla", "sb"]


def build_full(NT=NT_CORE, depth=DEPTH):
    m = MKA(NT)
    m.din("xT", [NCH, 128, NT])
    m.din("gains", [128, 3 * depth + 1, NCH])
    m.din("consts", [128, CW]); m.din("ropec", [128, 8]); m.din("masks", [2, 128, 8, 512]); m.din("pos", [1, NT])
    for i in range(depth):
        for f in (1, 2):
            m.din("wgu%d_%d" % (f, i), [NFG, 128, 2, NCH, FG * 128])
            m.din("wd%d_%d" % (f, i), [NCH, 128, NFC, 128])
        kind = KINDS[i]
        if kind == "mla":
            m.din("w_in_%d" % i, [128, NCH, 672]); m.din("w_uq_%d" % i, [128, 3, 1536])
            m.din("w_uk_%d" % i, [128, 2, 1024]); m.din("w_uv_%d" % i, [128, 2, 1024]); m.din("mlag_%d" % i, [128, 5])
        else:
            m.din("wqkv_%d" % i, [128, NCH, 3 * D])
        m.din("wo_%d" % i, [128, NCH, D])
    m.dout("yT", [NCH, 128, NT])
    m.dscr("hA", [NCH, 128, NT], F32); m.dscr("hB", [NCH, 128, NT], F32); m.dscr("hnmix", [NCH, 128, NT], BF16)
    m.alloc_scratch()
    with ExitStack() as cst:
        m.setup_consts(cst, m.dram["gains"], 3 * depth + 1)
        m.setup_attn_consts(cst)
        cur = "xT"
        for i in range(depth):
            kind = KINDS[i]
            m.pass_ffn(cur, "hA", "wgu1_%d" % i, "wd1_%d" % i, 3 * i, epi=("norm", 3 * i + 1, "hnmix"))
            if kind == "mla":
                m.pass_proj_mla("w_in_%d" % i, "w_uq_%d" % i, "w_uk_%d" % i, "w_uv_%d" % i, "mlag_%d" % i, "hnmix")
            else:
                m.pass_proj_qkv(kind, "wqkv_%d" % i, "hnmix")
            m.pass_attn(kind)
            m.pass_o("wo_%d" % i, "hA", "hB")
            last = i == depth - 1
            m.pass_ffn("hB", "hA", "wgu2_%d" % i, "wd2_%d" % i, 3 * i + 2,
                       epi=(("final", 3 * depth, "yT") if last else None))
            cur = "hA"
        nc = m.finish(["yT"])
    return m, nc


def host_inputs(inp, NT=NT_CORE, depth=DEPTH):
    f32 = lambda a: np.asarray(a, dtype=np.float32)
    shared = {}
    gs = []
    for i in range(depth):
        gs += [f32(inp["norm_ffn1"])[i], f32(inp["norm_mix"])[i], f32(inp["norm_ffn2"])[i]]
    gs.append(f32(inp["final_norm"]))
    shared["gains"] = lay_gain(gs)
    shared["consts"] = host_consts()
    shared["ropec"] = host_ropec()
    cnt = {"sb": 0, "moba": 0, "mla": 0}
    for i in range(depth):
        shared["wgu1_%d" % i] = lay_wgu(f32(inp["ffn1_w_gate_up"])[i])
        shared["wd1_%d" % i] = lay_wd(f32(inp["ffn1_w_down"])[i])
        shared["wgu2_%d" % i] = lay_wgu(f32(inp["ffn2_w_gate_up"])[i])
        shared["wd2_%d" % i] = lay_wd(f32(inp["ffn2_w_down"])[i])
        kind = KINDS[i]
        j = cnt[kind]
        cnt[kind] += 1
        if kind == "mla":
            shared["w_in_%d" % i] = lay_rows(f32(inp["mla_w_in"])[j])
            shared["w_uq_%d" % i] = lay_rows(f32(inp["mla_w_uq"])[j])
            wukv = f32(inp["mla_w_ukv"])[j].reshape(256, 16, 2, 64)
            shared["w_uk_%d" % i] = lay_rows(np.ascontiguousarray(wukv[:, :, 0].reshape(256, 1024)))
            shared["w_uv_%d" % i] = lay_rows(np.ascontiguousarray(wukv[:, :, 1].reshape(256, 1024)))
            shared["mlag_%d" % i] = np.ascontiguousarray(np.concatenate(
                [f32(inp["mla_q_norm"])[j].reshape(3, 128), f32(inp["mla_kv_norm"])[j].reshape(2, 128)], 0).T)
            shared["wo_%d" % i] = lay_rows(f32(inp["mla_w_o"])[j])
        elif kind == "moba":
            shared["wqkv_%d" % i] = lay_rows(f32(inp["moba_w_qkv"])[j])
            shared["wo_%d" % i] = lay_rows(f32(inp["moba_w_o"])[j])
        else:
            shared["wqkv_%d" % i] = lay_rows(f32(inp["sb_w_qkv"])[j])
            shared["wo_%d" % i] = lay_rows(f32(inp["sb_w_o"])[j])
    x = f32(inp["x"])
    in_maps = []
    for c in range(8):
        b, r = c // 2, c % 2
        d = dict(shared)
        d["xT"] = lay_xT(own_tokens(x[b], r))
        d["masks"] = host_masks(r)
        d["pos"] = host_pos(r, NT)
        in_maps.append(d)
    return in_maps


_CACHE = {}


def kernel(**inputs):
    if "nc" not in _CACHE:
        _CACHE["nc"] = build_full()[1]
    nc = _CACHE["nc"]
    in_maps = host_inputs(inputs)
    res = run_bass_kernel_spmd(nc, in_maps, core_ids=list(range(8)))
    B = 4
    out = np.empty((B, 2 * NT_CORE, D), np.float32)
    for b in range(B):
        out[b] = merge_tokens(unlay_xT(res.results[2 * b]["yT"]), unlay_xT(res.results[2 * b + 1]["yT"]))
    return out
```
